# Optimizing a Trainium2 kernel written in Bass

```python
import jax, jax.numpy as jnp
from jax import lax
import numpy as np

D_MODEL = 2048
BATCH = 8
SEQ = 4096
DEPTH = 4

N_MIXERS = 3
MEM_LEN = 256
EPS = 1e-6
BLOCK = 128

SSD_EXPAND = 2
SSD_D_INNER = SSD_EXPAND * D_MODEL
SSD_HEAD_DIM = 64
SSD_HEADS = SSD_D_INNER // SSD_HEAD_DIM
SSD_GROUPS = 8
SSD_HEADS_PER_GROUP = SSD_HEADS // SSD_GROUPS
SSD_STATE = 128
SSD_CONV = 4
SSD_CHUNK = 128
SSD_CONV_DIM = SSD_D_INNER + 2 * SSD_GROUPS * SSD_STATE
SSD_IN_DIM = 2 * SSD_D_INNER + 2 * SSD_GROUPS * SSD_STATE + SSD_HEADS

SG_WIDTH = 2 * D_MODEL
SG_GROUPS = 16
SG_CHUNK = 128

SB_HEAD_DIM = 128
SB_HEADS = D_MODEL // SB_HEAD_DIM

XA_HEADS = 4
XA_HEAD_DIM = 128
XA_WIDTH = XA_HEADS * XA_HEAD_DIM

FFN_DIM = 5632
FFN_CONV = 3

N_SSD = (DEPTH + N_MIXERS - 1) // N_MIXERS
N_SG = (DEPTH + N_MIXERS - 2) // N_MIXERS
N_SB = DEPTH // N_MIXERS

kernel_name = "interleaved_ssd_gmlp_stickbreak_trunk"


def rmsnorm(x, g):
    xf = x.astype(jnp.float32)
    y = xf * lax.rsqrt(jnp.mean(xf * xf, axis=-1, keepdims=True) + EPS)
    return (y * g.astype(jnp.float32)).astype(x.dtype)


def layernorm(x, g, b):
    xf = x.astype(jnp.float32)
    mu = jnp.mean(xf, axis=-1, keepdims=True)
    xc = xf - mu
    y = xc * lax.rsqrt(jnp.mean(xc * xc, axis=-1, keepdims=True) + EPS)
    return (y * g.astype(jnp.float32) + b.astype(jnp.float32)).astype(x.dtype)


def causal_dwconv(x, w, b):
    K, C = w.shape
    y = lax.conv_general_dilated(
        x, w[:, None, :].astype(x.dtype), window_strides=(1,), padding=[(K - 1, 0)],
        dimension_numbers=("NWC", "WIO", "NWC"), feature_group_count=C)
    return y + b.astype(x.dtype)


def ssd_mixer(h, w_in, conv_w, conv_b, dt_bias, a_log, d_skip, norm_g, w_out):
    Bsz, L, _ = h.shape
    G, R, P, N, Q = SSD_GROUPS, SSD_HEADS_PER_GROUP, SSD_HEAD_DIM, SSD_STATE, SSD_CHUNK
    nc = L // Q
    f32 = jnp.float32
    proj = h @ w_in
    z, xbc, dt = jnp.split(proj, [SSD_D_INNER, SSD_D_INNER + SSD_CONV_DIM], axis=-1)
    xbc = jax.nn.silu(causal_dwconv(xbc, conv_w, conv_b))
    xs, Bm, Cm = jnp.split(xbc, [SSD_D_INNER, SSD_D_INNER + G * N], axis=-1)
    dt = jax.nn.softplus(dt.astype(f32) + dt_bias.astype(f32))
    A = -jnp.exp(a_log.astype(f32)).reshape(G, R)
    xs_f = xs.astype(f32).reshape(Bsz, nc, Q, G, R, P)
    dtc = dt.reshape(Bsz, nc, Q, G, R)
    Bc = Bm.astype(f32).reshape(Bsz, nc, Q, G, N)
    Cc = Cm.astype(f32).reshape(Bsz, nc, Q, G, N)
    a_cum = jnp.cumsum(dtc * A, axis=2)
    xdt = xs_f * dtc[..., None]
    causal = jnp.tril(jnp.ones((Q, Q), dtype=bool))[None, None, :, :, None, None]
    seg = a_cum[:, :, :, None] - a_cum[:, :, None, :]
    Lmat = jnp.exp(jnp.where(causal, seg, -jnp.inf))
    CB = jnp.einsum("bclgn,bcsgn->bclsg", Cc, Bc)
    y_diag = jnp.einsum("bclsg,bclsgr,bcsgrp->bclgrp", CB, Lmat, xdt)
    decay_states = jnp.exp(a_cum[:, :, -1:] - a_cum)
    states = jnp.einsum("bclgn,bclgr,bclgrp->bcgrpn", Bc, decay_states, xdt)
    chunk_decay = jnp.exp(a_cum[:, :, -1])

    def step(carry, inp):
        st, dec = inp
        return carry * dec[..., None, None] + st, carry

    init = jnp.zeros((Bsz, G, R, P, N), f32)
    _, prev = lax.scan(step, init, (jnp.moveaxis(states, 1, 0), jnp.moveaxis(chunk_decay, 1, 0)))
    prev = jnp.moveaxis(prev, 0, 1)
    y_off = jnp.einsum("bclgn,bcgrpn,bclgr->bclgrp", Cc, prev, jnp.exp(a_cum))
    y = y_diag + y_off + xs_f * d_skip.astype(f32).reshape(G, R)[..., None]
    y = y.reshape(Bsz, L, SSD_D_INNER) * jax.nn.silu(z.astype(f32))
    y = rmsnorm(y, norm_g).astype(h.dtype)
    return y @ w_out


def sgu_mixer(h, w_in, v_norm_g, v_norm_b, w_spatial, b_spatial, w_out):
    Bsz, L, _ = h.shape
    nc = L // SG_CHUNK
    uv = jax.nn.gelu(h @ w_in)
    u, v = jnp.split(uv, 2, axis=-1)
    v = layernorm(v, v_norm_g, v_norm_b)
    vc = v.reshape(Bsz, nc, SG_CHUNK, SG_GROUPS, SG_WIDTH // SG_GROUPS)
    mask = jnp.tril(jnp.ones((SG_CHUNK, SG_CHUNK), dtype=w_spatial.dtype))
    ws = w_spatial * mask
    mixed = jnp.einsum("gts,bcsgd->bctgd", ws, vc) + b_spatial.T[None, None, :, :, None]
    return (u * mixed.reshape(Bsz, L, SG_WIDTH)) @ w_out


def stick_breaking_mixer(h, w_qkv, w_out):
    Bsz, L, _ = h.shape
    f32 = jnp.float32
    qkv = (h @ w_qkv).reshape(Bsz, L, 3, SB_HEADS, SB_HEAD_DIM)
    q, k, v = qkv[:, :, 0], qkv[:, :, 1], qkv[:, :, 2]
    scale = SB_HEAD_DIM ** -0.5
    outs = []
    for i in range(L // BLOCK):
        q0 = i * BLOCK
        kend = q0 + BLOCK
        qb, kb, vb = q[:, q0:kend], k[:, :kend], v[:, :kend]
        z = jnp.einsum("bthd,bshd->bhts", qb, kb).astype(f32) * scale
        t_idx = q0 + jnp.arange(BLOCK)
        s_idx = jnp.arange(kend)
        valid = s_idx[None, :] < t_idx[:, None]
        log_beta = jax.nn.log_sigmoid(z)
        log_1mb = jnp.where(valid, jax.nn.log_sigmoid(-z), 0.0)
        tail = lax.cumsum(log_1mb, axis=3, reverse=True) - log_1mb
        A = jnp.where(valid, jnp.exp(log_beta + tail), 0.0)
        outs.append(jnp.einsum("bhts,bshd->bthd", A.astype(vb.dtype), vb))
    o = jnp.concatenate(outs, axis=1).reshape(Bsz, L, D_MODEL)
    return o @ w_out


def memory_cross_attention(h, mem_n, wq, wkv, wo):
    Bsz, L, _ = h.shape
    M = mem_n.shape[1]
    q = (h @ wq).reshape(Bsz, L, XA_HEADS, XA_HEAD_DIM)
    kv = (mem_n @ wkv).reshape(Bsz, M, 2, XA_HEADS, XA_HEAD_DIM)
    k, v = kv[:, :, 0], kv[:, :, 1]
    s = jnp.einsum("bthd,bmhd->bhtm", q, k).astype(jnp.float32) * (XA_HEAD_DIM ** -0.5)
    p = jax.nn.softmax(s, axis=-1).astype(v.dtype)
    o = jnp.einsum("bhtm,bmhd->bthd", p, v).reshape(Bsz, L, XA_WIDTH)
    return o @ wo


def conv_gated_ffn(h, w_in, conv_w, conv_b, w_out):
    gu = causal_dwconv(h @ w_in, conv_w, conv_b)
    g, u = jnp.split(gu, 2, axis=-1)
    return (jax.nn.gelu(g, approximate=True) * u) @ w_out


def setup_inputs(seed: int = 0) -> dict:
    key = jax.random.key(seed)
    keys = iter(jax.random.split(key, 64))
    f32 = jnp.float32

    def nrm(shape, scale):
        return jax.random.normal(next(keys), shape, f32) * scale

    def gain(shape):
        return 1.0 + nrm(shape, 0.05)

    D = D_MODEL
    d = {}
    d["x"] = nrm((BATCH, SEQ, D), 1.0)
    d["mem"] = nrm((BATCH, MEM_LEN, D), 1.0)
    for name in ["ln_mix_pre", "ln_mix_post", "ln_mem", "ln_xa_pre", "ln_xa_post", "ln_ffn_pre", "ln_ffn_post"]:
        d[name] = gain((DEPTH, D))
    d["xa_wq"] = nrm((DEPTH, D, XA_WIDTH), D ** -0.5)
    d["xa_wkv"] = nrm((DEPTH, D, 2 * XA_WIDTH), D ** -0.5)
    d["xa_wo"] = nrm((DEPTH, XA_WIDTH, D), XA_WIDTH ** -0.5)
    d["ffn_w_in"] = nrm((DEPTH, D, 2 * FFN_DIM), D ** -0.5)
    d["ffn_conv_w"] = nrm((DEPTH, FFN_CONV, 2 * FFN_DIM), FFN_CONV ** -0.5)
    d["ffn_conv_b"] = nrm((DEPTH, 2 * FFN_DIM), 0.02)
    d["ffn_w_out"] = nrm((DEPTH, FFN_DIM, D), FFN_DIM ** -0.5)
    d["ssd_w_in"] = nrm((N_SSD, D, SSD_IN_DIM), D ** -0.5)
    d["ssd_conv_w"] = nrm((N_SSD, SSD_CONV, SSD_CONV_DIM), SSD_CONV ** -0.5)
    d["ssd_conv_b"] = nrm((N_SSD, SSD_CONV_DIM), 0.02)
    dt0 = jnp.exp(jax.random.uniform(next(keys), (N_SSD, SSD_HEADS), f32,
                                     minval=math_log(1e-3), maxval=math_log(1e-1)))
    d["ssd_dt_bias"] = dt0 + jnp.log(-jnp.expm1(-dt0))
    d["ssd_a_log"] = jnp.log(jax.random.uniform(next(keys), (N_SSD, SSD_HEADS), f32, minval=1.0, maxval=16.0))
    d["ssd_d"] = gain((N_SSD, SSD_HEADS))
    d["ssd_norm"] = gain((N_SSD, SSD_D_INNER))
    d["ssd_w_out"] = nrm((N_SSD, SSD_D_INNER, D), SSD_D_INNER ** -0.5)
    d["sg_w_in"] = nrm((N_SG, D, 2 * SG_WIDTH), D ** -0.5)
    d["sg_v_norm_g"] = gain((N_SG, SG_WIDTH))
    d["sg_v_norm_b"] = nrm((N_SG, SG_WIDTH), 0.02)
    d["sg_w_spatial"] = nrm((N_SG, SG_GROUPS, SG_CHUNK, SG_CHUNK), 0.5 * SG_CHUNK ** -0.5)
    d["sg_b_spatial"] = 1.0 + nrm((N_SG, SG_GROUPS, SG_CHUNK), 0.1)
    d["sg_w_out"] = nrm((N_SG, SG_WIDTH, D), SG_WIDTH ** -0.5)
    d["sb_w_qkv"] = nrm((N_SB, D, 3 * D), D ** -0.5)
    d["sb_w_out"] = nrm((N_SB, D, D), D ** -0.5)
    return d


def math_log(v):
    return float(np.log(v))


def reference(x, mem, ln_mix_pre, ln_mix_post, ln_mem, ln_xa_pre, ln_xa_post, ln_ffn_pre, ln_ffn_post,
              xa_wq, xa_wkv, xa_wo, ffn_w_in, ffn_conv_w, ffn_conv_b, ffn_w_out,
              ssd_w_in, ssd_conv_w, ssd_conv_b, ssd_dt_bias, ssd_a_log, ssd_d, ssd_norm, ssd_w_out,
              sg_w_in, sg_v_norm_g, sg_v_norm_b, sg_w_spatial, sg_b_spatial, sg_w_out,
              sb_w_qkv, sb_w_out):
    for i in range(DEPTH):
        kind = i % N_MIXERS
        j = i // N_MIXERS
        hn = rmsnorm(x, ln_mix_pre[i])
        if kind == 0:
            m = ssd_mixer(hn, ssd_w_in[j], ssd_conv_w[j], ssd_conv_b[j], ssd_dt_bias[j],
                          ssd_a_log[j], ssd_d[j], ssd_norm[j], ssd_w_out[j])
        elif kind == 1:
            m = sgu_mixer(hn, sg_w_in[j], sg_v_norm_g[j], sg_v_norm_b[j],
                          sg_w_spatial[j], sg_b_spatial[j], sg_w_out[j])
        else:
            m = stick_breaking_mixer(hn, sb_w_qkv[j], sb_w_out[j])
        x = x + rmsnorm(m, ln_mix_post[i])
        mem_n = rmsnorm(mem, ln_mem[i])
        c = memory_cross_attention(rmsnorm(x, ln_xa_pre[i]), mem_n, xa_wq[i], xa_wkv[i], xa_wo[i])
        x = x + rmsnorm(c, ln_xa_post[i])
        f = conv_gated_ffn(rmsnorm(x, ln_ffn_pre[i]), ffn_w_in[i], ffn_conv_w[i], ffn_conv_b[i], ffn_w_out[i])
        x = x + rmsnorm(f, ln_ffn_post[i])
    return x
```

```python
import numpy as np
import concourse.bass as bass
import concourse.mybir as mybir
from concourse.bass_utils import run_bass_kernel_spmd

F32 = mybir.dt.float32
BF16 = mybir.dt.bfloat16
AF = mybir.ActivationFunctionType
ALU = mybir.AluOpType

D = 2048
DC = 16
TG = 512
EPS = 1e-6
EPOCH = 30000
SBUF_BASE = 16384
SBUF_CAP = 229000
FFN = 5632
SAME_ENG_SYNC = True


class Dep:
    __slots__ = ("lw", "rd")

    def __init__(self):
        self.lw = None
        self.rd = {}


class T:
    def __init__(self, h, dep=None):
        self.h = h
        self.dep = dep or Dep()
        self.sem = None
        self.semcnt = 0

    def __getitem__(self, k):
        return self.h[k]


class KB:
    def __init__(self, nc):
        self.nc = nc
        self.E = {"pe": nc.tensor, "act": nc.scalar, "dve": nc.vector, "pool": nc.gpsimd, "sp": nc.sync}
        self.cur = {}
        self.seen = {e: {} for e in self.E}
        self.nsem = 0
        self.dmatiles = []
        self.free_sems = {}
        self.sb_off = SBUF_BASE
        self.nalloc = 0
        self.P = [T(nc.alloc_psum_tensor(f"bank{i}", [128, 512], F32)) for i in range(8)]

    def _newsem(self, name):
        self.nsem += 1
        return self.nc.alloc_semaphore(f"{name}_{self.nsem}")

    def _tick(self, eng):
        c = self.cur.get(eng)
        if c is None or c[1] >= EPOCH:
            c = [self._newsem(eng), 0]
            self.cur[eng] = c
        c[1] += 1
        return (c[0], c[1], eng)

    def _wait(self, eng, deps, raw_ts=()):
        for d in deps:
            if d is None:
                continue
            sem, val, src = d
            if src == eng and (eng == "pe" or not SAME_ENG_SYNC):
                continue
            if self.seen[eng].get(sem, 0) >= val:
                continue
            self.E[eng].wait_ge(sem, val)
            self.seen[eng][sem] = val

    def _deps(self, reads, writes):
        deps, raw = [], []
        for t in reads:
            deps.append(t.dep.lw)
            raw.append(t.dep.lw)
        for t in writes:
            deps.append(t.dep.lw)
            deps.extend(t.dep.rd.values())
        return deps, raw

    def _commit(self, d, reads, writes):
        for t in writes:
            t.dep.lw = d
            t.dep.rd = {}
        for t in reads:
            t.dep.rd[d[0]] = d

    def op(self, eng, fn, reads=(), writes=()):
        deps, raw = self._deps(reads, writes)
        self._wait(eng, deps, raw)
        ins = fn(self.E[eng])
        d = self._tick(eng)
        ins.then_inc(d[0], 1)
        self._commit(d, reads, writes)
        return ins

    def mmg(self, P, out_ap, pairs, reads):
        deps, raw = self._deps(reads, [P])
        self._wait("pe", deps, raw)
        n = len(pairs)
        ins = None
        for i, (l, r) in enumerate(pairs):
            ins = self.nc.tensor.matmul(out_ap, l, r, start=(i == 0), stop=(i == n - 1))
        d = self._tick("pe")
        ins.then_inc(d[0], 1)
        self._commit(d, reads, [P])

    def transpose(self, P, out_ap, in_ap, ident_ap, reads):
        deps, raw = self._deps(reads, [P])
        self._wait("pe", deps, raw)
        ins = self.nc.tensor.transpose(out_ap, in_ap, ident_ap)
        d = self._tick("pe")
        ins.then_inc(d[0], 1)
        self._commit(d, reads, [P])

    def dma(self, q, out_ap, in_ap, semT, reads=(), writes=()):
        deps, raw = self._deps(reads, writes)
        self._wait(q, deps, raw)
        if semT.sem is None or semT.semcnt >= EPOCH:
            fl = self.free_sems.setdefault(q, [])
            if fl and fl[0][1] < EPOCH - 2000:
                semT.sem, semT.semcnt = fl.pop(0)
            else:
                semT.sem = self._newsem("dma" + q)
                semT.semcnt = 0
            semT.semq = q
            self.dmatiles.append((semT, semT.sem))
        assert semT.semq == q
        semT.semcnt += 16
        self.E[q].dma_start(out=out_ap, in_=in_ap).then_inc(semT.sem, 16)
        d = (semT.sem, semT.semcnt, "dma")
        self._commit(d, reads, writes)

    def barrier(self):
        deps = [(c[0], c[1], e) for e, c in self.cur.items()]
        latest = {}
        for t, sem in self.dmatiles:
            if t.sem is sem:
                latest[id(sem)] = (sem, t.semcnt, "dma")
        deps += list(latest.values())
        for e in self.E:
            self._wait(e, deps)

    def stage(self, base=None):
        self.barrier()
        for t, sem in self.dmatiles:
            if t.sem is sem:
                self.free_sems[t.semq].append((sem, t.semcnt))
                t.sem = None
        self.dmatiles = []
        if base is not None:
            self.sb_off = base

    def sb(self, shape, dtype, name="t"):
        nbytes = int(np.prod(shape[1:])) * (2 if dtype == BF16 else 4)
        nbytes = (nbytes + 31) // 32 * 32
        off = self.sb_off
        self.sb_off += nbytes
        assert self.sb_off <= SBUF_CAP, f"SBUF overflow {self.sb_off}"
        self.nalloc += 1
        return T(self.nc.alloc_sbuf_tensor_at(f"{name}_{self.nalloc}", list(shape), dtype, offset=off))

    def sbs(self, n, shape, dtype, name="t"):
        return [self.sb(shape, dtype, name) for _ in range(n)]


def make_consts(k):
    C = {}
    d = k.sb([128, 128], F32, "iota")
    k.op("pool", lambda e: e.iota(d[:], [[1, 128]], base=0, channel_multiplier=-1,
                                  allow_small_or_imprecise_dtypes=True), writes=[d])

    def cmp(name, op, dtype, val=1.0, thr=0.0):
        t = k.sb([128, 128], dtype, name)
        k.op("dve", lambda e: e.tensor_scalar(out=t[:], in0=d[:], scalar1=thr, scalar2=val, op0=op, op1=ALU.mult),
             reads=[d], writes=[t])
        C[name] = t
    cmp("ident_bf", ALU.is_equal, BF16)
    cmp("ident_f", ALU.is_equal, F32)
    cmp("le_f", ALU.is_ge, F32)
    cmp("le_bf", ALU.is_ge, BF16)
    cmp("lt_f", ALU.is_gt, F32)
    cmp("gt_f", ALU.is_lt, F32)
    cmp("neg_ge_bf", ALU.is_le, BF16, val=-1.0)
    ones = k.sb([128, 128], BF16, "ones")
    k.op("dve", lambda e: e.memset(ones[:], 1.0), writes=[ones])
    C["ones_bf"] = ones
    onesf = k.sb([128, 128], F32, "onesf")
    k.op("dve", lambda e: e.memset(onesf[:], 1.0), writes=[onesf])
    C["ones_f"] = onesf
    return C


def load_small(k, dram_ap, shape, dtype=F32, q="sp"):
    t = k.sb(shape, dtype, "small")
    k.dma(q, t[:], dram_ap, t, writes=[t])
    return t


class NormBufs:
    def __init__(self, k):
        self.sq = k.sbs(2, [128, 4, TG], BF16, "sq")
        self.lnt = k.sb([128, TG], F32, "lnt")
        self.rstd = k.sb([128, TG], F32, "rstd")


def rms_stats(k, C, nb, src, nchunks, Pn, dim, w=TG, eps=EPS):
    allp, rd = [], []
    for q in range(nchunks // 4):
        sq = nb.sq[q % 2]
        k.op("act", lambda e: e.activation(out=sq[:, :, 0:w], in_=src[:, 4 * q:4 * q + 4, 0:w], func=AF.Square),
             reads=[src], writes=[sq])
        allp = [(C["ones_bf"][:], sq[:, j, 0:w]) for j in range(4)]
        emit_partial_group(k, Pn, allp, [sq], q == 0, q == nchunks // 4 - 1, out_ap=Pn[:, 0:w])
    k.op("act", lambda e: e.activation(out=nb.lnt[:, 0:w], in_=Pn[:, 0:w], func=AF.Ln, scale=1.0 / dim, bias=eps),
         reads=[Pn], writes=[nb.lnt])
    k.op("act", lambda e: e.activation(out=nb.rstd[:, 0:w], in_=nb.lnt[:, 0:w], func=AF.Exp, scale=-0.5),
         reads=[nb.lnt], writes=[nb.rstd])
    return nb.rstd


def pre_norm(k, C, nb, xs, gcol, hn, Pn, w=TG):
    rstd = rms_stats(k, C, nb, xs, DC, Pn, D, w)
    for c in range(DC):
        k.op("dve", lambda e: e.scalar_tensor_tensor(out=hn[c][:, 0:w], in0=xs[:, c, 0:w], scalar=gcol[:, c:c + 1],
                                                     in1=rstd[:, 0:w], op0=ALU.mult, op1=ALU.mult),
             reads=[xs, rstd, gcol], writes=[hn[c]])


def post_norm_residual(k, C, nb, y, gcol, xs, Pn, tmps):
    rstd = rms_stats(k, C, nb, y, DC, Pn, D)
    for c in range(DC):
        tmp = tmps[c % len(tmps)]
        k.op("dve", lambda e: e.scalar_tensor_tensor(out=tmp[:], in0=y[:, c, :], scalar=gcol[:, c:c + 1],
                                                     in1=rstd[:], op0=ALU.mult, op1=ALU.mult),
             reads=[y, rstd, gcol], writes=[tmp])
        k.op("pool", lambda e: e.tensor_tensor(out=xs[:, c, :], in0=xs[:, c, :], in1=tmp[:], op=ALU.add),
             reads=[xs, tmp], writes=[xs])


class WStream:
    def __init__(self, k, nbuf, kc, ncols):
        self.k = k
        self.bufs = k.sbs(nbuf, [128, kc, ncols], BF16, "w")
        self.i = 0

    def load(self, W, k0, kc, n0, ncols):
        t = self.bufs[self.i % len(self.bufs)]
        self.i += 1
        src = W[k0:k0 + kc * 128, n0:n0 + ncols].rearrange("(c p) n -> p c n", p=128)
        self.k.dma("pool", t[:, 0:kc, 0:ncols], src, t, writes=[t])
        return t


def x_tile_ap(xd, tg):
    return xd[:, tg * TG:(tg + 1) * TG].rearrange("(c p) t -> p c t", p=128)


def emit_ffn(k, C, NT, x_src, x_dst, W, li, base):
    k.stage(base)
    g_pre = load_small(k, W["ln_ffn_pre"][li], [128, DC])
    g_post = load_small(k, W["ln_ffn_post"][li], [128, DC])
    cw = load_small(k, W["ffn_conv_w"][li], [128, 88, 3])
    cb = load_small(k, W["ffn_conv_b"][li], [128, 88])
    halo = k.sb([128, 88, 2], F32, "halo")
    k.op("pool", lambda e: e.memset(halo[:], 0.0), writes=[halo])
    nb = NormBufs(k)
    xs = k.sb([128, DC, TG], F32, "xs")
    hn = k.sbs(DC, [128, TG], BF16, "hn")
    act = k.sbs(44, [128, TG], BF16, "act")
    y = k.sb([128, DC, TG], F32, "y")
    xpad = k.sbs(4, [128, TG + 2], F32, "xpad")
    cv = k.sbs(4, [128, TG], F32, "cv")
    tmp = k.sbs(2, [128, TG], F32, "tmp")
    ws = WStream(k, 3, 16, 256)
    w_in = W["ffn_w_in"][li]
    w_out = W["ffn_w_out"][li]
    Pn = k.P[7]
    for tg in range(NT):
        k.dma("sp", xs[:], x_tile_ap(x_src, tg), xs, reads=[x_src.T], writes=[xs])
        pre_norm(k, C, nb, xs, g_pre, hn, Pn)
        for J in range(22):
            wg = ws.load(w_in, 0, 16, J * 256, 256)
            wu = ws.load(w_in, 0, 16, FFN + J * 256, 256)
            for jj in range(2):
                j = 2 * J + jj
                outs = []
                for half, wt in enumerate((wg, wu)):
                    ch = j + 44 * half
                    Pb = k.P[(2 * j + half) % 4]
                    k.mmg(Pb, Pb[:], [(wt[:, c, jj * 128:(jj + 1) * 128], hn[c][:]) for c in range(DC)],
                          [wt] + hn)
                    xp = xpad[(2 * j + half) % 4]
                    o = cv[(2 * j + half) % 4]
                    k.op("pool", lambda e: e.tensor_copy(out=xp[:, 0:2], in_=halo[:, ch, :]), reads=[halo], writes=[xp])
                    k.op("act", lambda e: e.activation(out=xp[:, 2:TG + 2], in_=Pb[:], func=AF.Copy),
                         reads=[Pb], writes=[xp])
                    k.op("pool", lambda e: e.tensor_copy(out=halo[:, ch, :], in_=xp[:, TG:TG + 2]), reads=[xp], writes=[halo])
                    k.op("dve", lambda e: e.tensor_scalar(out=o[:], in0=xp[:, 2:TG + 2], scalar1=cw[:, ch, 2:3],
                                                          scalar2=cb[:, ch:ch + 1], op0=ALU.mult, op1=ALU.add),
                         reads=[xp, cw, cb], writes=[o])
                    for tap in (1, 0):
                        k.op("dve", lambda e: e.scalar_tensor_tensor(out=o[:], in0=xp[:, tap:tap + TG],
                                                                     scalar=cw[:, ch, tap:tap + 1], in1=o[:],
                                                                     op0=ALU.mult, op1=ALU.add),
                             reads=[xp, cw, o], writes=[o])
                    outs.append(o)
                og, ou = outs
                k.op("act", lambda e: e.activation(out=og[:], in_=og[:], func=AF.Gelu_apprx_tanh), reads=[og], writes=[og])
                k.op("dve", lambda e: e.tensor_tensor(out=act[j][:], in0=og[:], in1=ou[:], op=ALU.mult),
                     reads=[og, ou], writes=[act[j]])
        for nbk in range(4):
            for kp in range(4):
                wts = [ws.load(w_out, kp * 11 * 128, 11, nbk * 512 + h * 256, 256) for h in range(2)]
                for jj in range(4):
                    Pb = k.P[jj]
                    wt = wts[jj // 2]
                    pairs = [(wt[:, c, (jj % 2) * 128:(jj % 2 + 1) * 128], act[kp * 11 + c][:]) for c in range(11)]
                    emit_partial_group(k, Pb, pairs, [wt] + act[kp * 11:kp * 11 + 11], kp == 0, kp == 3)
            for jj in range(4):
                c = nbk * 4 + jj
                k.op("act", lambda e: e.activation(out=y[:, c, :], in_=k.P[jj][:], func=AF.Copy),
                     reads=[k.P[jj]], writes=[y])
        post_norm_residual(k, C, nb, y, g_post, xs, Pn, tmp)
        k.dma("sp", x_tile_ap(x_dst, tg), xs[:], xs, reads=[xs], writes=[x_dst.T])


def emit_partial_group(k, P, pairs, reads, first, last, out_ap=None):
    if out_ap is None:
        out_ap = P[:]
    deps, raw = k._deps(reads, [P] if first else [])
    k._wait("pe", deps, raw)
    n = len(pairs)
    ins = None
    for i, (l, r) in enumerate(pairs):
        ins = k.nc.tensor.matmul(out_ap, l, r, start=(first and i == 0), stop=(last and i == n - 1))
    d = k._tick("pe")
    ins.then_inc(d[0], 1)
    k._commit(d, reads, [P] if last else [])
    if not last:
        pass


def emit_xa(k, C, NT, x_src, x_dst, W, li, base):
    k.stage(base)
    g_mem = load_small(k, W["ln_mem"][li], [128, DC])
    g_pre = load_small(k, W["ln_xa_pre"][li], [128, DC])
    g_post = load_small(k, W["ln_xa_post"][li], [128, DC])
    nb = NormBufs(k)
    xs = k.sb([128, DC, TG], F32, "xs")
    hn = k.sbs(DC, [128, TG], BF16, "hn")
    y = k.sb([128, DC, TG], F32, "y")
    tmp = k.sbs(2, [128, TG], F32, "tmp")
    ws = WStream(k, 3, 16, 256)
    kT = k.sbs(4, [128, 256], BF16, "kT")
    v = k.sbs(2, [128, 512], BF16, "v")
    qT = k.sbs(4, [128, TG], BF16, "qT")
    E = k.sbs(8, [128, TG], BF16, "E")
    oT = k.sbs(4, [128, TG], BF16, "oT")
    rec = k.sbs(2, [128, TG], F32, "rec")
    Pn = k.P[7]
    wq, wkv, wo = W["xa_wq"][li], W["xa_wkv"][li], W["xa_wo"][li]
    memT = W["memT"]
    k.dma("sp", xs[:, :, 0:256], memT.ap.rearrange("(c p) t -> p c t", p=128), xs, reads=[memT.T], writes=[xs])
    pre_norm(k, C, nb, xs, g_mem, hn, Pn, w=256)
    for t4 in range(4):
        wt = ws.load(wkv, 0, 16, t4 * 256, 256)
        if t4 < 2:
            for jj in range(2):
                h = t4 * 2 + jj
                Pb = k.P[h % 4]
                k.mmg(Pb, Pb[:, 0:256], [(wt[:, c, jj * 128:(jj + 1) * 128], hn[c][:, 0:256]) for c in range(DC)], [wt] + hn)
                k.op("act", lambda e: e.activation(out=kT[h][:], in_=Pb[:, 0:256], func=AF.Copy), reads=[Pb], writes=[kT[h]])
        else:
            half = t4 - 2
            for mc in range(2):
                Pb = k.P[4 + mc]
                k.mmg(Pb, Pb[:, half * 256:(half + 1) * 256],
                      [(hn[c][:, mc * 128:(mc + 1) * 128], wt[:, c, :]) for c in range(DC)], [wt] + hn)
                k.op("act", lambda e: e.activation(out=v[mc][:, half * 256:(half + 1) * 256],
                                                   in_=Pb[:, half * 256:(half + 1) * 256], func=AF.Copy),
                     reads=[Pb], writes=[v[mc]])
    scale = 128.0 ** -0.5
    for tg in range(NT):
        k.dma("sp", xs[:], x_tile_ap(x_src, tg), xs, reads=[x_src.T], writes=[xs])
        pre_norm(k, C, nb, xs, g_pre, hn, Pn)
        for t2 in range(2):
            wt = ws.load(wq, 0, 16, t2 * 256, 256)
            for jj in range(2):
                h = t2 * 2 + jj
                Pb = k.P[h % 4]
                k.mmg(Pb, Pb[:], [(wt[:, c, jj * 128:(jj + 1) * 128], hn[c][:]) for c in range(DC)], [wt] + hn)
                k.op("act", lambda e: e.activation(out=qT[h][:], in_=Pb[:], func=AF.Copy), reads=[Pb], writes=[qT[h]])
        for h in range(4):
            for mc in range(2):
                Pb = k.P[(2 * h + mc) % 4]
                k.mmg(Pb, Pb[:], [(kT[h][:, mc * 128:(mc + 1) * 128], qT[h][:])], [kT[h], qT[h]])
                Eh = E[2 * h + mc]
                k.op("act", lambda e: e.activation(out=Eh[:], in_=Pb[:], func=AF.Exp, scale=scale), reads=[Pb], writes=[Eh])
            Pd = k.P[4 + h % 2]
            k.mmg(Pd, Pd[:], [(C["ones_bf"][:], E[2 * h + mc][:]) for mc in range(2)], [E[2 * h], E[2 * h + 1]])
            rc = rec[h % 2]
            k.op("dve", lambda e: e.reciprocal(out=rc[:], in_=Pd[:]), reads=[Pd], writes=[rc])
            Po = k.P[6]
            k.mmg(Po, Po[:], [(v[mc][:, h * 128:(h + 1) * 128], E[2 * h + mc][:]) for mc in range(2)],
                  [v[0], v[1], E[2 * h], E[2 * h + 1]])
            k.op("dve", lambda e: e.tensor_tensor(out=oT[h][:], in0=Po[:], in1=rc[:], op=ALU.mult),
                 reads=[Po, rc], writes=[oT[h]])
        for t8 in range(8):
            wt = ws.load(wo, 0, 4, t8 * 256, 256)
            for jj in range(2):
                c = t8 * 2 + jj
                Pb = k.P[c % 4]
                k.mmg(Pb, Pb[:], [(wt[:, h, jj * 128:(jj + 1) * 128], oT[h][:]) for h in range(4)], [wt] + oT)
                k.op("act", lambda e: e.activation(out=y[:, c, :], in_=Pb[:], func=AF.Copy), reads=[Pb], writes=[y])
        post_norm_residual(k, C, nb, y, g_post, xs, Pn, tmp)
        k.dma("sp", x_tile_ap(x_dst, tg), xs[:], xs, reads=[xs], writes=[x_dst.T])


def bcast_mid(ap, n):
    pat = [list(p) for p in ap.ap]
    assert len(pat) == 2
    return bass.AP(ap.tensor, ap.offset, [pat[0], [0, n], pat[1]])


def bcast_last(ap, n):
    pat = [list(p) for p in ap.ap]
    assert len(pat) == 2
    return bass.AP(ap.tensor, ap.offset, [pat[0], pat[1], [0, n]])


def emit_outproj(k, C, NT, feat, KC, w_out, g_post_ap, x_src, x_dst, base):
    k.stage(base)
    g_post = load_small(k, g_post_ap, [128, DC])
    nb = NormBufs(k)
    xs = k.sb([128, DC, TG], F32, "xs")
    y = k.sb([128, DC, TG], F32, "y")
    f = k.sbs(2, [128, KC, TG], BF16, "feat")
    tmp = k.sbs(2, [128, TG], F32, "tmp")
    ws = WStream(k, 3, 16, 256)
    Pn = k.P[7]
    nkh = (KC + 15) // 16
    for tg in range(NT):
        ft = f[tg % 2]
        k.dma("sp", ft[:], feat.ap[0:KC * 128, tg * TG:(tg + 1) * TG].rearrange("(c p) t -> p c t", p=128), ft,
              reads=[feat.T], writes=[ft])
        k.dma("sp", xs[:], x_tile_ap(x_src, tg), xs, reads=[x_src.T], writes=[xs])
        for ct in range(8):
            for kh in range(nkh):
                kc = min(16, KC - kh * 16)
                wt = ws.load(w_out, kh * 2048, kc, ct * 256, 256)
                for jj in range(2):
                    Pb = k.P[(2 * ct + jj) % 4]
                    pairs = [(wt[:, c, jj * 128:(jj + 1) * 128], ft[:, kh * 16 + c, :]) for c in range(kc)]
                    emit_partial_group(k, Pb, pairs, [wt, ft], kh == 0, kh == nkh - 1)
            for jj in range(2):
                c = 2 * ct + jj
                Pb = k.P[c % 4]
                k.op("act", lambda e: e.activation(out=y[:, c, :], in_=Pb[:], func=AF.Copy), reads=[Pb], writes=[y])
        post_norm_residual(k, C, nb, y, g_post, xs, Pn, tmp)
        k.dma("sp", x_tile_ap(x_dst, tg), xs[:], xs, reads=[xs], writes=[x_dst.T])


def emit_sgu_front(k, C, NT, x_src, W, li, feat, base):
    k.stage(base)
    j = li // 3
    g_pre = load_small(k, W["ln_mix_pre"][li], [128, DC])
    vg = load_small(k, W["sg_v_norm_g"][j], [128, 32])
    vb = load_small(k, W["sg_v_norm_b"][j], [128, 32])
    wsp_f = load_small(k, W["sg_w_spatial"][j], [128, 16, 128])
    bias_bc = k.sb([128, 16, 128], F32, "bias_bc")
    k.dma("sp", bias_bc[:], W["sg_b_spatial"].ap[j].rearrange("g t -> (g t)").partition_broadcast(128)
          .rearrange("p (g t) -> p g t", g=16), bias_bc, writes=[bias_bc])
    k.op("dve", lambda e: e.tensor_tensor(out=wsp_f[:], in0=wsp_f[:], in1=bcast_mid(C["le_f"][:], 16), op=ALU.mult),
         reads=[wsp_f, C["le_f"]], writes=[wsp_f])
    wsp_b = k.sb([128, 16, 128], BF16, "wsp_b")
    k.op("dve", lambda e: e.tensor_copy(out=wsp_b[:], in_=wsp_f[:]), reads=[wsp_f], writes=[wsp_b])
    rs_bc = k.sb([128, 16, 128], F32, "rs_bc")
    for q in range(4):
        Pb = k.P[q]
        k.mmg(Pb, Pb[:], [(C["ones_bf"][:], wsp_b[:, 4 * q:4 * q + 4, :])], [wsp_b])
        k.op("act", lambda e: e.activation(out=rs_bc[:, 4 * q:4 * q + 4, :], in_=Pb[:], func=AF.Copy), reads=[Pb], writes=[rs_bc])
    nb = NormBufs(k)
    xs = k.sb([128, DC, TG], F32, "xs")
    hn = k.sbs(DC, [128, TG], BF16, "hn")
    uT = k.sbs(32, [128, TG], BF16, "uT")
    vraw = k.sbs(4, [128, 4096], BF16, "vraw")
    junk = k.sbs(2, [128, 256], BF16, "junk")
    s1 = k.sbs(4, [128, 16], F32, "s1")
    s2 = k.sbs(4, [128, 16], F32, "s2")
    st = k.sbs(4, [128, 8], F32, "st")
    wsn = k.sbs(4, [128, 16, 128], BF16, "wsn")
    mrep = k.sbs(4, [128, 128], BF16, "mrep")
    stsem = T(None)
    t1 = k.sbs(2, [128, TG], F32, "t1")
    ws = WStream(k, 3, 16, 256)
    w_in = W["sg_w_in"][j]
    Pn = k.P[7]
    for tg in range(NT):
        k.dma("sp", xs[:], x_tile_ap(x_src, tg), xs, reads=[x_src.T], writes=[xs])
        pre_norm(k, C, nb, xs, g_pre, hn, Pn)
        for ct in range(16):
            wt = ws.load(w_in, 0, 16, ct * 256, 256)
            for jj in range(2):
                dc = 2 * ct + jj
                Pb = k.P[dc % 4]
                k.mmg(Pb, Pb[:], [(wt[:, c, jj * 128:(jj + 1) * 128], hn[c][:]) for c in range(DC)], [wt] + hn)
                k.op("act", lambda e: e.activation(out=uT[dc][:], in_=Pb[:], func=AF.Gelu_apprx_tanh), reads=[Pb], writes=[uT[dc]])
        for ct in range(16):
            wt = ws.load(w_in, 0, 16, 4096 + ct * 256, 256)
            for tb in range(4):
                Pb = k.P[tb]
                k.mmg(Pb, Pb[:, 0:256], [(hn[c][:, tb * 128:(tb + 1) * 128], wt[:, c, :]) for c in range(DC)], [wt] + hn)
                k.op("act", lambda e: e.activation(out=vraw[tb][:, ct * 256:(ct + 1) * 256], in_=Pb[:, 0:256],
                                                   func=AF.Gelu_apprx_tanh, accum_out=s1[tb][:, ct:ct + 1]),
                     reads=[Pb], writes=[vraw[tb], s1[tb]])
                jk = junk[tb % 2]
                k.op("act", lambda e: e.activation(out=jk[:], in_=vraw[tb][:, ct * 256:(ct + 1) * 256],
                                                   func=AF.Square, accum_out=s2[tb][:, ct:ct + 1]),
                     reads=[vraw[tb]], writes=[jk, s2[tb]])
        for tb in range(4):
            S = st[tb]
            k.op("dve", lambda e: e.reduce_sum(out=S[:, 0:1], in_=s1[tb][:], axis=mybir.AxisListType.X), reads=[s1[tb]], writes=[S])
            k.op("dve", lambda e: e.reduce_sum(out=S[:, 1:2], in_=s2[tb][:], axis=mybir.AxisListType.X), reads=[s2[tb]], writes=[S])
            k.op("dve", lambda e: e.tensor_scalar(out=S[:, 0:2], in0=S[:, 0:2], scalar1=1.0 / 4096, scalar2=None, op0=ALU.mult),
                 reads=[S], writes=[S])
            k.op("dve", lambda e: e.tensor_tensor(out=S[:, 2:3], in0=S[:, 0:1], in1=S[:, 0:1], op=ALU.mult), reads=[S], writes=[S])
            k.op("dve", lambda e: e.tensor_tensor(out=S[:, 2:3], in0=S[:, 1:2], in1=S[:, 2:3], op=ALU.subtract), reads=[S], writes=[S])
            k.op("act", lambda e: e.activation(out=S[:, 3:4], in_=S[:, 2:3], func=AF.Ln, bias=EPS), reads=[S], writes=[S])
            k.op("act", lambda e: e.activation(out=S[:, 3:4], in_=S[:, 3:4], func=AF.Exp, scale=-0.5), reads=[S], writes=[S])
            k.op("dve", lambda e: e.tensor_scalar(out=S[:, 4:5], in0=S[:, 0:1], scalar1=-1.0, scalar2=None, op0=ALU.mult),
                 reads=[S], writes=[S])
            k.op("dve", lambda e: e.tensor_scalar(out=wsn[tb][:], in0=wsp_f[:], scalar1=S[:, 3:4], scalar2=None, op0=ALU.mult),
                 reads=[wsp_f, S], writes=[wsn[tb]])
            k.op("dve", lambda e: e.tensor_scalar(out=mrep[tb][:], in0=C["ones_f"][:], scalar1=S[:, 4:5], scalar2=None, op0=ALU.mult),
                 reads=[C["ones_f"], S], writes=[mrep[tb]])
        for dc in range(32):
            g = dc // 2
            Pb = k.P[dc % 4]
            for tb in range(4):
                k.mmg(Pb, Pb[:, tb * 128:(tb + 1) * 128],
                      [(vraw[tb][:, dc * 128:(dc + 1) * 128], wsn[tb][:, g, :]), (mrep[tb][:], wsn[tb][:, g, :])],
                      [vraw[tb], wsn[tb], mrep[tb]])
            tt = t1[dc % 2]
            ttv = tt[:].rearrange("p (a t) -> p a t", a=4)
            k.op("dve", lambda e: e.scalar_tensor_tensor(out=ttv, in0=Pb[:].rearrange("p (a t) -> p a t", a=4),
                                                         scalar=vg[:, dc:dc + 1], in1=bcast_mid(bias_bc[:, g, :], 4),
                                                         op0=ALU.mult, op1=ALU.add),
                 reads=[Pb, vg, bias_bc], writes=[tt])
            k.op("dve", lambda e: e.scalar_tensor_tensor(out=ttv, in0=bcast_mid(rs_bc[:, g, :], 4),
                                                          scalar=vb[:, dc:dc + 1], in1=ttv, op0=ALU.mult, op1=ALU.add),
                 reads=[rs_bc, vb, tt], writes=[tt])
            k.op("dve", lambda e: e.tensor_tensor(out=uT[dc][:], in0=tt[:], in1=uT[dc][:], op=ALU.mult),
                 reads=[tt, uT[dc]], writes=[uT[dc]])
            k.dma("sp", feat.ap[dc * 128:(dc + 1) * 128, tg * TG:(tg + 1) * TG], uT[dc][:], uT[dc],
                  reads=[uT[dc]], writes=[feat.T])


def emit_sb_front(k, C, NT, x_src, W, li, qk_d, v_d, o_d, base):
    L = NT * TG
    NB = L // 128
    j = li // 3
    w_qkv = W["sb_w_qkv"][j]
    k.stage(base)
    g_pre = load_small(k, W["ln_mix_pre"][li], [128, DC])
    nb = NormBufs(k)
    xs = k.sb([128, DC, TG], F32, "xs")
    hn = k.sbs(DC, [128, TG], BF16, "hn")
    stg = k.sbs(4, [128, TG], BF16, "stg")
    vtok = k.sbs(4, [128, 2048], BF16, "vtok")
    stsem = T(None)
    ws = WStream(k, 3, 16, 256)
    Pn = k.P[7]
    scale = 128.0 ** -0.5
    for tg in range(NT):
        k.dma("sp", xs[:], x_tile_ap(x_src, tg), xs, reads=[x_src.T], writes=[xs])
        pre_norm(k, C, nb, xs, g_pre, hn, Pn)
        for ct in range(16):
            wt = ws.load(w_qkv, 0, 16, ct * 256, 256)
            for jj in range(2):
                r = 2 * ct + jj
                Pb = k.P[r % 4]
                k.mmg(Pb, Pb[:], [(wt[:, c, jj * 128:(jj + 1) * 128], hn[c][:]) for c in range(DC)], [wt] + hn)
                sg = stg[r % 4]
                k.op("act", lambda e: e.activation(out=sg[:], in_=Pb[:], func=AF.Copy, scale=(scale if ct < 8 else 1.0)),
                     reads=[Pb], writes=[sg])
                k.dma("sp", qk_d.ap[r * 128:(r + 1) * 128, tg * TG:(tg + 1) * TG], sg[:], sg, reads=[sg], writes=[qk_d.T])
        for ct in range(8):
            wt = ws.load(w_qkv, 0, 16, 4096 + ct * 256, 256)
            for tb in range(4):
                Pb = k.P[tb]
                k.mmg(Pb, Pb[:, 0:256], [(hn[c][:, tb * 128:(tb + 1) * 128], wt[:, c, :]) for c in range(DC)], [wt] + hn)
                k.op("act", lambda e: e.activation(out=vtok[tb][:, ct * 256:(ct + 1) * 256], in_=Pb[:, 0:256], func=AF.Copy),
                     reads=[Pb], writes=[vtok[tb]])
        for tb in range(4):
            r0 = tg * TG + tb * 128
            k.dma("sp", v_d.ap[r0:r0 + 128, :], vtok[tb][:], vtok[tb], reads=[vtok[tb]], writes=[v_d.T])
    k.stage(base)
    NEGV = -30000.0
    neg = k.sb([128, 4, TG], BF16, "neg")
    negge = k.sb([128, 128], BF16, "negge")
    k.op("dve", lambda e: e.tensor_scalar(out=negge[:], in0=C["gt_f"][:], scalar1=-1.0, scalar2=NEGV, op0=ALU.add, op1=ALU.mult),
         reads=[C["gt_f"]], writes=[negge])
    k.op("dve", lambda e: e.tensor_scalar(out=negge[:], in0=C["lt_f"][:], scalar1=-1.0, scalar2=-NEGV, op0=ALU.add, op1=ALU.mult),
         reads=[C["lt_f"]], writes=[negge])
    k.op("dve", lambda e: e.memset(neg[:], 0.0), writes=[neg])
    for a in range(4):
        k.op("dve", lambda e: e.tensor_copy(out=neg[:, a, a * 128:(a + 1) * 128], in_=negge[:]), reads=[negge], writes=[neg])
        if a > 0:
            k.op("dve", lambda e: e.memset(neg[:, a, 0:a * 128], NEGV), writes=[neg])
    negrow = k.sb([1, 128], BF16, "negrow")
    k.op("dve", lambda e: e.memset(negrow[:], -1.0), writes=[negrow])
    qT = k.sbs(2, [128, L], BF16, "qT")
    kT = k.sbs(2, [128, L], BF16, "kT")
    vh = k.sbs(2, [128, NB, 128], BF16, "vh")
    eb = k.sbs(2, [128, TG], F32, "eb")
    spb = k.sbs(2, [128, TG], BF16, "spb")
    Ab = k.sbs(2, [128, TG], BF16, "Ab")
    Rf = k.sb([1, TG], F32, "Rf")
    Rb = k.sbs(2, [1, TG], BF16, "Rb")
    ob = k.sbs(2, [128, TG], BF16, "ob")
    ost = T(None)
    step = 0
    for h in range(16):
        q_, k_, v_ = qT[h % 2], kT[h % 2], vh[h % 2]
        k.dma("sp", q_[:], qk_d.ap[h * 128:(h + 1) * 128, :], q_, reads=[qk_d.T], writes=[q_])
        k.dma("sp", k_[:], qk_d.ap[2048 + h * 128:2048 + (h + 1) * 128, :], k_, reads=[qk_d.T], writes=[k_])
        k.dma("sp", v_[:], v_d.ap[:, h * 128:(h + 1) * 128].rearrange("(b p) d -> p b d", p=128), v_, reads=[v_d.T], writes=[v_])
        for tg in range(NT):
            nkb = 4 * (tg + 1)
            Po = k.P[5 + tg % 2]
            qs = q_[:, tg * TG:(tg + 1) * TG]
            k.op("dve", lambda e: e.memset(Rf[:], 0.0), writes=[Rf])
            for sb in range(nkb - 1, -1, -1):
                a = sb - 4 * tg
                ks = k_[:, sb * 128:(sb + 1) * 128]
                zp = [(ks, qs)] + ([(C["ident_bf"][:], neg[:, a, :])] if a >= 0 else [])
                zr = [k_, q_] + ([neg] if a >= 0 else [])
                P1 = k.P[step % 2]
                P2 = k.P[2 + step % 2]
                e_, sp_, A_, R_ = eb[step % 2], spb[step % 2], Ab[step % 2], Rb[step % 2]
                k.mmg(P1, P1[:], zp, zr)
                k.op("act", lambda e: e.activation(out=e_[:], in_=P1[:], func=AF.Exp), reads=[P1], writes=[e_])
                k.op("act", lambda e: e.activation(out=sp_[:], in_=e_[:], func=AF.Ln, bias=1.0), reads=[e_], writes=[sp_])
                k.op("dve", lambda e: e.tensor_copy(out=R_[:], in_=Rf[:]), reads=[Rf], writes=[R_])
                k.mmg(P2, P2[:], zp + [(C["neg_ge_bf"][:], sp_[:]), (negrow[:], R_[:])], zr + [sp_, R_, negrow, C["neg_ge_bf"]])
                k.op("act", lambda e: e.activation(out=A_[:], in_=P2[:], func=AF.Exp), reads=[P2], writes=[A_])
                emit_partial_group(k, Po, [(v_[:, sb, :], A_[:])], [v_, A_], sb == nkb - 1, sb == 0)
                if sb > 0:
                    Pt = k.P[4]
                    k.mmg(Pt, Pt[0:1, :], [(C["ones_bf"][:, 0:1], sp_[:])], [sp_])
                    k.op("dve", lambda e: e.tensor_tensor(out=Rf[:], in0=Rf[:], in1=Pt[0:1, :], op=ALU.add), reads=[Rf, Pt], writes=[Rf])
                step += 1
            o_ = ob[tg % 2]
            k.op("act", lambda e: e.activation(out=o_[:], in_=Po[:], func=AF.Copy), reads=[Po], writes=[o_])
            k.dma("sp", o_d.ap[h * 128:(h + 1) * 128, tg * TG:(tg + 1) * TG], o_[:], o_, reads=[o_], writes=[o_d.T])


def emit_ssd_front(k, C, NT, x_src, W, li, S, base):
    L = NT * TG
    NCH = L // 128
    j = li // 3
    w_in = W["ssd_w_in"][j]
    zs_d, bc_d, btok_d, xs_d, y_d, dtda_d, feat = S["featA"], S["featC"], S["featV"], S["ssdX"], S["ssdY"], S["dtda"], S["featB"]
    AX = mybir.AxisListType.X
    k.stage(base)
    g_pre = load_small(k, W["ln_mix_pre"][li], [128, DC])
    cw = load_small(k, W["ssd_conv_w"][j], [128, 48, 4])
    cb = load_small(k, W["ssd_conv_b"][j], [128, 48])
    dtb = load_small(k, W["ssd_dt_bias"][j], [64, 1])
    alog = load_small(k, W["ssd_a_log"][j], [64, 1])
    aneg = k.sb([64, 1], F32, "aneg")
    k.op("act", lambda e: e.activation(out=aneg[:], in_=alog[:], func=AF.Exp), reads=[alog], writes=[aneg])
    k.op("dve", lambda e: e.tensor_scalar(out=aneg[:], in0=aneg[:], scalar1=-1.0, scalar2=None, op0=ALU.mult), reads=[aneg], writes=[aneg])
    halo = k.sb([128, 48, 3], F32, "halo")
    k.op("pool", lambda e: e.memset(halo[:], 0.0), writes=[halo])
    nb = NormBufs(k)
    xs = k.sb([128, DC, TG], F32, "xs")
    hn = k.sbs(DC, [128, TG], BF16, "hn")
    xs_tok = k.sb([128, 4, 4096], F32, "xs_tok")
    b_tok = k.sb([128, 4, 1024], BF16, "b_tok")
    xpad = k.sbs(3, [128, TG + 3], F32, "xpad")
    cv = k.sbs(3, [128, TG], F32, "cv")
    stg = k.sbs(4, [128, TG], BF16, "stg")
    e1 = k.sb([64, TG], F32, "e1")
    dtT = k.sb([64, TG], F32, "dtT")
    daT = k.sb([64, TG], F32, "daT")
    dtda_tok = k.sb([128, 4, 128], F32, "dtda_tok")
    ws = WStream(k, 3, 16, 256)
    Pn = k.P[7]
    nstg = 0
    for tg in range(NT):
        tcols = slice(tg * TG, (tg + 1) * TG)
        k.dma("sp", xs[:], x_tile_ap(x_src, tg), xs, reads=[x_src.T], writes=[xs])
        pre_norm(k, C, nb, xs, g_pre, hn, Pn)
        for ct in range(16):
            wt = ws.load(w_in, 0, 16, ct * 256, 256)
            for jj in range(2):
                r = 2 * ct + jj
                Pb = k.P[r % 4]
                k.mmg(Pb, Pb[:], [(wt[:, c, jj * 128:(jj + 1) * 128], hn[c][:]) for c in range(DC)], [wt] + hn)
                sg = stg[nstg % 4]
                nstg += 1
                k.op("act", lambda e: e.activation(out=sg[:], in_=Pb[:], func=AF.Silu), reads=[Pb], writes=[sg])
                k.dma("sp", zs_d.ap[r * 128:(r + 1) * 128, tcols], sg[:], sg, reads=[sg], writes=[zs_d.T])
        for ct in range(24):
            wt = ws.load(w_in, 0, 16, 4096 + ct * 256, 256)
            for jj in range(2):
                ch = 2 * ct + jj
                Pb = k.P[ch % 4]
                k.mmg(Pb, Pb[:], [(wt[:, c, jj * 128:(jj + 1) * 128], hn[c][:]) for c in range(DC)], [wt] + hn)
                xp = xpad[ch % 3]
                o = cv[ch % 3]
                k.op("pool", lambda e: e.tensor_copy(out=xp[:, 0:3], in_=halo[:, ch, :]), reads=[halo], writes=[xp])
                k.op("act", lambda e: e.activation(out=xp[:, 3:TG + 3], in_=Pb[:], func=AF.Copy), reads=[Pb], writes=[xp])
                k.op("pool", lambda e: e.tensor_copy(out=halo[:, ch, :], in_=xp[:, TG:TG + 3]), reads=[xp], writes=[halo])
                k.op("dve", lambda e: e.tensor_scalar(out=o[:], in0=xp[:, 3:TG + 3], scalar1=cw[:, ch, 3:4],
                                                      scalar2=cb[:, ch:ch + 1], op0=ALU.mult, op1=ALU.add),
                     reads=[xp, cw, cb], writes=[o])
                for tap in (2, 1, 0):
                    k.op("dve", lambda e: e.scalar_tensor_tensor(out=o[:], in0=xp[:, tap:tap + TG], scalar=cw[:, ch, tap:tap + 1],
                                                                 in1=o[:], op0=ALU.mult, op1=ALU.add),
                         reads=[xp, cw, o], writes=[o])
                k.op("act", lambda e: e.activation(out=o[:], in_=o[:], func=AF.Silu), reads=[o], writes=[o])
                if ch < 40:
                    Pt = k.P[4 + ch % 2]
                    for tb in range(4):
                        k.transpose(Pt, Pt[:, tb * 128:(tb + 1) * 128], o[:, tb * 128:(tb + 1) * 128], C["ident_f"][:], [o, C["ident_f"]])
                    pv = Pt[:].rearrange("p (a t) -> p a t", a=4)
                    if ch < 32:
                        k.op("act", lambda e: e.activation(out=xs_tok[:, :, ch * 128:(ch + 1) * 128], in_=pv, func=AF.Copy),
                             reads=[Pt], writes=[xs_tok])
                    else:
                        g = ch - 32
                        k.op("act", lambda e: e.activation(out=b_tok[:, :, g * 128:(g + 1) * 128], in_=pv, func=AF.Copy),
                             reads=[Pt], writes=[b_tok])
                if ch >= 32:
                    sg = stg[nstg % 4]
                    nstg += 1
                    k.op("pool", lambda e: e.tensor_copy(out=sg[:], in_=o[:]), reads=[o], writes=[sg])
                    r = ch - 32
                    k.dma("sp", bc_d.ap[r * 128:(r + 1) * 128, tcols], sg[:], sg, reads=[sg], writes=[bc_d.T])
        wt = ws.load(w_in, 0, 16, 10240, 64)
        Pd = k.P[6]
        k.mmg(Pd, Pd[0:64, :], [(wt[:, c, 0:64], hn[c][:]) for c in range(DC)], [wt] + hn)
        k.op("act", lambda e: e.activation(out=e1[:], in_=Pd[0:64, :], func=AF.Exp, bias=dtb[:, 0:1]), reads=[Pd, dtb], writes=[e1])
        k.op("act", lambda e: e.activation(out=dtT[:], in_=e1[:], func=AF.Ln, bias=1.0), reads=[e1], writes=[dtT])
        k.op("dve", lambda e: e.tensor_scalar(out=daT[:], in0=dtT[:], scalar1=aneg[:, 0:1], scalar2=None, op0=ALU.mult),
             reads=[dtT, aneg], writes=[daT])
        Pt = k.P[4]
        for tb in range(4):
            k.transpose(Pt, Pt[:, tb * 128:tb * 128 + 64], dtT[:, tb * 128:(tb + 1) * 128], C["ident_f"][0:64, 0:64], [dtT, C["ident_f"]])
            k.transpose(Pt, Pt[:, tb * 128 + 64:tb * 128 + 128], daT[:, tb * 128:(tb + 1) * 128], C["ident_f"][0:64, 0:64], [daT, C["ident_f"]])
        k.op("act", lambda e: e.activation(out=dtda_tok[:], in_=Pt[:].rearrange("p (a t) -> p a t", a=4), func=AF.Copy),
             reads=[Pt], writes=[dtda_tok])
        rows = slice(tg * TG, (tg + 1) * TG)
        k.dma("sp", dtda_d.ap[rows, :].rearrange("(a p) f -> p a f", p=128), dtda_tok[:], dtda_tok, reads=[dtda_tok], writes=[dtda_d.T])
        k.dma("sp", xs_d.ap[rows, :].rearrange("(a p) f -> p a f", p=128), xs_tok[:], xs_tok, reads=[xs_tok], writes=[xs_d.T])
        k.dma("sp", btok_d.ap[rows, 0:1024].rearrange("(a p) f -> p a f", p=128), b_tok[:], b_tok, reads=[b_tok], writes=[btok_d.T])
    import os
    if os.environ.get("SSD_STOP") == "1":
        return
    k.stage(base)
    dbc = k.sb([128, 64], F32, "dbc")
    k.dma("sp", dbc[:], W["ssd_d"].ap[j].partition_broadcast(128), dbc, writes=[dbc])
    xs_c = k.sbs(2, [128, 4096], F32, "xs_c")
    bt_c = k.sbs(2, [128, 8, 128], BF16, "bt_c")
    ct_c = k.sbs(2, [128, 8, 128], BF16, "ct_c")
    bk_c = k.sbs(2, [128, 1024], BF16, "bk_c")
    dd_c = k.sbs(2, [128, 128], F32, "dd_c")
    xdt = k.sb([128, 4096], BF16, "xdt")
    xdtd = k.sb([128, 4096], BF16, "xdtd")
    prev_f = k.sb([128, 4096], F32, "prev_f")
    prev_b = k.sb([128, 4096], BF16, "prev_b")
    k.op("dve", lambda e: e.memset(prev_f[:], 0.0), writes=[prev_f])
    k.op("dve", lambda e: e.memset(prev_b[:], 0.0), writes=[prev_b])
    y_tok = k.sbs(2, [128, 4096], F32, "y_tok")
    acum = k.sb([128, 64], F32, "acum")
    dah = k.sb([128, 64], BF16, "dah")
    dal = k.sb([128, 64], BF16, "dal")
    ea = k.sb([128, 64], F32, "ea")
    cdb = k.sb([128, 64], F32, "cdb")
    decs = k.sb([128, 64], F32, "decs")
    w2 = k.sb([128, 64], F32, "w2")
    rhs4 = k.sbs(2, [128, 4, 128], F32, "rhs4")
    Eb = k.sbs(2, [128, TG], F32, "Eb")
    MT = k.sbs(2, [128, 4, 128], BF16, "MT")
    cbm = k.sbs(2, [128, 128], F32, "cbm")
    tA = k.sbs(2, [128, TG], F32, "tA")
    tB = k.sbs(2, [128, TG], F32, "tB")
    for c in range(NCH):
        xc, bt, ct, bk, dd = xs_c[c % 2], bt_c[c % 2], ct_c[c % 2], bk_c[c % 2], dd_c[c % 2]
        rows = slice(c * 128, (c + 1) * 128)
        k.dma("sp", xc[:], xs_d.ap[rows, :], xc, reads=[xs_d.T], writes=[xc])
        k.dma("sp", bt[:], bc_d.ap[0:1024, rows].rearrange("(g n) s -> n g s", n=128), bt, reads=[bc_d.T], writes=[bt])
        k.dma("sp", ct[:], bc_d.ap[1024:2048, rows].rearrange("(g n) s -> n g s", n=128), ct, reads=[bc_d.T], writes=[ct])
        k.dma("sp", bk[:], btok_d.ap[rows, 0:1024], bk, reads=[btok_d.T], writes=[bk])
        k.dma("sp", dd[:], dtda_d.ap[rows, :], dd, reads=[dtda_d.T], writes=[dd])
        dt_ap, da_ap = dd[:, 0:64], dd[:, 64:128]
        Pa, Ptot = k.P[0], k.P[1]
        k.op("dve", lambda e: e.tensor_copy(out=dah[:], in_=da_ap), reads=[dd], writes=[dah])
        k.op("dve", lambda e: e.tensor_tensor(out=dal[:], in0=da_ap, in1=dah[:], op=ALU.subtract), reads=[dd, dah], writes=[dal])
        k.mmg(Pa, Pa[:, 0:64], [(C["le_bf"][:], dah[:]), (C["le_bf"][:], dal[:])], [C["le_bf"], dah, dal])
        k.mmg(Ptot, Ptot[:, 0:64], [(C["ones_bf"][:], dah[:]), (C["ones_bf"][:], dal[:])], [C["ones_bf"], dah, dal])
        k.op("act", lambda e: e.activation(out=acum[:], in_=Pa[:, 0:64], func=AF.Copy), reads=[Pa], writes=[acum])
        k.op("act", lambda e: e.activation(out=ea[:], in_=Pa[:, 0:64], func=AF.Exp), reads=[Pa], writes=[ea])
        k.op("act", lambda e: e.activation(out=cdb[:], in_=Ptot[:, 0:64], func=AF.Exp), reads=[Ptot], writes=[cdb])
        k.op("dve", lambda e: e.tensor_tensor(out=decs[:], in0=Ptot[:, 0:64], in1=acum[:], op=ALU.subtract), reads=[Ptot, acum], writes=[decs])
        k.op("act", lambda e: e.activation(out=decs[:], in_=decs[:], func=AF.Exp), reads=[decs], writes=[decs])
        k.op("dve", lambda e: e.tensor_tensor(out=w2[:], in0=decs[:], in1=dt_ap, op=ALU.mult), reads=[decs, dd], writes=[w2])
        xv = xc[:].rearrange("p (h d) -> p h d", d=64)
        k.op("dve", lambda e: e.tensor_tensor(out=xdt[:].rearrange("p (h d) -> p h d", d=64), in0=xv, in1=bcast_last(dt_ap, 64), op=ALU.mult),
             reads=[xc, dd], writes=[xdt])
        k.op("pool", lambda e: e.tensor_tensor(out=xdtd[:].rearrange("p (h d) -> p h d", d=64), in0=xv, in1=bcast_last(w2[:], 64), op=ALU.mult),
             reads=[xc, w2], writes=[xdtd])
        yt = y_tok[c % 2]
        for g in range(8):
            gc = slice(g * 512, (g + 1) * 512)
            Pcb = k.P[2]
            k.mmg(Pcb, Pcb[:, 0:128], [(bt[:, g, :], ct[:, g, :])], [bt, ct])
            cm = cbm[g % 2]
            k.op("dve", lambda e: e.tensor_tensor(out=cm[:], in0=Pcb[:, 0:128], in1=C["le_f"][:], op=ALU.mult), reads=[Pcb, C["le_f"]], writes=[cm])
            Py = k.P[5]
            for hb in range(2):
                h0 = g * 8 + hb * 4
                r4 = rhs4[hb]
                k.op("dve", lambda e: e.tensor_tensor(out=r4[:], in0=bcast_mid(C["le_f"][:], 4), in1=bcast_last(dd[:, 64 + h0:64 + h0 + 4], 128),
                                                      op=ALU.mult), reads=[C["le_f"], dd], writes=[r4])
                Ps = k.P[3 + hb]
                k.mmg(Ps, Ps[:], [(C["gt_f"][:], r4[:].rearrange("p a t -> p (a t)"))], [C["gt_f"], r4])
                E_ = Eb[hb]
                k.op("act", lambda e: e.activation(out=E_[:], in_=Ps[:], func=AF.Exp), reads=[Ps], writes=[E_])
                M_ = MT[hb]
                k.op("dve", lambda e: e.tensor_tensor(out=M_[:], in0=E_[:].rearrange("p (a t) -> p a t", a=4), in1=bcast_mid(cm[:], 4), op=ALU.mult),
                     reads=[E_, cm], writes=[M_])
                for hh in range(4):
                    h = h0 + hh
                    col = (hb * 4 + hh) * 64
                    k.mmg(Py, Py[:, col:col + 64], [(M_[:, hh, :], xdt[:, h * 64:(h + 1) * 64])], [M_, xdt])
            Pyo = k.P[6]
            k.mmg(Pyo, Pyo[:], [(ct[:, g, :], prev_b[:, gc])], [ct, prev_b])
            Pst = k.P[7]
            k.mmg(Pst, Pst[:], [(bk[:, g * 128:(g + 1) * 128], xdtd[:, gc])], [bk, xdtd])
            ta, tb_ = tA[g % 2], tB[g % 2]
            v3 = lambda ap: ap.rearrange("p (h d) -> p h d", d=64)
            k.op("dve", lambda e: e.tensor_tensor(out=v3(ta[:]), in0=v3(Pyo[:]), in1=bcast_last(ea[:, g * 8:(g + 1) * 8], 64), op=ALU.mult),
                 reads=[Pyo, ea], writes=[ta])
            k.op("dve", lambda e: e.tensor_tensor(out=ta[:], in0=ta[:], in1=Py[:], op=ALU.add), reads=[ta, Py], writes=[ta])
            k.op("pool", lambda e: e.tensor_tensor(out=v3(tb_[:]), in0=v3(xc[:, gc]), in1=bcast_last(dbc[:, g * 8:(g + 1) * 8], 64), op=ALU.mult),
                 reads=[xc, dbc], writes=[tb_])
            k.op("pool", lambda e: e.tensor_tensor(out=yt[:, gc], in0=ta[:], in1=tb_[:], op=ALU.add), reads=[ta, tb_], writes=[yt])
            k.op("pool", lambda e: e.tensor_tensor(out=v3(prev_f[:, gc]), in0=v3(prev_f[:, gc]), in1=bcast_last(cdb[:, g * 8:(g + 1) * 8], 64), op=ALU.mult),
                 reads=[prev_f, cdb], writes=[prev_f])
            k.op("dve", lambda e: e.tensor_tensor(out=prev_f[:, gc], in0=prev_f[:, gc], in1=Pst[:], op=ALU.add), reads=[prev_f, Pst], writes=[prev_f])
            k.op("act", lambda e: e.activation(out=prev_b[:, gc], in_=prev_f[:, gc], func=AF.Copy), reads=[prev_f], writes=[prev_b])
        k.dma("sp", y_d.ap[rows, :], yt[:], yt, reads=[yt], writes=[y_d.T])
    if os.environ.get("SSD_STOP") == "2":
        return
    k.stage(base)
    ng = load_small(k, W["ssd_norm"][j], [128, 32])
    nb = NormBufs(k)
    ytk = k.sb([128, 4, 4096], F32, "ytk")
    yg = k.sb([128, 32, TG], F32, "yg")
    zsb = k.sbs(4, [128, TG], BF16, "zsb")
    fo = k.sb([128, 32, TG], BF16, "fo")
    for tg in range(NT):
        rows = slice(tg * TG, (tg + 1) * TG)
        tcols = slice(tg * TG, (tg + 1) * TG)
        k.dma("sp", ytk[:], y_d.ap[rows, :].rearrange("(a p) f -> p a f", p=128), ytk, reads=[y_d.T], writes=[ytk])
        for fc in range(32):
            Pt = k.P[fc % 4]
            for tb in range(4):
                k.transpose(Pt, Pt[:, tb * 128:(tb + 1) * 128], ytk[:, tb, fc * 128:(fc + 1) * 128], C["ident_f"][:], [ytk, C["ident_f"]])
            zt = zsb[fc % 4]
            k.dma("sp", zt[:], zs_d.ap[fc * 128:(fc + 1) * 128, tcols], zt, reads=[zs_d.T], writes=[zt])
            k.op("dve", lambda e: e.tensor_tensor(out=yg[:, fc, :], in0=Pt[:], in1=zt[:], op=ALU.mult),
                 reads=[Pt, zt], writes=[yg])
        rstd = rms_stats(k, C, nb, yg, 32, k.P[7], 4096)
        for fc in range(32):
            k.op("dve", lambda e: e.scalar_tensor_tensor(out=fo[:, fc, :], in0=yg[:, fc, :], scalar=ng[:, fc:fc + 1], in1=rstd[:],
                                                         op0=ALU.mult, op1=ALU.mult), reads=[yg, ng, rstd], writes=[fo])
        k.dma("sp", feat.ap[:, tcols].rearrange("(c p) t -> p c t", p=128), fo[:], fo, reads=[fo], writes=[feat.T])


WEIGHT_SPECS = {
    "ln_mix_pre": [4, 128, DC], "ln_mix_post": [4, 128, DC], "ln_mem": [4, 128, DC],
    "ln_xa_pre": [4, 128, DC], "ln_xa_post": [4, 128, DC], "ln_ffn_pre": [4, 128, DC], "ln_ffn_post": [4, 128, DC],
    "xa_wq": [4, D, 512], "xa_wkv": [4, D, 1024], "xa_wo": [4, 512, D],
    "ffn_w_in": [4, D, 2 * FFN], "ffn_conv_w": [4, 128, 88, 3], "ffn_conv_b": [4, 128, 88], "ffn_w_out": [4, FFN, D],
    "sg_w_in": [1, D, 8192], "sg_v_norm_g": [1, 128, 32], "sg_v_norm_b": [1, 128, 32],
    "sg_w_spatial": [1, 128, 16, 128], "sg_b_spatial": [1, 16, 128], "sg_w_out": [1, 4096, D],
    "sb_w_qkv": [1, D, 3 * D], "sb_w_out": [1, D, D],
    "ssd_w_in": [2, D, 10304], "ssd_conv_w": [2, 128, 48, 4], "ssd_conv_b": [2, 128, 48], "ssd_dt_bias": [2, 64, 1],
    "ssd_a_log": [2, 64, 1], "ssd_d": [2, 64], "ssd_norm": [2, 128, 32], "ssd_w_out": [2, 4096, D],
}


class DT:
    def __init__(self, nc, name, shape, dtype, kind):
        self.h = nc.dram_tensor(name, list(shape), dtype, kind=kind)
        self.ap = self.h.ap()
        self.T = T(self.h)
        self.shape = shape

    def __getitem__(self, key):
        return self.ap[key]


LAST_KB = None

NEED = {
    "ffn": ["ln_ffn_pre", "ln_ffn_post", "ffn_w_in", "ffn_conv_w", "ffn_conv_b", "ffn_w_out"],
    "xa": ["ln_mem", "ln_xa_pre", "ln_xa_post", "xa_wq", "xa_wkv", "xa_wo"],
    "mix0": ["ln_mix_pre", "ln_mix_post", "ssd_w_in", "ssd_conv_w", "ssd_conv_b", "ssd_dt_bias", "ssd_a_log", "ssd_d",
             "ssd_norm", "ssd_w_out"],
    "mix1": ["ln_mix_pre", "ln_mix_post", "sg_w_in", "sg_v_norm_g", "sg_v_norm_b", "sg_w_spatial", "sg_b_spatial", "sg_w_out"],
    "mix2": ["ln_mix_pre", "ln_mix_post", "sb_w_qkv", "sb_w_out"],
}


def needed_weights(plan):
    out = []
    for kind, li in plan:
        key = kind if kind != "mix" else f"mix{li % 3}"
        for n in NEED[key]:
            if n not in out:
                out.append(n)
    return out


def build_program(L, plan, wnames):
    global LAST_KB
    NT = L // TG
    nc = bass.Bass("TRN2", target_bir_lowering=False)
    k = KB(nc)
    LAST_KB = k
    xin = DT(nc, "xT", [D, L], F32, "ExternalInput")
    memT = DT(nc, "memT", [D, 256], F32, "ExternalInput")
    xout = DT(nc, "outT", [D, L], F32, "ExternalOutput")
    xr = DT(nc, "xr", [D, L], F32, "Internal")
    featA = DT(nc, "featA", [4096, L], BF16, "Internal")
    featB = DT(nc, "featB", [4096, L], BF16, "Internal")
    featV = DT(nc, "featV", [L, 2048], BF16, "Internal")
    S = {"featA": featA, "featB": featB, "featV": featV}
    if any(kind == "mix" and li % 3 == 0 for kind, li in plan):
        S["featC"] = DT(nc, "featC", [2048, L], BF16, "Internal")
        S["ssdX"] = DT(nc, "ssdX", [L, 4096], F32, "Internal")
        S["ssdY"] = DT(nc, "ssdY", [L, 4096], F32, "Internal")
        S["dtda"] = DT(nc, "dtda", [L, 128], F32, "Internal")
    W = {"memT": memT}
    for n in wnames:
        W[n] = DT(nc, n, WEIGHT_SPECS[n], F32, "ExternalInput")
    C = make_consts(k)
    base = k.sb_off
    cur = xin
    for i, (kind, li) in enumerate(plan):
        dst = xout if i == len(plan) - 1 else xr
        if kind == "ffn":
            emit_ffn(k, C, NT, cur, dst, W, li, base)
        elif kind == "xa":
            emit_xa(k, C, NT, cur, dst, W, li, base)
        elif kind == "mix" and li % 3 == 0:
            emit_ssd_front(k, C, NT, cur, W, li, S, base)
            emit_outproj(k, C, NT, featB, 32, W["ssd_w_out"][li // 3], W["ln_mix_post"][li], cur, dst, base)
        elif kind == "mix" and li % 3 == 2:
            emit_sb_front(k, C, NT, cur, W, li, featA, featV, featB, base)
            emit_outproj(k, C, NT, featB, 16, W["sb_w_out"][li // 3], W["ln_mix_post"][li], cur, dst, base)
        elif kind == "mix" and li % 3 == 1:
            emit_sgu_front(k, C, NT, cur, W, li, featA, base)
            emit_outproj(k, C, NT, featA, 32, W["sg_w_out"][li // 3], W["ln_mix_post"][li], cur, dst, base)
        else:
            raise ValueError(kind)
        cur = dst
    k.barrier()
    return nc


def host_layout(inputs, wnames):
    out = {}
    for n in wnames:
        a = np.asarray(inputs[n])
        if n.startswith("ln_"):
            a = a.reshape(4, DC, 128).transpose(0, 2, 1)
        elif n == "ffn_conv_w":
            a = a.reshape(4, 3, 88, 128).transpose(0, 3, 2, 1)
        elif n == "ffn_conv_b":
            a = a.reshape(4, 88, 128).transpose(0, 2, 1)
        elif n in ("sg_v_norm_g", "sg_v_norm_b"):
            a = a.reshape(1, 32, 128).transpose(0, 2, 1)
        elif n == "sg_w_spatial":
            a = a.transpose(0, 3, 1, 2)
        elif n == "ssd_conv_w":
            a = a.reshape(2, 4, 48, 128).transpose(0, 3, 2, 1)
        elif n == "ssd_conv_b":
            a = a.reshape(2, 48, 128).transpose(0, 2, 1)
        elif n in ("ssd_dt_bias", "ssd_a_log"):
            a = a.reshape(2, 64, 1)
        elif n == "ssd_norm":
            a = a.reshape(2, 32, 128).transpose(0, 2, 1)
        out[n] = np.ascontiguousarray(a)
    return out


FULL_PLAN = []
for _i in range(4):
    FULL_PLAN += [("mix", _i), ("xa", _i), ("ffn", _i)]


def kernel(**inputs):
    L = 4096
    plan = FULL_PLAN
    wn = needed_weights(plan)
    hl = host_layout(inputs, wn)
    nc = build_program(L, plan, wn)
    x = np.asarray(inputs["x"])
    mem = np.asarray(inputs["mem"])
    in_maps = []
    for b in range(8):
        m = {"xT": np.ascontiguousarray(x[b].T), "memT": np.ascontiguousarray(mem[b].T)}
        m.update(hl)
        in_maps.append(m)
    res = run_bass_kernel_spmd(nc, in_maps, core_ids=list(range(8)))
    out = np.stack([np.ascontiguousarray(res.results[b]["outT"].T) for b in range(8)], axis=0)
    return out.astype(np.float32)
```

```python
import numpy as np
import concourse.bass as bass
import concourse.mybir as mybir
from concourse.bass_utils import run_bass_kernel_spmd

F32 = mybir.dt.float32
BF16 = mybir.dt.bfloat16
AF = mybir.ActivationFunctionType
ALU = mybir.AluOpType

D = 2048
DC = 16
TG = 512
EPS = 1e-6
EPOCH = 30000
SBUF_BASE = 16384
SBUF_CAP = 229000
FFN = 5632
SAME_ENG_SYNC = True


class Dep:
    __slots__ = ("lw", "rd")

    def __init__(self):
        self.lw = None
        self.rd = {}


class T:
    def __init__(self, h, dep=None):
        self.h = h
        self.dep = dep or Dep()
        self.sem = None
        self.semcnt = 0

    def __getitem__(self, k):
        return self.h[k]


class KB:
    def __init__(self, nc):
        self.nc = nc
        self.E = {"pe": nc.tensor, "act": nc.scalar, "dve": nc.vector, "pool": nc.gpsimd, "sp": nc.sync}
        self.cur = {}
        self.seen = {e: {} for e in self.E}
        self.nsem = 0
        self.dmatiles = []
        self.free_sems = {}
        self.sb_off = SBUF_BASE
        self.nalloc = 0
        self.P = [T(nc.alloc_psum_tensor(f"bank{i}", [128, 512], F32)) for i in range(8)]

    def _newsem(self, name):
        self.nsem += 1
        return self.nc.alloc_semaphore(f"{name}_{self.nsem}")

    def _tick(self, eng):
        c = self.cur.get(eng)
        if c is None or c[1] >= EPOCH:
            c = [self._newsem(eng), 0]
            self.cur[eng] = c
        c[1] += 1
        return (c[0], c[1], eng)

    def _wait(self, eng, deps, raw_ts=()):
        for d in deps:
            if d is None:
                continue
            sem, val, src = d
            if src == eng and (eng == "pe" or not SAME_ENG_SYNC):
                continue
            if self.seen[eng].get(sem, 0) >= val:
                continue
            self.E[eng].wait_ge(sem, val)
            self.seen[eng][sem] = val

    def _deps(self, reads, writes):
        deps, raw = [], []
        for t in reads:
            deps.append(t.dep.lw)
            raw.append(t.dep.lw)
        for t in writes:
            deps.append(t.dep.lw)
            deps.extend(t.dep.rd.values())
        return deps, raw

    def _commit(self, d, reads, writes):
        for t in writes:
            t.dep.lw = d
            t.dep.rd = {}
        for t in reads:
            t.dep.rd[d[0]] = d

    def op(self, eng, fn, reads=(), writes=()):
        deps, raw = self._deps(reads, writes)
        self._wait(eng, deps, raw)
        ins = fn(self.E[eng])
        d = self._tick(eng)
        ins.then_inc(d[0], 1)
        self._commit(d, reads, writes)
        return ins

    def mmg(self, P, out_ap, pairs, reads):
        deps, raw = self._deps(reads, [P])
        self._wait("pe", deps, raw)
        n = len(pairs)
        ins = None
        for i, (l, r) in enumerate(pairs):
            ins = self.nc.tensor.matmul(out_ap, l, r, start=(i == 0), stop=(i == n - 1))
        d = self._tick("pe")
        ins.then_inc(d[0], 1)
        self._commit(d, reads, [P])

    def transpose(self, P, out_ap, in_ap, ident_ap, reads):
        deps, raw = self._deps(reads, [P])
        self._wait("pe", deps, raw)
        ins = self.nc.tensor.transpose(out_ap, in_ap, ident_ap)
        d = self._tick("pe")
        ins.then_inc(d[0], 1)
        self._commit(d, reads, [P])

    def dma(self, q, out_ap, in_ap, semT, reads=(), writes=(), accum=False):
        deps, raw = self._deps(reads, writes)
        self._wait(q, deps, raw)
        if semT.sem is None or semT.semcnt >= EPOCH:
            fl = self.free_sems.setdefault(q, [])
            if fl and fl[0][1] < EPOCH - 2000:
                semT.sem, semT.semcnt = fl.pop(0)
            else:
                semT.sem = self._newsem("dma" + q)
                semT.semcnt = 0
            semT.semq = q
            self.dmatiles.append((semT, semT.sem))
        assert semT.semq == q
        semT.semcnt += 16
        if accum:
            self.E[q].dma_start(out=out_ap, in_=in_ap, accum_op=ALU.add).then_inc(semT.sem, 16)
        else:
            self.E[q].dma_start(out=out_ap, in_=in_ap).then_inc(semT.sem, 16)
        d = (semT.sem, semT.semcnt, "dma")
        self._commit(d, reads, writes)

    def barrier(self):
        deps = [(c[0], c[1], e) for e, c in self.cur.items()]
        latest = {}
        for t, sem in self.dmatiles:
            if t.sem is sem:
                latest[id(sem)] = (sem, t.semcnt, "dma")
        deps += list(latest.values())
        for e in self.E:
            self._wait(e, deps)

    def stage(self, base=None, name=None):
        self.barrier()
        if getattr(self, "_scope", None) is not None:
            self.nc.leave_named_scope(self._scope[0], self._scope[1], False)
            self._scope = None
        if name is not None:
            self._nscope = getattr(self, "_nscope", 0) + 1
            nm = f"{self._nscope:02d}_{name}"
            sid, _ = self.nc.enter_named_scope(nm, False)
            self._scope = (nm, sid)
        for t, sem in self.dmatiles:
            if t.sem is sem:
                self.free_sems[t.semq].append((sem, t.semcnt))
                t.sem = None
        self.dmatiles = []
        if base is not None:
            self.sb_off = base

    def sb(self, shape, dtype, name="t"):
        nbytes = int(np.prod(shape[1:])) * (2 if dtype == BF16 else 4)
        nbytes = (nbytes + 31) // 32 * 32
        off = self.sb_off
        self.sb_off += nbytes
        assert self.sb_off <= SBUF_CAP, f"SBUF overflow {self.sb_off}"
        self.nalloc += 1
        return T(self.nc.alloc_sbuf_tensor_at(f"{name}_{self.nalloc}", list(shape), dtype, offset=off))

    def sbs(self, n, shape, dtype, name="t"):
        return [self.sb(shape, dtype, name) for _ in range(n)]


def make_consts(k):
    C = {}
    d = k.sb([128, 128], F32, "iota")
    k.op("pool", lambda e: e.iota(d[:], [[1, 128]], base=0, channel_multiplier=-1,
                                  allow_small_or_imprecise_dtypes=True), writes=[d])

    def cmp(name, op, dtype, val=1.0, thr=0.0):
        t = k.sb([128, 128], dtype, name)
        k.op("dve", lambda e: e.tensor_scalar(out=t[:], in0=d[:], scalar1=thr, scalar2=val, op0=op, op1=ALU.mult),
             reads=[d], writes=[t])
        C[name] = t
    cmp("ident_bf", ALU.is_equal, BF16)
    cmp("ident_f", ALU.is_equal, F32)
    cmp("le_f", ALU.is_ge, F32)
    cmp("le_bf", ALU.is_ge, BF16)
    cmp("lt_f", ALU.is_gt, F32)
    cmp("gt_f", ALU.is_lt, F32)
    cmp("neg_ge_bf", ALU.is_le, BF16, val=-1.0)
    ones = k.sb([128, 128], BF16, "ones")
    k.op("dve", lambda e: e.memset(ones[:], 1.0), writes=[ones])
    C["ones_bf"] = ones
    onesf = k.sb([128, 128], F32, "onesf")
    k.op("dve", lambda e: e.memset(onesf[:], 1.0), writes=[onesf])
    C["ones_f"] = onesf
    return C


def load_small(k, dram_ap, shape, dtype=F32, q="sp"):
    t = k.sb(shape, dtype, "small")
    k.dma(q, t[:], dram_ap, t, writes=[t])
    return t


class NormBufs:
    def __init__(self, k):
        self.sq = k.sbs(2, [128, 4, TG], BF16, "sq")
        self.lnt = k.sb([128, TG], F32, "lnt")
        self.rstd = k.sb([128, TG], F32, "rstd")


def rms_stats(k, C, nb, src, nchunks, Pn, dim, w=TG, eps=EPS):
    allp, rd = [], []
    for q in range(nchunks // 4):
        sq = nb.sq[q % 2]
        k.op("act", lambda e: e.activation(out=sq[:, :, 0:w], in_=src[:, 4 * q:4 * q + 4, 0:w], func=AF.Square),
             reads=[src], writes=[sq])
        allp = [(C["ones_bf"][:], sq[:, j, 0:w]) for j in range(4)]
        emit_partial_group(k, Pn, allp, [sq], q == 0, q == nchunks // 4 - 1, out_ap=Pn[:, 0:w])
    k.op("act", lambda e: e.activation(out=nb.lnt[:, 0:w], in_=Pn[:, 0:w], func=AF.Ln, scale=1.0 / dim, bias=eps),
         reads=[Pn], writes=[nb.lnt])
    k.op("act", lambda e: e.activation(out=nb.rstd[:, 0:w], in_=nb.lnt[:, 0:w], func=AF.Exp, scale=-0.5),
         reads=[nb.lnt], writes=[nb.rstd])
    return nb.rstd


def pre_norm(k, C, nb, xs, gcol, hn, Pn, w=TG):
    rstd = rms_stats(k, C, nb, xs, DC, Pn, D, w)
    for c in range(DC):
        k.op("dve", lambda e: e.scalar_tensor_tensor(out=hn[c][:, 0:w], in0=xs[:, c, 0:w], scalar=gcol[:, c:c + 1],
                                                     in1=rstd[:, 0:w], op0=ALU.mult, op1=ALU.mult),
             reads=[xs, rstd, gcol], writes=[hn[c]])


def post_norm_accum(k, C, nb, y, gcol, Pn, x_dst, tg, ws=None, after=1):
    rstd = rms_stats(k, C, nb, y, DC, Pn, D)
    for c in range(DC):
        k.op("dve", lambda e: e.scalar_tensor_tensor(out=y[:, c, :], in0=y[:, c, :], scalar=gcol[:, c:c + 1],
                                                     in1=rstd[:], op0=ALU.mult, op1=ALU.mult),
             reads=[y, rstd, gcol], writes=[y])

    def store():
        k.dma("pool", x_tile_ap(x_dst, tg), y[:], y, reads=[y], writes=[x_dst.tt(tg)], accum=True)
    if ws is None:
        store()
    else:
        ws.defer(store, after)


class WStream:
    def __init__(self, k, nbuf, kc, ncols):
        self.k = k
        self.bufs = k.sbs(nbuf, [128, kc, ncols], BF16, "w")
        self.i = 0
        self.pending = []

    def defer(self, fn, after=4):
        self.pending.append([after, fn])

    def flush(self):
        for _, fn in self.pending:
            fn()
        self.pending = []

    def load(self, W, k0, kc, n0, ncols):
        t = self.bufs[self.i % len(self.bufs)]
        self.i += 1
        src = W[k0:k0 + kc * 128, n0:n0 + ncols].rearrange("(c p) n -> p c n", p=128)
        self.k.dma("pool", t[:, 0:kc, 0:ncols], src, t, writes=[t])
        for p in list(self.pending):
            p[0] -= 1
            if p[0] <= 0:
                self.pending.remove(p)
                p[1]()
        return t


def run_lockstep(gens, width):
    it = iter(gens)
    active = []
    done = False
    while True:
        while not done and len(active) < width:
            g = next(it, None)
            if g is None:
                done = True
                break
            active.append(g)
        if not active:
            break
        for g in list(active):
            try:
                next(g)
            except StopIteration:
                active.remove(g)


def x_tile_ap(xd, tg):
    return xd[:, tg * TG:(tg + 1) * TG].rearrange("(c p) t -> p c t", p=128)


def emit_ffn(k, C, NT, x_src, x_dst, W, li, base):
    k.stage(base, "ffn")
    g_pre = load_small(k, W["ln_ffn_pre"][li], [128, DC])
    g_post = load_small(k, W["ln_ffn_post"][li], [128, DC])
    cw = load_small(k, W["ffn_conv_w"][li], [128, 88, 3])
    cb = load_small(k, W["ffn_conv_b"][li], [128, 88])
    halo = k.sb([128, 88, 2], F32, "halo")
    k.op("pool", lambda e: e.memset(halo[:], 0.0), writes=[halo])
    nb = NormBufs(k)
    xs = k.sb([128, DC, TG], F32, "xs")
    hn = k.sbs(DC, [128, TG], BF16, "hn")
    act = k.sbs(44, [128, TG], BF16, "act")
    y = k.sb([128, DC, TG], F32, "y")
    xpad = k.sbs(4, [128, TG + 2], F32, "xpad")
    cv = k.sbs(4, [128, TG], F32, "cv")
    ws = WStream(k, 4, 16, 256)
    w_in = W["ffn_w_in"][li]
    w_out = W["ffn_w_out"][li]
    Pn = k.P[7]
    def prefetch(tg):
        k.dma("sp", xs[:], x_tile_ap(x_src, tg), xs, reads=[x_src.tt(tg)], writes=[xs])
        pre_norm(k, C, nb, xs, g_pre, hn, Pn)

    prefetch(0)
    for tg in range(NT):
        wcache = {}

        def pair(j):
            J, jj = j // 2, j % 2
            if jj == 0:
                wcache[J] = (ws.load(w_in, 0, 16, J * 256, 256), ws.load(w_in, 0, 16, FFN + J * 256, 256))
            wts = wcache[J]
            chs = (j, j + 44)
            Pbs = (k.P[(2 * j) % 4], k.P[(2 * j + 1) % 4])
            xps = (xpad[(2 * j) % 4], xpad[(2 * j + 1) % 4])
            os_ = (cv[(2 * j) % 4], cv[(2 * j + 1) % 4])
            for hf in range(2):
                k.mmg(Pbs[hf], Pbs[hf][:], [(wts[hf][:, c, jj * 128:(jj + 1) * 128], hn[c][:]) for c in range(DC)], [wts[hf]] + hn)
            yield
            for hf in range(2):
                xp, ch, Pb = xps[hf], chs[hf], Pbs[hf]
                k.op("act", lambda e: e.activation(out=xp[:, 0:2], in_=halo[:, ch, :], func=AF.Copy), reads=[halo], writes=[xp])
                k.op("act", lambda e: e.activation(out=xp[:, 2:TG + 2], in_=Pb[:], func=AF.Copy), reads=[Pb], writes=[xp])
                k.op("act", lambda e: e.activation(out=halo[:, ch, :], in_=xp[:, TG:TG + 2], func=AF.Copy), reads=[xp], writes=[halo])
            yield
            for hf in range(2):
                xp, ch, o = xps[hf], chs[hf], os_[hf]
                k.op("dve", lambda e: e.tensor_scalar(out=o[:], in0=xp[:, 2:TG + 2], scalar1=cw[:, ch, 2:3],
                                                      scalar2=cb[:, ch:ch + 1], op0=ALU.mult, op1=ALU.add),
                     reads=[xp, cw, cb], writes=[o])
            yield
            for tap in (1, 0):
                for hf in range(2):
                    xp, ch, o = xps[hf], chs[hf], os_[hf]
                    k.op("dve", lambda e: e.scalar_tensor_tensor(out=o[:], in0=xp[:, tap:tap + TG],
                                                                 scalar=cw[:, ch, tap:tap + 1], in1=o[:],
                                                                 op0=ALU.mult, op1=ALU.add),
                         reads=[xp, cw, o], writes=[o])
                yield
            og, ou = os_
            k.op("act", lambda e: e.activation(out=og[:], in_=og[:], func=AF.Gelu_apprx_tanh), reads=[og], writes=[og])
            yield
            k.op("dve", lambda e: e.tensor_tensor(out=act[j][:], in0=og[:], in1=ou[:], op=ALU.mult),
                 reads=[og, ou], writes=[act[j]])

        run_lockstep((pair(j) for j in range(44)), 2)
        if tg + 1 < NT:
            prefetch(tg + 1)
        for nbk in range(4):
            for kp in range(4):
                wts = [ws.load(w_out, kp * 11 * 128, 11, nbk * 512 + h * 256, 256) for h in range(2)]
                for jj in range(4):
                    Pb = k.P[jj]
                    wt = wts[jj // 2]
                    pairs = [(wt[:, c, (jj % 2) * 128:(jj % 2 + 1) * 128], act[kp * 11 + c][:]) for c in range(11)]
                    emit_partial_group(k, Pb, pairs, [wt] + act[kp * 11:kp * 11 + 11], kp == 0, kp == 3)
            for jj in range(4):
                c = nbk * 4 + jj
                k.op("act", lambda e: e.activation(out=y[:, c, :], in_=k.P[jj][:], func=AF.Copy),
                     reads=[k.P[jj]], writes=[y])
        post_norm_accum(k, C, nb, y, g_post, Pn, x_dst, tg, ws, after=4)
    ws.flush()


def emit_partial_group(k, P, pairs, reads, first, last, out_ap=None):
    if out_ap is None:
        out_ap = P[:]
    deps, raw = k._deps(reads, [P] if first else [])
    k._wait("pe", deps, raw)
    n = len(pairs)
    ins = None
    for i, (l, r) in enumerate(pairs):
        ins = k.nc.tensor.matmul(out_ap, l, r, start=(first and i == 0), stop=(last and i == n - 1))
    d = k._tick("pe")
    ins.then_inc(d[0], 1)
    k._commit(d, reads, [P] if last else [])
    if not last:
        pass


def emit_xa(k, C, NT, x_src, x_dst, W, li, base):
    k.stage(base, "xa")
    g_mem = load_small(k, W["ln_mem"][li], [128, DC])
    g_pre = load_small(k, W["ln_xa_pre"][li], [128, DC])
    g_post = load_small(k, W["ln_xa_post"][li], [128, DC])
    nb = NormBufs(k)
    xs = k.sb([128, DC, TG], F32, "xs")
    hn = k.sbs(DC, [128, TG], BF16, "hn")
    y = k.sb([128, DC, TG], F32, "y")
    tmp = k.sbs(2, [128, TG], F32, "tmp")
    ws = WStream(k, 3, 16, 256)
    kT = k.sbs(4, [128, 256], BF16, "kT")
    v = k.sbs(2, [128, 512], BF16, "v")
    qT = k.sbs(4, [128, TG], BF16, "qT")
    E = k.sbs(8, [128, TG], BF16, "E")
    oT = k.sbs(4, [128, TG], BF16, "oT")
    rec = k.sbs(2, [128, TG], F32, "rec")
    Pn = k.P[7]
    wq, wkv, wo = W["xa_wq"][li], W["xa_wkv"][li], W["xa_wo"][li]
    memT = W["memT"]
    k.dma("sp", xs[:, :, 0:256], memT.ap.rearrange("(c p) t -> p c t", p=128), xs, reads=[memT.T], writes=[xs])
    pre_norm(k, C, nb, xs, g_mem, hn, Pn, w=256)
    for t4 in range(4):
        wt = ws.load(wkv, 0, 16, t4 * 256, 256)
        if t4 < 2:
            for jj in range(2):
                h = t4 * 2 + jj
                Pb = k.P[h % 4]
                k.mmg(Pb, Pb[:, 0:256], [(wt[:, c, jj * 128:(jj + 1) * 128], hn[c][:, 0:256]) for c in range(DC)], [wt] + hn)
                k.op("act", lambda e: e.activation(out=kT[h][:], in_=Pb[:, 0:256], func=AF.Copy), reads=[Pb], writes=[kT[h]])
        else:
            half = t4 - 2
            for mc in range(2):
                Pb = k.P[4 + mc]
                k.mmg(Pb, Pb[:, half * 256:(half + 1) * 256],
                      [(hn[c][:, mc * 128:(mc + 1) * 128], wt[:, c, :]) for c in range(DC)], [wt] + hn)
                k.op("act", lambda e: e.activation(out=v[mc][:, half * 256:(half + 1) * 256],
                                                   in_=Pb[:, half * 256:(half + 1) * 256], func=AF.Copy),
                     reads=[Pb], writes=[v[mc]])
    scale = 128.0 ** -0.5
    def prefetch(tg):
        k.dma("sp", xs[:], x_tile_ap(x_src, tg), xs, reads=[x_src.tt(tg)], writes=[xs])
        pre_norm(k, C, nb, xs, g_pre, hn, Pn)

    prefetch(0)
    for tg in range(NT):
        for t2 in range(2):
            wt = ws.load(wq, 0, 16, t2 * 256, 256)
            for jj in range(2):
                h = t2 * 2 + jj
                Pb = k.P[h % 4]
                k.mmg(Pb, Pb[:], [(wt[:, c, jj * 128:(jj + 1) * 128], hn[c][:]) for c in range(DC)], [wt] + hn)
                k.op("act", lambda e: e.activation(out=qT[h][:], in_=Pb[:], func=AF.Copy), reads=[Pb], writes=[qT[h]])
        if tg + 1 < NT:
            prefetch(tg + 1)
        for h in range(4):
            for mc in range(2):
                Pb = k.P[(2 * h + mc) % 4]
                k.mmg(Pb, Pb[:], [(kT[h][:, mc * 128:(mc + 1) * 128], qT[h][:])], [kT[h], qT[h]])
                Eh = E[2 * h + mc]
                k.op("act", lambda e: e.activation(out=Eh[:], in_=Pb[:], func=AF.Exp, scale=scale), reads=[Pb], writes=[Eh])
            Pd = k.P[4 + h % 2]
            k.mmg(Pd, Pd[:], [(C["ones_bf"][:], E[2 * h + mc][:]) for mc in range(2)], [E[2 * h], E[2 * h + 1]])
            rc = rec[h % 2]
            k.op("dve", lambda e: e.reciprocal(out=rc[:], in_=Pd[:]), reads=[Pd], writes=[rc])
            Po = k.P[6]
            k.mmg(Po, Po[:], [(v[mc][:, h * 128:(h + 1) * 128], E[2 * h + mc][:]) for mc in range(2)],
                  [v[0], v[1], E[2 * h], E[2 * h + 1]])
            k.op("dve", lambda e: e.tensor_tensor(out=oT[h][:], in0=Po[:], in1=rc[:], op=ALU.mult),
                 reads=[Po, rc], writes=[oT[h]])
        for t8 in range(8):
            wt = ws.load(wo, 0, 4, t8 * 256, 256)
            for jj in range(2):
                c = t8 * 2 + jj
                Pb = k.P[c % 4]
                k.mmg(Pb, Pb[:], [(wt[:, h, jj * 128:(jj + 1) * 128], oT[h][:]) for h in range(4)], [wt] + oT)
                k.op("act", lambda e: e.activation(out=y[:, c, :], in_=Pb[:], func=AF.Copy), reads=[Pb], writes=[y])
        post_norm_accum(k, C, nb, y, g_post, Pn, x_dst, tg, ws, after=2)
    ws.flush()


def bcast_mid(ap, n):
    pat = [list(p) for p in ap.ap]
    assert len(pat) == 2
    return bass.AP(ap.tensor, ap.offset, [pat[0], [0, n], pat[1]])


def bcast_last(ap, n):
    pat = [list(p) for p in ap.ap]
    assert len(pat) == 2
    return bass.AP(ap.tensor, ap.offset, [pat[0], pat[1], [0, n]])


def emit_outproj(k, C, NT, feat, KC, w_out, g_post_ap, x_src, x_dst, base):
    k.stage(base, "outproj")
    g_post = load_small(k, g_post_ap, [128, DC])
    nb = NormBufs(k)
    y = k.sb([128, DC, TG], F32, "y")
    f = k.sbs(2, [128, KC, TG], BF16, "feat")
    ws = WStream(k, 3, 16, 256)
    Pn = k.P[7]
    nkh = (KC + 15) // 16
    for tg in range(NT):
        ft = f[tg % 2]
        k.dma("sp", ft[:], feat.ap[0:KC * 128, tg * TG:(tg + 1) * TG].rearrange("(c p) t -> p c t", p=128), ft,
              reads=[feat.T], writes=[ft])
        for ct in range(8):
            for kh in range(nkh):
                kc = min(16, KC - kh * 16)
                wt = ws.load(w_out, kh * 2048, kc, ct * 256, 256)
                for jj in range(2):
                    Pb = k.P[(2 * ct + jj) % 4]
                    pairs = [(wt[:, c, jj * 128:(jj + 1) * 128], ft[:, kh * 16 + c, :]) for c in range(kc)]
                    emit_partial_group(k, Pb, pairs, [wt, ft], kh == 0, kh == nkh - 1)
            for jj in range(2):
                c = 2 * ct + jj
                Pb = k.P[c % 4]
                k.op("act", lambda e: e.activation(out=y[:, c, :], in_=Pb[:], func=AF.Copy), reads=[Pb], writes=[y])
        post_norm_accum(k, C, nb, y, g_post, Pn, x_dst, tg, ws)
    ws.flush()


def emit_sgu_front(k, C, NT, x_src, W, li, feat, base):
    k.stage(base, "sgu")
    j = li // 3
    g_pre = load_small(k, W["ln_mix_pre"][li], [128, DC])
    vg = load_small(k, W["sg_v_norm_g"][j], [128, 32])
    vb = load_small(k, W["sg_v_norm_b"][j], [128, 32])
    wsp_f = load_small(k, W["sg_w_spatial"][j], [128, 16, 128])
    bias_bc = k.sb([128, 16, 128], F32, "bias_bc")
    k.dma("sp", bias_bc[:], W["sg_b_spatial"].ap[j].rearrange("g t -> (g t)").partition_broadcast(128)
          .rearrange("p (g t) -> p g t", g=16), bias_bc, writes=[bias_bc])
    k.op("dve", lambda e: e.tensor_tensor(out=wsp_f[:], in0=wsp_f[:], in1=bcast_mid(C["le_f"][:], 16), op=ALU.mult),
         reads=[wsp_f, C["le_f"]], writes=[wsp_f])
    wsp_b = k.sb([128, 16, 128], BF16, "wsp_b")
    k.op("dve", lambda e: e.tensor_copy(out=wsp_b[:], in_=wsp_f[:]), reads=[wsp_f], writes=[wsp_b])
    rs_bc = k.sb([128, 16, 128], F32, "rs_bc")
    for q in range(4):
        Pb = k.P[q]
        k.mmg(Pb, Pb[:], [(C["ones_bf"][:], wsp_b[:, 4 * q:4 * q + 4, :])], [wsp_b])
        k.op("act", lambda e: e.activation(out=rs_bc[:, 4 * q:4 * q + 4, :], in_=Pb[:], func=AF.Copy), reads=[Pb], writes=[rs_bc])
    nb = NormBufs(k)
    xs = k.sb([128, DC, TG], F32, "xs")
    hn = k.sbs(DC, [128, TG], BF16, "hn")
    uT = k.sbs(32, [128, TG], BF16, "uT")
    vraw = k.sbs(4, [128, 4096], BF16, "vraw")
    junk = k.sbs(2, [128, 256], BF16, "junk")
    s1 = k.sbs(4, [128, 16], F32, "s1")
    s2 = k.sbs(4, [128, 16], F32, "s2")
    st = k.sbs(4, [128, 8], F32, "st")
    wsn = k.sbs(4, [128, 16, 128], BF16, "wsn")
    mrep = k.sbs(4, [128, 128], BF16, "mrep")
    stsem = T(None)
    t1 = k.sbs(2, [128, TG], F32, "t1")
    ws = WStream(k, 3, 16, 256)
    w_in = W["sg_w_in"][j]
    Pn = k.P[7]
    for tg in range(NT):
        k.dma("sp", xs[:], x_tile_ap(x_src, tg), xs, reads=[x_src.tt(tg)], writes=[xs])
        pre_norm(k, C, nb, xs, g_pre, hn, Pn)
        for ct in range(16):
            wt = ws.load(w_in, 0, 16, ct * 256, 256)
            for jj in range(2):
                dc = 2 * ct + jj
                Pb = k.P[dc % 4]
                k.mmg(Pb, Pb[:], [(wt[:, c, jj * 128:(jj + 1) * 128], hn[c][:]) for c in range(DC)], [wt] + hn)
                k.op("act", lambda e: e.activation(out=uT[dc][:], in_=Pb[:], func=AF.Gelu_apprx_tanh), reads=[Pb], writes=[uT[dc]])
        for ct in range(16):
            wt = ws.load(w_in, 0, 16, 4096 + ct * 256, 256)
            for tb in range(4):
                Pb = k.P[tb]
                k.mmg(Pb, Pb[:, 0:256], [(hn[c][:, tb * 128:(tb + 1) * 128], wt[:, c, :]) for c in range(DC)], [wt] + hn)
                k.op("act", lambda e: e.activation(out=vraw[tb][:, ct * 256:(ct + 1) * 256], in_=Pb[:, 0:256],
                                                   func=AF.Gelu_apprx_tanh, accum_out=s1[tb][:, ct:ct + 1]),
                     reads=[Pb], writes=[vraw[tb], s1[tb]])
                jk = junk[tb % 2]
                k.op("act", lambda e: e.activation(out=jk[:], in_=vraw[tb][:, ct * 256:(ct + 1) * 256],
                                                   func=AF.Square, accum_out=s2[tb][:, ct:ct + 1]),
                     reads=[vraw[tb]], writes=[jk, s2[tb]])
        for tb in range(4):
            S = st[tb]
            k.op("dve", lambda e: e.reduce_sum(out=S[:, 0:1], in_=s1[tb][:], axis=mybir.AxisListType.X), reads=[s1[tb]], writes=[S])
            k.op("dve", lambda e: e.reduce_sum(out=S[:, 1:2], in_=s2[tb][:], axis=mybir.AxisListType.X), reads=[s2[tb]], writes=[S])
            k.op("dve", lambda e: e.tensor_scalar(out=S[:, 0:2], in0=S[:, 0:2], scalar1=1.0 / 4096, scalar2=None, op0=ALU.mult),
                 reads=[S], writes=[S])
            k.op("dve", lambda e: e.tensor_tensor(out=S[:, 2:3], in0=S[:, 0:1], in1=S[:, 0:1], op=ALU.mult), reads=[S], writes=[S])
            k.op("dve", lambda e: e.tensor_tensor(out=S[:, 2:3], in0=S[:, 1:2], in1=S[:, 2:3], op=ALU.subtract), reads=[S], writes=[S])
            k.op("act", lambda e: e.activation(out=S[:, 3:4], in_=S[:, 2:3], func=AF.Ln, bias=EPS), reads=[S], writes=[S])
            k.op("act", lambda e: e.activation(out=S[:, 3:4], in_=S[:, 3:4], func=AF.Exp, scale=-0.5), reads=[S], writes=[S])
            k.op("dve", lambda e: e.tensor_scalar(out=S[:, 4:5], in0=S[:, 0:1], scalar1=-1.0, scalar2=None, op0=ALU.mult),
                 reads=[S], writes=[S])
            k.op("dve", lambda e: e.tensor_scalar(out=wsn[tb][:], in0=wsp_f[:], scalar1=S[:, 3:4], scalar2=None, op0=ALU.mult),
                 reads=[wsp_f, S], writes=[wsn[tb]])
            k.op("dve", lambda e: e.tensor_scalar(out=mrep[tb][:], in0=C["ones_f"][:], scalar1=S[:, 4:5], scalar2=None, op0=ALU.mult),
                 reads=[C["ones_f"], S], writes=[mrep[tb]])
        for dc in range(32):
            g = dc // 2
            Pb = k.P[dc % 4]
            for tb in range(4):
                k.mmg(Pb, Pb[:, tb * 128:(tb + 1) * 128],
                      [(vraw[tb][:, dc * 128:(dc + 1) * 128], wsn[tb][:, g, :]), (mrep[tb][:], wsn[tb][:, g, :])],
                      [vraw[tb], wsn[tb], mrep[tb]])
            tt = t1[dc % 2]
            ttv = tt[:].rearrange("p (a t) -> p a t", a=4)
            k.op("dve", lambda e: e.scalar_tensor_tensor(out=ttv, in0=Pb[:].rearrange("p (a t) -> p a t", a=4),
                                                         scalar=vg[:, dc:dc + 1], in1=bcast_mid(bias_bc[:, g, :], 4),
                                                         op0=ALU.mult, op1=ALU.add),
                 reads=[Pb, vg, bias_bc], writes=[tt])
            k.op("dve", lambda e: e.scalar_tensor_tensor(out=ttv, in0=bcast_mid(rs_bc[:, g, :], 4),
                                                          scalar=vb[:, dc:dc + 1], in1=ttv, op0=ALU.mult, op1=ALU.add),
                 reads=[rs_bc, vb, tt], writes=[tt])
            k.op("dve", lambda e: e.tensor_tensor(out=uT[dc][:], in0=tt[:], in1=uT[dc][:], op=ALU.mult),
                 reads=[tt, uT[dc]], writes=[uT[dc]])
            k.dma("sp", feat.ap[dc * 128:(dc + 1) * 128, tg * TG:(tg + 1) * TG], uT[dc][:], uT[dc],
                  reads=[uT[dc]], writes=[feat.T])


def emit_sb_front(k, C, NT, x_src, W, li, qk_d, v_d, o_d, base):
    L = NT * TG
    NB = L // 128
    j = li // 3
    w_qkv = W["sb_w_qkv"][j]
    k.stage(base, "sb1")
    g_pre = load_small(k, W["ln_mix_pre"][li], [128, DC])
    nb = NormBufs(k)
    xs = k.sb([128, DC, TG], F32, "xs")
    hn = k.sbs(DC, [128, TG], BF16, "hn")
    stg = k.sbs(4, [128, TG], BF16, "stg")
    vtok = k.sbs(4, [128, 2048], BF16, "vtok")
    stsem = T(None)
    ws = WStream(k, 3, 16, 256)
    Pn = k.P[7]
    scale = 128.0 ** -0.5
    for tg in range(NT):
        k.dma("sp", xs[:], x_tile_ap(x_src, tg), xs, reads=[x_src.tt(tg)], writes=[xs])
        pre_norm(k, C, nb, xs, g_pre, hn, Pn)
        for ct in range(16):
            wt = ws.load(w_qkv, 0, 16, ct * 256, 256)
            for jj in range(2):
                r = 2 * ct + jj
                Pb = k.P[r % 4]
                k.mmg(Pb, Pb[:], [(wt[:, c, jj * 128:(jj + 1) * 128], hn[c][:]) for c in range(DC)], [wt] + hn)
                sg = stg[r % 4]
                k.op("act", lambda e: e.activation(out=sg[:], in_=Pb[:], func=AF.Copy, scale=(scale if ct < 8 else 1.0)),
                     reads=[Pb], writes=[sg])
                k.dma("sp", qk_d.ap[r * 128:(r + 1) * 128, tg * TG:(tg + 1) * TG], sg[:], sg, reads=[sg], writes=[qk_d.T])
        for ct in range(8):
            wt = ws.load(w_qkv, 0, 16, 4096 + ct * 256, 256)
            for tb in range(4):
                Pb = k.P[tb]
                k.mmg(Pb, Pb[:, 0:256], [(hn[c][:, tb * 128:(tb + 1) * 128], wt[:, c, :]) for c in range(DC)], [wt] + hn)
                k.op("act", lambda e: e.activation(out=vtok[tb][:, ct * 256:(ct + 1) * 256], in_=Pb[:, 0:256], func=AF.Copy),
                     reads=[Pb], writes=[vtok[tb]])
        for tb in range(4):
            r0 = tg * TG + tb * 128
            k.dma("sp", v_d.ap[r0:r0 + 128, :], vtok[tb][:], vtok[tb], reads=[vtok[tb]], writes=[v_d.T])
    k.stage(base, "sb2")
    NEGV = -30000.0
    neg = k.sb([128, 4, TG], BF16, "neg")
    negge = k.sb([128, 128], BF16, "negge")
    k.op("dve", lambda e: e.tensor_scalar(out=negge[:], in0=C["gt_f"][:], scalar1=-1.0, scalar2=NEGV, op0=ALU.add, op1=ALU.mult),
         reads=[C["gt_f"]], writes=[negge])
    k.op("dve", lambda e: e.tensor_scalar(out=negge[:], in0=C["lt_f"][:], scalar1=-1.0, scalar2=-NEGV, op0=ALU.add, op1=ALU.mult),
         reads=[C["lt_f"]], writes=[negge])
    k.op("dve", lambda e: e.memset(neg[:], 0.0), writes=[neg])
    for a in range(4):
        k.op("dve", lambda e: e.tensor_copy(out=neg[:, a, a * 128:(a + 1) * 128], in_=negge[:]), reads=[negge], writes=[neg])
        if a > 0:
            k.op("dve", lambda e: e.memset(neg[:, a, 0:a * 128], NEGV), writes=[neg])
    negrow = k.sb([1, 128], BF16, "negrow")
    k.op("dve", lambda e: e.memset(negrow[:], -1.0), writes=[negrow])
    qT = [k.sbs(2, [128, L], BF16, "qT") for _ in range(2)]
    kT = [k.sbs(2, [128, L], BF16, "kT") for _ in range(2)]
    vh = [k.sbs(2, [128, NB, 128], BF16, "vh") for _ in range(2)]
    eb = [k.sbs(2, [128, TG], F32, "eb") for _ in range(2)]
    spb = [k.sbs(2, [128, TG], BF16, "spb") for _ in range(2)]
    Ab = [k.sbs(2, [128, TG], BF16, "Ab") for _ in range(2)]
    Rb = [k.sbs(2, [1, TG], BF16, "Rb") for _ in range(2)]
    Rf = k.sbs(2, [1, TG], F32, "Rf")
    ob = [k.sbs(2, [128, TG], BF16, "ob") for _ in range(2)]
    Pz1 = [k.P[0], k.P[1]]
    Pz2 = [k.P[2], k.P[3]]
    Po = [k.P[4], k.P[5]]
    Pt = [k.P[6], k.P[7]]
    steps = []
    for hp in range(8):
        for tg in range(NT):
            nkb = 4 * (tg + 1)
            for sb in range(nkb - 1, -1, -1):
                steps.append((hp, tg, sb, sb == nkb - 1, sb == 0))
    ns = len(steps)

    def operands(c, i):
        hp, tg, sb, first, last = steps[i]
        q_, k_, v_ = qT[c][hp % 2], kT[c][hp % 2], vh[c][hp % 2]
        a = sb - 4 * tg
        zp = [(k_[:, sb * 128:(sb + 1) * 128], q_[:, tg * TG:(tg + 1) * TG])]
        zr = [k_, q_]
        if a >= 0:
            zp.append((C["ident_bf"][:], neg[:, a, :]))
            zr.append(neg)
        return zp, zr, v_

    for i in range(-1, ns):
        f = i + 1
        for c in range(2):
            if i >= 0:
                zp, zr, v_ = operands(c, i)
                sp_, R_ = spb[c][i % 2], Rb[c][i % 2]
                k.mmg(Pz2[c], Pz2[c][:], zp + [(C["neg_ge_bf"][:], sp_[:]), (negrow[:], R_[:])],
                      zr + [sp_, R_, negrow, C["neg_ge_bf"]])
            if f < ns:
                hp, tg, sb, first, last = steps[f]
                if first and tg == 0:
                    h = 2 * hp + c
                    q_, k_, v_ = qT[c][hp % 2], kT[c][hp % 2], vh[c][hp % 2]
                    k.dma("sp", q_[:], qk_d.ap[h * 128:(h + 1) * 128, :], q_, reads=[qk_d.T], writes=[q_])
                    k.dma("sp", k_[:], qk_d.ap[2048 + h * 128:2048 + (h + 1) * 128, :], k_, reads=[qk_d.T], writes=[k_])
                    for q4 in range(0, NB, 8):
                        k.dma("sp", v_[:, q4:q4 + 8, :],
                              v_d.ap[q4 * 128:(q4 + 8) * 128, h * 128:(h + 1) * 128].rearrange("(b p) d -> p b d", p=128), v_,
                              reads=[v_d.T], writes=[v_])
                zp, zr, v_ = operands(c, f)
                k.mmg(Pz1[c], Pz1[c][:], zp, zr)
        for c in range(2):
            if i >= 0:
                A_ = Ab[c][i % 2]
                k.op("act", lambda e: e.activation(out=A_[:], in_=Pz2[c][:], func=AF.Exp), reads=[Pz2[c]], writes=[A_])
            if f < ns:
                e_ = eb[c][f % 2]
                k.op("act", lambda e: e.activation(out=e_[:], in_=Pz1[c][:], func=AF.Exp), reads=[Pz1[c]], writes=[e_])
        for c in range(2):
            if f < ns:
                e_, sp_ = eb[c][f % 2], spb[c][f % 2]
                k.op("act", lambda e: e.activation(out=sp_[:], in_=e_[:], func=AF.Ln, bias=1.0), reads=[e_], writes=[sp_])
        for c in range(2):
            if i >= 0:
                hp, tg, sb, first, last = steps[i]
                zp, zr, v_ = operands(c, i)
                A_ = Ab[c][i % 2]
                emit_partial_group(k, Po[c], [(v_[:, sb, :], A_[:])], [v_, A_], first, last)
                if last:
                    h = 2 * hp + c
                    o_ = ob[c][tg % 2]
                    k.op("dve", lambda e: e.tensor_copy(out=o_[:], in_=Po[c][:]), reads=[Po[c]], writes=[o_])
                    k.dma("sp", o_d.ap[h * 128:(h + 1) * 128, tg * TG:(tg + 1) * TG], o_[:], o_, reads=[o_], writes=[o_d.T])
            if f < ns:
                hp, tg, sb, first, last = steps[f]
                sp_, R_ = spb[c][f % 2], Rb[c][f % 2]
                if first:
                    k.op("pool", lambda e: e.memset(Rf[c][:], 0.0), writes=[Rf[c]])
                k.op("pool", lambda e: e.tensor_copy(out=R_[:], in_=Rf[c][:]), reads=[Rf[c]], writes=[R_])
                if not last:
                    k.mmg(Pt[c], Pt[c][0:1, :], [(C["ones_bf"][:, 0:1], sp_[:])], [sp_])
                    k.op("dve", lambda e: e.tensor_tensor(out=Rf[c][:], in0=Rf[c][:], in1=Pt[c][0:1, :], op=ALU.add),
                         reads=[Rf[c], Pt[c]], writes=[Rf[c]])


def emit_ssd_front(k, C, NT, x_src, W, li, S, base):
    L = NT * TG
    NCH = L // 128
    j = li // 3
    w_in = W["ssd_w_in"][j]
    zs_d, bc_d, btok_d, xs_d, y_d, dtda_d, feat = S["featA"], S["featC"], S["featV"], S["ssdX"], S["ssdY"], S["dtda"], S["featB"]
    AX = mybir.AxisListType.X
    k.stage(base, "ssd1")
    g_pre = load_small(k, W["ln_mix_pre"][li], [128, DC])
    cw = load_small(k, W["ssd_conv_w"][j], [128, 48, 4])
    cb = load_small(k, W["ssd_conv_b"][j], [128, 48])
    dtb = load_small(k, W["ssd_dt_bias"][j], [64, 1])
    alog = load_small(k, W["ssd_a_log"][j], [64, 1])
    aneg = k.sb([64, 1], F32, "aneg")
    k.op("act", lambda e: e.activation(out=aneg[:], in_=alog[:], func=AF.Exp), reads=[alog], writes=[aneg])
    k.op("dve", lambda e: e.tensor_scalar(out=aneg[:], in0=aneg[:], scalar1=-1.0, scalar2=None, op0=ALU.mult), reads=[aneg], writes=[aneg])
    halo = k.sb([128, 48, 3], F32, "halo")
    k.op("pool", lambda e: e.memset(halo[:], 0.0), writes=[halo])
    nb = NormBufs(k)
    xs = k.sb([128, DC, TG], F32, "xs")
    hn = k.sbs(DC, [128, TG], BF16, "hn")
    xs_tok = k.sb([128, 4, 4096], F32, "xs_tok")
    b_tok = k.sb([128, 4, 1024], BF16, "b_tok")
    xpad = k.sbs(3, [128, TG + 3], F32, "xpad")
    cv = k.sbs(3, [128, TG], F32, "cv")
    stg = k.sbs(4, [128, TG], BF16, "stg")
    e1 = k.sb([64, TG], F32, "e1")
    dtT = k.sb([64, TG], F32, "dtT")
    daT = k.sb([64, TG], F32, "daT")
    dtda_tok = k.sb([128, 4, 128], F32, "dtda_tok")
    ws = WStream(k, 3, 16, 256)
    Pn = k.P[7]
    nstg = 0
    for tg in range(NT):
        tcols = slice(tg * TG, (tg + 1) * TG)
        k.dma("sp", xs[:], x_tile_ap(x_src, tg), xs, reads=[x_src.tt(tg)], writes=[xs])
        pre_norm(k, C, nb, xs, g_pre, hn, Pn)
        for ct in range(16):
            wt = ws.load(w_in, 0, 16, ct * 256, 256)
            for jj in range(2):
                r = 2 * ct + jj
                Pb = k.P[r % 4]
                k.mmg(Pb, Pb[:], [(wt[:, c, jj * 128:(jj + 1) * 128], hn[c][:]) for c in range(DC)], [wt] + hn)
                sg = stg[nstg % 4]
                nstg += 1
                k.op("act", lambda e: e.activation(out=sg[:], in_=Pb[:], func=AF.Silu), reads=[Pb], writes=[sg])
                k.dma("sp", zs_d.ap[r * 128:(r + 1) * 128, tcols], sg[:], sg, reads=[sg], writes=[zs_d.T])
        for ct in range(24):
            wt = ws.load(w_in, 0, 16, 4096 + ct * 256, 256)
            for jj in range(2):
                ch = 2 * ct + jj
                Pb = k.P[ch % 4]
                k.mmg(Pb, Pb[:], [(wt[:, c, jj * 128:(jj + 1) * 128], hn[c][:]) for c in range(DC)], [wt] + hn)
                xp = xpad[ch % 3]
                o = cv[ch % 3]
                k.op("pool", lambda e: e.tensor_copy(out=xp[:, 0:3], in_=halo[:, ch, :]), reads=[halo], writes=[xp])
                k.op("act", lambda e: e.activation(out=xp[:, 3:TG + 3], in_=Pb[:], func=AF.Copy), reads=[Pb], writes=[xp])
                k.op("pool", lambda e: e.tensor_copy(out=halo[:, ch, :], in_=xp[:, TG:TG + 3]), reads=[xp], writes=[halo])
                k.op("dve", lambda e: e.tensor_scalar(out=o[:], in0=xp[:, 3:TG + 3], scalar1=cw[:, ch, 3:4],
                                                      scalar2=cb[:, ch:ch + 1], op0=ALU.mult, op1=ALU.add),
                     reads=[xp, cw, cb], writes=[o])
                for tap in (2, 1, 0):
                    k.op("dve", lambda e: e.scalar_tensor_tensor(out=o[:], in0=xp[:, tap:tap + TG], scalar=cw[:, ch, tap:tap + 1],
                                                                 in1=o[:], op0=ALU.mult, op1=ALU.add),
                         reads=[xp, cw, o], writes=[o])
                k.op("act", lambda e: e.activation(out=o[:], in_=o[:], func=AF.Silu), reads=[o], writes=[o])
                if ch < 40:
                    Pt = k.P[4 + ch % 2]
                    for tb in range(4):
                        k.transpose(Pt, Pt[:, tb * 128:(tb + 1) * 128], o[:, tb * 128:(tb + 1) * 128], C["ident_f"][:], [o, C["ident_f"]])
                    pv = Pt[:].rearrange("p (a t) -> p a t", a=4)
                    if ch < 32:
                        k.op("act", lambda e: e.activation(out=xs_tok[:, :, ch * 128:(ch + 1) * 128], in_=pv, func=AF.Copy),
                             reads=[Pt], writes=[xs_tok])
                    else:
                        g = ch - 32
                        k.op("act", lambda e: e.activation(out=b_tok[:, :, g * 128:(g + 1) * 128], in_=pv, func=AF.Copy),
                             reads=[Pt], writes=[b_tok])
                if ch >= 32:
                    sg = stg[nstg % 4]
                    nstg += 1
                    k.op("pool", lambda e: e.tensor_copy(out=sg[:], in_=o[:]), reads=[o], writes=[sg])
                    r = ch - 32
                    k.dma("sp", bc_d.ap[r * 128:(r + 1) * 128, tcols], sg[:], sg, reads=[sg], writes=[bc_d.T])
        wt = ws.load(w_in, 0, 16, 10240, 64)
        Pd = k.P[6]
        k.mmg(Pd, Pd[0:64, :], [(wt[:, c, 0:64], hn[c][:]) for c in range(DC)], [wt] + hn)
        k.op("act", lambda e: e.activation(out=e1[:], in_=Pd[0:64, :], func=AF.Exp, bias=dtb[:, 0:1]), reads=[Pd, dtb], writes=[e1])
        k.op("act", lambda e: e.activation(out=dtT[:], in_=e1[:], func=AF.Ln, bias=1.0), reads=[e1], writes=[dtT])
        k.op("dve", lambda e: e.tensor_scalar(out=daT[:], in0=dtT[:], scalar1=aneg[:, 0:1], scalar2=None, op0=ALU.mult),
             reads=[dtT, aneg], writes=[daT])
        Pt = k.P[4]
        for tb in range(4):
            k.transpose(Pt, Pt[:, tb * 128:tb * 128 + 64], dtT[:, tb * 128:(tb + 1) * 128], C["ident_f"][0:64, 0:64], [dtT, C["ident_f"]])
            k.transpose(Pt, Pt[:, tb * 128 + 64:tb * 128 + 128], daT[:, tb * 128:(tb + 1) * 128], C["ident_f"][0:64, 0:64], [daT, C["ident_f"]])
        k.op("act", lambda e: e.activation(out=dtda_tok[:], in_=Pt[:].rearrange("p (a t) -> p a t", a=4), func=AF.Copy),
             reads=[Pt], writes=[dtda_tok])
        rows = slice(tg * TG, (tg + 1) * TG)
        k.dma("sp", dtda_d.ap[rows, :].rearrange("(a p) f -> p a f", p=128), dtda_tok[:], dtda_tok, reads=[dtda_tok], writes=[dtda_d.T])
        k.dma("sp", xs_d.ap[rows, :].rearrange("(a p) f -> p a f", p=128), xs_tok[:], xs_tok, reads=[xs_tok], writes=[xs_d.T])
        k.dma("sp", btok_d.ap[rows, 0:1024].rearrange("(a p) f -> p a f", p=128), b_tok[:], b_tok, reads=[b_tok], writes=[btok_d.T])
    import os
    if os.environ.get("SSD_STOP") == "1":
        return
    k.stage(base, "ssd2")
    dbc = k.sb([128, 64], F32, "dbc")
    k.dma("sp", dbc[:], W["ssd_d"].ap[j].partition_broadcast(128), dbc, writes=[dbc])
    xs_c = k.sbs(2, [128, 4096], F32, "xs_c")
    bt_c = k.sbs(2, [128, 8, 128], BF16, "bt_c")
    ct_c = k.sbs(2, [128, 8, 128], BF16, "ct_c")
    bk_c = k.sbs(2, [128, 1024], BF16, "bk_c")
    dd_c = k.sbs(2, [128, 128], F32, "dd_c")
    xdt = k.sb([128, 4096], BF16, "xdt")
    xdtd = k.sb([128, 4096], BF16, "xdtd")
    prev_f = k.sb([128, 4096], F32, "prev_f")
    prev_b = k.sb([128, 4096], BF16, "prev_b")
    k.op("dve", lambda e: e.memset(prev_f[:], 0.0), writes=[prev_f])
    k.op("dve", lambda e: e.memset(prev_b[:], 0.0), writes=[prev_b])
    y_tok = k.sbs(2, [128, 4096], F32, "y_tok")
    acum = k.sb([128, 64], F32, "acum")
    dah = k.sb([128, 64], BF16, "dah")
    dal = k.sb([128, 64], BF16, "dal")
    ea = k.sb([128, 64], F32, "ea")
    cdb = k.sb([128, 64], F32, "cdb")
    decs = k.sb([128, 64], F32, "decs")
    w2 = k.sb([128, 64], F32, "w2")
    rhs4 = k.sbs(2, [128, 4, 128], F32, "rhs4")
    Eb = k.sbs(2, [128, TG], F32, "Eb")
    MT = k.sbs(2, [128, 4, 128], BF16, "MT")
    cbm = k.sbs(2, [128, 128], F32, "cbm")
    tA = k.sbs(2, [128, TG], F32, "tA")
    tB = k.sbs(2, [128, TG], F32, "tB")
    for c in range(NCH):
        xc, bt, ct, bk, dd = xs_c[c % 2], bt_c[c % 2], ct_c[c % 2], bk_c[c % 2], dd_c[c % 2]
        rows = slice(c * 128, (c + 1) * 128)
        k.dma("sp", xc[:], xs_d.ap[rows, :], xc, reads=[xs_d.T], writes=[xc])
        k.dma("sp", bt[:], bc_d.ap[0:1024, rows].rearrange("(g n) s -> n g s", n=128), bt, reads=[bc_d.T], writes=[bt])
        k.dma("sp", ct[:], bc_d.ap[1024:2048, rows].rearrange("(g n) s -> n g s", n=128), ct, reads=[bc_d.T], writes=[ct])
        k.dma("sp", bk[:], btok_d.ap[rows, 0:1024], bk, reads=[btok_d.T], writes=[bk])
        k.dma("sp", dd[:], dtda_d.ap[rows, :], dd, reads=[dtda_d.T], writes=[dd])
        dt_ap, da_ap = dd[:, 0:64], dd[:, 64:128]
        Pa, Ptot = k.P[0], k.P[1]
        k.op("dve", lambda e: e.tensor_copy(out=dah[:], in_=da_ap), reads=[dd], writes=[dah])
        k.op("dve", lambda e: e.tensor_tensor(out=dal[:], in0=da_ap, in1=dah[:], op=ALU.subtract), reads=[dd, dah], writes=[dal])
        k.mmg(Pa, Pa[:, 0:64], [(C["le_bf"][:], dah[:]), (C["le_bf"][:], dal[:])], [C["le_bf"], dah, dal])
        k.mmg(Ptot, Ptot[:, 0:64], [(C["ones_bf"][:], dah[:]), (C["ones_bf"][:], dal[:])], [C["ones_bf"], dah, dal])
        k.op("act", lambda e: e.activation(out=acum[:], in_=Pa[:, 0:64], func=AF.Copy), reads=[Pa], writes=[acum])
        k.op("act", lambda e: e.activation(out=ea[:], in_=Pa[:, 0:64], func=AF.Exp), reads=[Pa], writes=[ea])
        k.op("act", lambda e: e.activation(out=cdb[:], in_=Ptot[:, 0:64], func=AF.Exp), reads=[Ptot], writes=[cdb])
        k.op("dve", lambda e: e.tensor_tensor(out=decs[:], in0=Ptot[:, 0:64], in1=acum[:], op=ALU.subtract), reads=[Ptot, acum], writes=[decs])
        k.op("act", lambda e: e.activation(out=decs[:], in_=decs[:], func=AF.Exp), reads=[decs], writes=[decs])
        k.op("dve", lambda e: e.tensor_tensor(out=w2[:], in0=decs[:], in1=dt_ap, op=ALU.mult), reads=[decs, dd], writes=[w2])
        xv = xc[:].rearrange("p (h d) -> p h d", d=64)
        k.op("dve", lambda e: e.tensor_tensor(out=xdt[:].rearrange("p (h d) -> p h d", d=64), in0=xv, in1=bcast_last(dt_ap, 64), op=ALU.mult),
             reads=[xc, dd], writes=[xdt])
        k.op("pool", lambda e: e.tensor_tensor(out=xdtd[:].rearrange("p (h d) -> p h d", d=64), in0=xv, in1=bcast_last(w2[:], 64), op=ALU.mult),
             reads=[xc, w2], writes=[xdtd])
        yt = y_tok[c % 2]
        for g in range(8):
            gc = slice(g * 512, (g + 1) * 512)
            Pcb = k.P[2]
            k.mmg(Pcb, Pcb[:, 0:128], [(bt[:, g, :], ct[:, g, :])], [bt, ct])
            cm = cbm[g % 2]
            k.op("dve", lambda e: e.tensor_tensor(out=cm[:], in0=Pcb[:, 0:128], in1=C["le_f"][:], op=ALU.mult), reads=[Pcb, C["le_f"]], writes=[cm])
            Py = k.P[5]
            for hb in range(2):
                h0 = g * 8 + hb * 4
                r4 = rhs4[hb]
                k.op("dve", lambda e: e.tensor_tensor(out=r4[:], in0=bcast_mid(C["le_f"][:], 4), in1=bcast_last(dd[:, 64 + h0:64 + h0 + 4], 128),
                                                      op=ALU.mult), reads=[C["le_f"], dd], writes=[r4])
                Ps = k.P[3 + hb]
                k.mmg(Ps, Ps[:], [(C["gt_f"][:], r4[:].rearrange("p a t -> p (a t)"))], [C["gt_f"], r4])
                E_ = Eb[hb]
                k.op("act", lambda e: e.activation(out=E_[:], in_=Ps[:], func=AF.Exp), reads=[Ps], writes=[E_])
                M_ = MT[hb]
                k.op("dve", lambda e: e.tensor_tensor(out=M_[:], in0=E_[:].rearrange("p (a t) -> p a t", a=4), in1=bcast_mid(cm[:], 4), op=ALU.mult),
                     reads=[E_, cm], writes=[M_])
                for hh in range(4):
                    h = h0 + hh
                    col = (hb * 4 + hh) * 64
                    k.mmg(Py, Py[:, col:col + 64], [(M_[:, hh, :], xdt[:, h * 64:(h + 1) * 64])], [M_, xdt])
            Pyo = k.P[6]
            k.mmg(Pyo, Pyo[:], [(ct[:, g, :], prev_b[:, gc])], [ct, prev_b])
            Pst = k.P[7]
            k.mmg(Pst, Pst[:], [(bk[:, g * 128:(g + 1) * 128], xdtd[:, gc])], [bk, xdtd])
            ta, tb_ = tA[g % 2], tB[g % 2]
            v3 = lambda ap: ap.rearrange("p (h d) -> p h d", d=64)
            k.op("dve", lambda e: e.tensor_tensor(out=v3(ta[:]), in0=v3(Pyo[:]), in1=bcast_last(ea[:, g * 8:(g + 1) * 8], 64), op=ALU.mult),
                 reads=[Pyo, ea], writes=[ta])
            k.op("dve", lambda e: e.tensor_tensor(out=ta[:], in0=ta[:], in1=Py[:], op=ALU.add), reads=[ta, Py], writes=[ta])
            k.op("pool", lambda e: e.tensor_tensor(out=v3(tb_[:]), in0=v3(xc[:, gc]), in1=bcast_last(dbc[:, g * 8:(g + 1) * 8], 64), op=ALU.mult),
                 reads=[xc, dbc], writes=[tb_])
            k.op("pool", lambda e: e.tensor_tensor(out=yt[:, gc], in0=ta[:], in1=tb_[:], op=ALU.add), reads=[ta, tb_], writes=[yt])
            k.op("pool", lambda e: e.tensor_tensor(out=v3(prev_f[:, gc]), in0=v3(prev_f[:, gc]), in1=bcast_last(cdb[:, g * 8:(g + 1) * 8], 64), op=ALU.mult),
                 reads=[prev_f, cdb], writes=[prev_f])
            k.op("dve", lambda e: e.tensor_tensor(out=prev_f[:, gc], in0=prev_f[:, gc], in1=Pst[:], op=ALU.add), reads=[prev_f, Pst], writes=[prev_f])
            k.op("act", lambda e: e.activation(out=prev_b[:, gc], in_=prev_f[:, gc], func=AF.Copy), reads=[prev_f], writes=[prev_b])
        k.dma("sp", y_d.ap[rows, :], yt[:], yt, reads=[yt], writes=[y_d.T])
    if os.environ.get("SSD_STOP") == "2":
        return
    k.stage(base, "ssd3")
    ng = load_small(k, W["ssd_norm"][j], [128, 32])
    nb = NormBufs(k)
    ytk = k.sb([128, 4, 4096], F32, "ytk")
    yg = k.sb([128, 32, TG], F32, "yg")
    zsb = k.sbs(4, [128, TG], BF16, "zsb")
    fo = k.sb([128, 32, TG], BF16, "fo")
    for tg in range(NT):
        rows = slice(tg * TG, (tg + 1) * TG)
        tcols = slice(tg * TG, (tg + 1) * TG)
        k.dma("sp", ytk[:], y_d.ap[rows, :].rearrange("(a p) f -> p a f", p=128), ytk, reads=[y_d.T], writes=[ytk])
        for fc in range(32):
            Pt = k.P[fc % 4]
            for tb in range(4):
                k.transpose(Pt, Pt[:, tb * 128:(tb + 1) * 128], ytk[:, tb, fc * 128:(fc + 1) * 128], C["ident_f"][:], [ytk, C["ident_f"]])
            zt = zsb[fc % 4]
            k.dma("sp", zt[:], zs_d.ap[fc * 128:(fc + 1) * 128, tcols], zt, reads=[zs_d.T], writes=[zt])
            k.op("dve", lambda e: e.tensor_tensor(out=yg[:, fc, :], in0=Pt[:], in1=zt[:], op=ALU.mult),
                 reads=[Pt, zt], writes=[yg])
        rstd = rms_stats(k, C, nb, yg, 32, k.P[7], 4096)
        for fc in range(32):
            k.op("dve", lambda e: e.scalar_tensor_tensor(out=fo[:, fc, :], in0=yg[:, fc, :], scalar=ng[:, fc:fc + 1], in1=rstd[:],
                                                         op0=ALU.mult, op1=ALU.mult), reads=[yg, ng, rstd], writes=[fo])
        k.dma("sp", feat.ap[:, tcols].rearrange("(c p) t -> p c t", p=128), fo[:], fo, reads=[fo], writes=[feat.T])


WEIGHT_SPECS = {
    "ln_mix_pre": [4, 128, DC], "ln_mix_post": [4, 128, DC], "ln_mem": [4, 128, DC],
    "ln_xa_pre": [4, 128, DC], "ln_xa_post": [4, 128, DC], "ln_ffn_pre": [4, 128, DC], "ln_ffn_post": [4, 128, DC],
    "xa_wq": [4, D, 512], "xa_wkv": [4, D, 1024], "xa_wo": [4, 512, D],
    "ffn_w_in": [4, D, 2 * FFN], "ffn_conv_w": [4, 128, 88, 3], "ffn_conv_b": [4, 128, 88], "ffn_w_out": [4, FFN, D],
    "sg_w_in": [1, D, 8192], "sg_v_norm_g": [1, 128, 32], "sg_v_norm_b": [1, 128, 32],
    "sg_w_spatial": [1, 128, 16, 128], "sg_b_spatial": [1, 16, 128], "sg_w_out": [1, 4096, D],
    "sb_w_qkv": [1, D, 3 * D], "sb_w_out": [1, D, D],
    "ssd_w_in": [2, D, 10304], "ssd_conv_w": [2, 128, 48, 4], "ssd_conv_b": [2, 128, 48], "ssd_dt_bias": [2, 64, 1],
    "ssd_a_log": [2, 64, 1], "ssd_d": [2, 64], "ssd_norm": [2, 128, 32], "ssd_w_out": [2, 4096, D],
}


class DT:
    def __init__(self, nc, name, shape, dtype, kind):
        self.h = nc.dram_tensor(name, list(shape), dtype, kind=kind)
        self.ap = self.h.ap()
        self.T = T(self.h)
        self.shape = shape
        self._tt = {}

    def tt(self, tg):
        if tg not in self._tt:
            self._tt[tg] = T(self.h)
        return self._tt[tg]

    def __getitem__(self, key):
        return self.ap[key]


LAST_KB = None

NEED = {
    "ffn": ["ln_ffn_pre", "ln_ffn_post", "ffn_w_in", "ffn_conv_w", "ffn_conv_b", "ffn_w_out"],
    "xa": ["ln_mem", "ln_xa_pre", "ln_xa_post", "xa_wq", "xa_wkv", "xa_wo"],
    "mix0": ["ln_mix_pre", "ln_mix_post", "ssd_w_in", "ssd_conv_w", "ssd_conv_b", "ssd_dt_bias", "ssd_a_log", "ssd_d",
             "ssd_norm", "ssd_w_out"],
    "mix1": ["ln_mix_pre", "ln_mix_post", "sg_w_in", "sg_v_norm_g", "sg_v_norm_b", "sg_w_spatial", "sg_b_spatial", "sg_w_out"],
    "mix2": ["ln_mix_pre", "ln_mix_post", "sb_w_qkv", "sb_w_out"],
}


def needed_weights(plan):
    out = []
    for kind, li in plan:
        key = kind if kind != "mix" else f"mix{li % 3}"
        for n in NEED[key]:
            if n not in out:
                out.append(n)
    return out


def build_program(L, plan, wnames):
    global LAST_KB
    NT = L // TG
    nc = bass.Bass("TRN2", target_bir_lowering=False)
    k = KB(nc)
    LAST_KB = k
    xin = DT(nc, "xT", [D, L], F32, "ExternalInput")
    memT = DT(nc, "memT", [D, 256], F32, "ExternalInput")
    xout = DT(nc, "outT", [D, L], F32, "ExternalOutput")
    xr = DT(nc, "xr", [D, L], F32, "Internal")
    featA = DT(nc, "featA", [4096, L], BF16, "Internal")
    featB = DT(nc, "featB", [4096, L], BF16, "Internal")
    featV = DT(nc, "featV", [L, 2048], BF16, "Internal")
    S = {"featA": featA, "featB": featB, "featV": featV}
    if any(kind == "mix" and li % 3 == 0 for kind, li in plan):
        S["featC"] = DT(nc, "featC", [2048, L], BF16, "Internal")
        S["ssdX"] = DT(nc, "ssdX", [L, 4096], F32, "Internal")
        S["ssdY"] = DT(nc, "ssdY", [L, 4096], F32, "Internal")
        S["dtda"] = DT(nc, "dtda", [L, 128], F32, "Internal")
    W = {"memT": memT}
    for n in wnames:
        W[n] = DT(nc, n, WEIGHT_SPECS[n], F32, "ExternalInput")
    C = make_consts(k)
    base = k.sb_off
    for tg in range(NT):
        k.dma("sp", xout.ap[:, tg * TG:(tg + 1) * TG], xin.ap[:, tg * TG:(tg + 1) * TG], xout.tt(tg),
              reads=[xin.T], writes=[xout.tt(tg)])
    cur = xout
    for i, (kind, li) in enumerate(plan):
        dst = xout
        if kind == "ffn":
            emit_ffn(k, C, NT, cur, dst, W, li, base)
        elif kind == "xa":
            emit_xa(k, C, NT, cur, dst, W, li, base)
        elif kind == "mix" and li % 3 == 0:
            emit_ssd_front(k, C, NT, cur, W, li, S, base)
            emit_outproj(k, C, NT, featB, 32, W["ssd_w_out"][li // 3], W["ln_mix_post"][li], cur, dst, base)
        elif kind == "mix" and li % 3 == 2:
            emit_sb_front(k, C, NT, cur, W, li, featA, featV, featB, base)
            emit_outproj(k, C, NT, featB, 16, W["sb_w_out"][li // 3], W["ln_mix_post"][li], cur, dst, base)
        elif kind == "mix" and li % 3 == 1:
            emit_sgu_front(k, C, NT, cur, W, li, featA, base)
            emit_outproj(k, C, NT, featA, 32, W["sg_w_out"][li // 3], W["ln_mix_post"][li], cur, dst, base)
        else:
            raise ValueError(kind)
        cur = dst
    k.stage(None, None)
    return nc


def host_layout(inputs, wnames):
    out = {}
    for n in wnames:
        a = np.asarray(inputs[n])
        if n.startswith("ln_"):
            a = a.reshape(4, DC, 128).transpose(0, 2, 1)
        elif n == "ffn_conv_w":
            a = a.reshape(4, 3, 88, 128).transpose(0, 3, 2, 1)
        elif n == "ffn_conv_b":
            a = a.reshape(4, 88, 128).transpose(0, 2, 1)
        elif n in ("sg_v_norm_g", "sg_v_norm_b"):
            a = a.reshape(1, 32, 128).transpose(0, 2, 1)
        elif n == "sg_w_spatial":
            a = a.transpose(0, 3, 1, 2)
        elif n == "ssd_conv_w":
            a = a.reshape(2, 4, 48, 128).transpose(0, 3, 2, 1)
        elif n == "ssd_conv_b":
            a = a.reshape(2, 48, 128).transpose(0, 2, 1)
        elif n in ("ssd_dt_bias", "ssd_a_log"):
            a = a.reshape(2, 64, 1)
        elif n == "ssd_norm":
            a = a.reshape(2, 32, 128).transpose(0, 2, 1)
        out[n] = np.ascontiguousarray(a)
    return out


FULL_PLAN = []
for _i in range(4):
    FULL_PLAN += [("mix", _i), ("xa", _i), ("ffn", _i)]


def kernel(**inputs):
    L = 4096
    plan = FULL_PLAN
    wn = needed_weights(plan)
    hl = host_layout(inputs, wn)
    nc = build_program(L, plan, wn)
    x = np.asarray(inputs["x"])
    mem = np.asarray(inputs["mem"])
    in_maps = []
    for b in range(8):
        m = {"xT": np.ascontiguousarray(x[b].T), "memT": np.ascontiguousarray(mem[b].T)}
        m.update(hl)
        in_maps.append(m)
    res = run_bass_kernel_spmd(nc, in_maps, core_ids=list(range(8)))
    out = np.stack([np.ascontiguousarray(res.results[b]["outT"].T) for b in range(8)], axis=0)
    return out.astype(np.float32)
```

```python
import numpy as np
import concourse.bass as bass
import concourse.mybir as mybir
from concourse.bass_utils import run_bass_kernel_spmd

F32 = mybir.dt.float32
BF16 = mybir.dt.bfloat16
AF = mybir.ActivationFunctionType
ALU = mybir.AluOpType

D = 2048
DC = 16
TG = 512
EPS = 1e-6
EPOCH = 30000
SBUF_BASE = 16384
SBUF_CAP = 229000
FFN = 5632
SAME_ENG_SYNC = True


class Dep:
    __slots__ = ("lw", "rd")

    def __init__(self):
        self.lw = None
        self.rd = {}


class T:
    def __init__(self, h, dep=None):
        self.h = h
        self.dep = dep or Dep()
        self.sem = None
        self.semcnt = 0

    def __getitem__(self, k):
        return self.h[k]


class KB:
    def __init__(self, nc):
        self.nc = nc
        self.E = {"pe": nc.tensor, "act": nc.scalar, "dve": nc.vector, "pool": nc.gpsimd, "sp": nc.sync}
        self.cur = {}
        self.seen = {e: {} for e in self.E}
        self.nsem = 0
        self.dmatiles = []
        self.free_sems = {}
        self.sb_off = SBUF_BASE
        self.nalloc = 0
        self.P = [T(nc.alloc_psum_tensor(f"bank{i}", [128, 512], F32)) for i in range(8)]

    def _newsem(self, name):
        self.nsem += 1
        return self.nc.alloc_semaphore(f"{name}_{self.nsem}")

    def _tick(self, eng):
        c = self.cur.get(eng)
        if c is None or c[1] >= EPOCH:
            c = [self._newsem(eng), 0]
            self.cur[eng] = c
        c[1] += 1
        return (c[0], c[1], eng)

    def _wait(self, eng, deps, raw_ts=()):
        for d in deps:
            if d is None:
                continue
            sem, val, src = d
            if src == eng and (eng == "pe" or not SAME_ENG_SYNC):
                continue
            if self.seen[eng].get(sem, 0) >= val:
                continue
            self.E[eng].wait_ge(sem, val)
            self.seen[eng][sem] = val

    def _deps(self, reads, writes):
        deps, raw = [], []
        for t in reads:
            deps.append(t.dep.lw)
            raw.append(t.dep.lw)
        for t in writes:
            deps.append(t.dep.lw)
            deps.extend(t.dep.rd.values())
        return deps, raw

    def _commit(self, d, reads, writes):
        for t in writes:
            t.dep.lw = d
            t.dep.rd = {}
        for t in reads:
            t.dep.rd[d[0]] = d

    def op(self, eng, fn, reads=(), writes=()):
        deps, raw = self._deps(reads, writes)
        self._wait(eng, deps, raw)
        ins = fn(self.E[eng])
        d = self._tick(eng)
        ins.then_inc(d[0], 1)
        self._commit(d, reads, writes)
        return ins

    def mmg(self, P, out_ap, pairs, reads):
        deps, raw = self._deps(reads, [P])
        self._wait("pe", deps, raw)
        n = len(pairs)
        ins = None
        for i, (l, r) in enumerate(pairs):
            ins = self.nc.tensor.matmul(out_ap, l, r, start=(i == 0), stop=(i == n - 1))
        d = self._tick("pe")
        ins.then_inc(d[0], 1)
        self._commit(d, reads, [P])

    def transpose(self, P, out_ap, in_ap, ident_ap, reads):
        deps, raw = self._deps(reads, [P])
        self._wait("pe", deps, raw)
        ins = self.nc.tensor.transpose(out_ap, in_ap, ident_ap)
        d = self._tick("pe")
        ins.then_inc(d[0], 1)
        self._commit(d, reads, [P])

    def dma(self, q, out_ap, in_ap, semT, reads=(), writes=(), accum=False):
        deps, raw = self._deps(reads, writes)
        self._wait(q, deps, raw)
        if semT.sem is None or semT.semcnt >= EPOCH:
            fl = self.free_sems.setdefault(q, [])
            if fl and fl[0][1] < EPOCH - 2000:
                semT.sem, semT.semcnt = fl.pop(0)
            else:
                semT.sem = self._newsem("dma" + q)
                semT.semcnt = 0
            semT.semq = q
            self.dmatiles.append((semT, semT.sem))
        assert semT.semq == q
        semT.semcnt += 16
        if accum:
            self.E[q].dma_start(out=out_ap, in_=in_ap, accum_op=ALU.add).then_inc(semT.sem, 16)
        else:
            self.E[q].dma_start(out=out_ap, in_=in_ap).then_inc(semT.sem, 16)
        d = (semT.sem, semT.semcnt, "dma")
        self._commit(d, reads, writes)

    def barrier(self):
        deps = [(c[0], c[1], e) for e, c in self.cur.items()]
        latest = {}
        for t, sem in self.dmatiles:
            if t.sem is sem:
                latest[id(sem)] = (sem, t.semcnt, "dma")
        deps += list(latest.values())
        for e in self.E:
            self._wait(e, deps)

    def stage(self, base=None, name=None):
        self.barrier()
        if getattr(self, "_scope", None) is not None:
            self.nc.leave_named_scope(self._scope[0], self._scope[1], False)
            self._scope = None
        if name is not None:
            self._nscope = getattr(self, "_nscope", 0) + 1
            nm = f"{self._nscope:02d}_{name}"
            sid, _ = self.nc.enter_named_scope(nm, False)
            self._scope = (nm, sid)
        for t, sem in self.dmatiles:
            if t.sem is sem:
                self.free_sems[t.semq].append((sem, t.semcnt))
                t.sem = None
        self.dmatiles = []
        if base is not None:
            self.sb_off = base

    def sb(self, shape, dtype, name="t"):
        nbytes = int(np.prod(shape[1:])) * (2 if dtype == BF16 else 4)
        nbytes = (nbytes + 31) // 32 * 32
        off = self.sb_off
        self.sb_off += nbytes
        assert self.sb_off <= SBUF_CAP, f"SBUF overflow {self.sb_off}"
        self.nalloc += 1
        return T(self.nc.alloc_sbuf_tensor_at(f"{name}_{self.nalloc}", list(shape), dtype, offset=off))

    def sbs(self, n, shape, dtype, name="t"):
        return [self.sb(shape, dtype, name) for _ in range(n)]


def make_consts(k):
    C = {}
    d = k.sb([128, 128], F32, "iota")
    k.op("pool", lambda e: e.iota(d[:], [[1, 128]], base=0, channel_multiplier=-1,
                                  allow_small_or_imprecise_dtypes=True), writes=[d])

    def cmp(name, op, dtype, val=1.0, thr=0.0):
        t = k.sb([128, 128], dtype, name)
        k.op("dve", lambda e: e.tensor_scalar(out=t[:], in0=d[:], scalar1=thr, scalar2=val, op0=op, op1=ALU.mult),
             reads=[d], writes=[t])
        C[name] = t
    cmp("ident_bf", ALU.is_equal, BF16)
    cmp("ident_f", ALU.is_equal, F32)
    cmp("le_f", ALU.is_ge, F32)
    cmp("le_bf", ALU.is_ge, BF16)
    cmp("lt_f", ALU.is_gt, F32)
    cmp("gt_f", ALU.is_lt, F32)
    cmp("neg_ge_bf", ALU.is_le, BF16, val=-1.0)
    ones = k.sb([128, 128], BF16, "ones")
    k.op("dve", lambda e: e.memset(ones[:], 1.0), writes=[ones])
    C["ones_bf"] = ones
    onesf = k.sb([128, 128], F32, "onesf")
    k.op("dve", lambda e: e.memset(onesf[:], 1.0), writes=[onesf])
    C["ones_f"] = onesf
    return C


def load_small(k, dram_ap, shape, dtype=F32, q="sp"):
    t = k.sb(shape, dtype, "small")
    k.dma(q, t[:], dram_ap, t, writes=[t])
    return t


class NormBufs:
    def __init__(self, k):
        self.sq = k.sbs(2, [128, 4, TG], BF16, "sq")
        self.lnt = k.sb([128, TG], F32, "lnt")
        self.rstd = k.sb([128, TG], F32, "rstd")


def rms_stats(k, C, nb, src, nchunks, Pn, dim, w=TG, eps=EPS):
    allp, rd = [], []
    for q in range(nchunks // 4):
        sq = nb.sq[q % 2]
        k.op("act", lambda e: e.activation(out=sq[:, :, 0:w], in_=src[:, 4 * q:4 * q + 4, 0:w], func=AF.Square),
             reads=[src], writes=[sq])
        allp = [(C["ones_bf"][:], sq[:, j, 0:w]) for j in range(4)]
        emit_partial_group(k, Pn, allp, [sq], q == 0, q == nchunks // 4 - 1, out_ap=Pn[:, 0:w])
    k.op("act", lambda e: e.activation(out=nb.lnt[:, 0:w], in_=Pn[:, 0:w], func=AF.Ln, scale=1.0 / dim, bias=eps),
         reads=[Pn], writes=[nb.lnt])
    k.op("act", lambda e: e.activation(out=nb.rstd[:, 0:w], in_=nb.lnt[:, 0:w], func=AF.Exp, scale=-0.5),
         reads=[nb.lnt], writes=[nb.rstd])
    return nb.rstd


def pre_norm(k, C, nb, xs, gcol, hn, Pn, w=TG):
    rstd = rms_stats(k, C, nb, xs, DC, Pn, D, w)
    for c in range(DC):
        k.op("dve", lambda e: e.scalar_tensor_tensor(out=hn[c][:, 0:w], in0=xs[:, c, 0:w], scalar=gcol[:, c:c + 1],
                                                     in1=rstd[:, 0:w], op0=ALU.mult, op1=ALU.mult),
             reads=[xs, rstd, gcol], writes=[hn[c]])


def post_norm_accum(k, C, nb, y, gcol, Pn, x_dst, tg, ws=None, after=1):
    rstd = rms_stats(k, C, nb, y, DC, Pn, D)
    for c in range(DC):
        k.op("dve", lambda e: e.scalar_tensor_tensor(out=y[:, c, :], in0=y[:, c, :], scalar=gcol[:, c:c + 1],
                                                     in1=rstd[:], op0=ALU.mult, op1=ALU.mult),
             reads=[y, rstd, gcol], writes=[y])

    def store():
        k.dma("pool", x_tile_ap(x_dst, tg), y[:], y, reads=[y], writes=[x_dst.tt(tg)], accum=True)
    if ws is None:
        store()
    else:
        ws.defer(store, after)


class WStream:
    def __init__(self, k, nbuf, kc, ncols):
        self.k = k
        self.bufs = k.sbs(nbuf, [128, kc, ncols], BF16, "w")
        self.i = 0
        self.pending = []

    def defer(self, fn, after=4):
        self.pending.append([after, fn])

    def flush(self):
        for _, fn in self.pending:
            fn()
        self.pending = []

    def load(self, W, k0, kc, n0, ncols):
        t = self.bufs[self.i % len(self.bufs)]
        self.i += 1
        src = W[k0:k0 + kc * 128, n0:n0 + ncols].rearrange("(c p) n -> p c n", p=128)
        self.k.dma("pool", t[:, 0:kc, 0:ncols], src, t, writes=[t])
        for p in list(self.pending):
            p[0] -= 1
            if p[0] <= 0:
                self.pending.remove(p)
                p[1]()
        return t


def run_lockstep(gens, width):
    it = iter(gens)
    active = []
    done = False
    while True:
        while not done and len(active) < width:
            g = next(it, None)
            if g is None:
                done = True
                break
            active.append(g)
        if not active:
            break
        for g in list(active):
            try:
                next(g)
            except StopIteration:
                active.remove(g)


def x_tile_ap(xd, tg):
    return xd[:, tg * TG:(tg + 1) * TG].rearrange("(c p) t -> p c t", p=128)


def emit_ffn(k, C, NT, x_src, x_dst, W, li, base):
    k.stage(base, "ffn")
    g_pre = load_small(k, W["ln_ffn_pre"][li], [128, DC])
    g_post = load_small(k, W["ln_ffn_post"][li], [128, DC])
    cw = load_small(k, W["ffn_conv_w"][li], [128, 88, 3])
    cb = load_small(k, W["ffn_conv_b"][li], [128, 88])
    halo = k.sb([128, 88, 2], F32, "halo")
    k.op("pool", lambda e: e.memset(halo[:], 0.0), writes=[halo])
    nb = NormBufs(k)
    xs = k.sb([128, DC, TG], F32, "xs")
    hn = k.sbs(DC, [128, TG], BF16, "hn")
    act = k.sbs(44, [128, TG], BF16, "act")
    y = k.sb([128, DC, TG], F32, "y")
    xpad = k.sbs(4, [128, TG + 2], F32, "xpad")
    cv = k.sbs(4, [128, TG], F32, "cv")
    ws = WStream(k, 4, 16, 256)
    w_in = W["ffn_w_in"][li]
    w_out = W["ffn_w_out"][li]
    Pn = k.P[7]
    def prefetch(tg):
        k.dma("sp", xs[:], x_tile_ap(x_src, tg), xs, reads=[x_src.tt(tg)], writes=[xs])
        pre_norm(k, C, nb, xs, g_pre, hn, Pn)

    prefetch(0)
    for tg in range(NT):
        wcache = {}

        def pair(j):
            J, jj = j // 2, j % 2
            if jj == 0:
                wcache[J] = (ws.load(w_in, 0, 16, J * 256, 256), ws.load(w_in, 0, 16, FFN + J * 256, 256))
            wts = wcache[J]
            chs = (j, j + 44)
            Pbs = (k.P[(2 * j) % 4], k.P[(2 * j + 1) % 4])
            xps = (xpad[(2 * j) % 4], xpad[(2 * j + 1) % 4])
            os_ = (cv[(2 * j) % 4], cv[(2 * j + 1) % 4])
            for hf in range(2):
                k.mmg(Pbs[hf], Pbs[hf][:], [(wts[hf][:, c, jj * 128:(jj + 1) * 128], hn[c][:]) for c in range(DC)], [wts[hf]] + hn)
            yield
            for hf in range(2):
                xp, ch, Pb = xps[hf], chs[hf], Pbs[hf]
                k.op("act", lambda e: e.activation(out=xp[:, 0:2], in_=halo[:, ch, :], func=AF.Copy), reads=[halo], writes=[xp])
                k.op("act", lambda e: e.activation(out=xp[:, 2:TG + 2], in_=Pb[:], func=AF.Copy), reads=[Pb], writes=[xp])
                k.op("act", lambda e: e.activation(out=halo[:, ch, :], in_=xp[:, TG:TG + 2], func=AF.Copy), reads=[xp], writes=[halo])
            yield
            for hf in range(2):
                xp, ch, o = xps[hf], chs[hf], os_[hf]
                k.op("dve", lambda e: e.tensor_scalar(out=o[:], in0=xp[:, 2:TG + 2], scalar1=cw[:, ch, 2:3],
                                                      scalar2=cb[:, ch:ch + 1], op0=ALU.mult, op1=ALU.add),
                     reads=[xp, cw, cb], writes=[o])
            yield
            for tap in (1, 0):
                for hf in range(2):
                    xp, ch, o = xps[hf], chs[hf], os_[hf]
                    k.op("dve", lambda e: e.scalar_tensor_tensor(out=o[:], in0=xp[:, tap:tap + TG],
                                                                 scalar=cw[:, ch, tap:tap + 1], in1=o[:],
                                                                 op0=ALU.mult, op1=ALU.add),
                         reads=[xp, cw, o], writes=[o])
                yield
            og, ou = os_
            k.op("act", lambda e: e.activation(out=og[:], in_=og[:], func=AF.Gelu_apprx_tanh), reads=[og], writes=[og])
            yield
            k.op("dve", lambda e: e.tensor_tensor(out=act[j][:], in0=og[:], in1=ou[:], op=ALU.mult),
                 reads=[og, ou], writes=[act[j]])

        run_lockstep((pair(j) for j in range(44)), 2)
        if tg + 1 < NT:
            prefetch(tg + 1)
        for nbk in range(4):
            for kp in range(4):
                wts = [ws.load(w_out, kp * 11 * 128, 11, nbk * 512 + h * 256, 256) for h in range(2)]
                for jj in range(4):
                    Pb = k.P[jj]
                    wt = wts[jj // 2]
                    pairs = [(wt[:, c, (jj % 2) * 128:(jj % 2 + 1) * 128], act[kp * 11 + c][:]) for c in range(11)]
                    emit_partial_group(k, Pb, pairs, [wt] + act[kp * 11:kp * 11 + 11], kp == 0, kp == 3)
            for jj in range(4):
                c = nbk * 4 + jj
                k.op("act", lambda e: e.activation(out=y[:, c, :], in_=k.P[jj][:], func=AF.Copy),
                     reads=[k.P[jj]], writes=[y])
        post_norm_accum(k, C, nb, y, g_post, Pn, x_dst, tg, ws, after=4)
    ws.flush()


def emit_partial_group(k, P, pairs, reads, first, last, out_ap=None):
    if out_ap is None:
        out_ap = P[:]
    deps, raw = k._deps(reads, [P] if first else [])
    k._wait("pe", deps, raw)
    n = len(pairs)
    ins = None
    for i, (l, r) in enumerate(pairs):
        ins = k.nc.tensor.matmul(out_ap, l, r, start=(first and i == 0), stop=(last and i == n - 1))
    d = k._tick("pe")
    ins.then_inc(d[0], 1)
    k._commit(d, reads, [P] if last else [])
    if not last:
        pass


def emit_xa(k, C, NT, x_src, x_dst, W, li, base):
    k.stage(base, "xa")
    g_mem = load_small(k, W["ln_mem"][li], [128, DC])
    g_pre = load_small(k, W["ln_xa_pre"][li], [128, DC])
    g_post = load_small(k, W["ln_xa_post"][li], [128, DC])
    nb = NormBufs(k)
    xs = k.sb([128, DC, TG], F32, "xs")
    hn = k.sbs(DC, [128, TG], BF16, "hn")
    y = k.sb([128, DC, TG], F32, "y")
    tmp = k.sbs(2, [128, TG], F32, "tmp")
    ws = WStream(k, 3, 16, 256)
    kT = k.sbs(4, [128, 256], BF16, "kT")
    v = k.sbs(2, [128, 512], BF16, "v")
    qT = k.sbs(4, [128, TG], BF16, "qT")
    E = k.sbs(8, [128, TG], BF16, "E")
    oT = k.sbs(4, [128, TG], BF16, "oT")
    rec = k.sbs(2, [128, TG], F32, "rec")
    Pn = k.P[7]
    wq, wkv, wo = W["xa_wq"][li], W["xa_wkv"][li], W["xa_wo"][li]
    memT = W["memT"]
    k.dma("sp", xs[:, :, 0:256], memT.ap.rearrange("(c p) t -> p c t", p=128), xs, reads=[memT.T], writes=[xs])
    pre_norm(k, C, nb, xs, g_mem, hn, Pn, w=256)
    for t4 in range(4):
        wt = ws.load(wkv, 0, 16, t4 * 256, 256)
        if t4 < 2:
            for jj in range(2):
                h = t4 * 2 + jj
                Pb = k.P[h % 4]
                k.mmg(Pb, Pb[:, 0:256], [(wt[:, c, jj * 128:(jj + 1) * 128], hn[c][:, 0:256]) for c in range(DC)], [wt] + hn)
                k.op("act", lambda e: e.activation(out=kT[h][:], in_=Pb[:, 0:256], func=AF.Copy), reads=[Pb], writes=[kT[h]])
        else:
            half = t4 - 2
            for mc in range(2):
                Pb = k.P[4 + mc]
                k.mmg(Pb, Pb[:, half * 256:(half + 1) * 256],
                      [(hn[c][:, mc * 128:(mc + 1) * 128], wt[:, c, :]) for c in range(DC)], [wt] + hn)
                k.op("act", lambda e: e.activation(out=v[mc][:, half * 256:(half + 1) * 256],
                                                   in_=Pb[:, half * 256:(half + 1) * 256], func=AF.Copy),
                     reads=[Pb], writes=[v[mc]])
    scale = 128.0 ** -0.5
    def prefetch(tg):
        k.dma("sp", xs[:], x_tile_ap(x_src, tg), xs, reads=[x_src.tt(tg)], writes=[xs])
        pre_norm(k, C, nb, xs, g_pre, hn, Pn)

    prefetch(0)
    for tg in range(NT):
        for t2 in range(2):
            wt = ws.load(wq, 0, 16, t2 * 256, 256)
            for jj in range(2):
                h = t2 * 2 + jj
                Pb = k.P[h % 4]
                k.mmg(Pb, Pb[:], [(wt[:, c, jj * 128:(jj + 1) * 128], hn[c][:]) for c in range(DC)], [wt] + hn)
                k.op("act", lambda e: e.activation(out=qT[h][:], in_=Pb[:], func=AF.Copy), reads=[Pb], writes=[qT[h]])
        if tg + 1 < NT:
            prefetch(tg + 1)
        for h in range(4):
            for mc in range(2):
                Pb = k.P[(2 * h + mc) % 4]
                k.mmg(Pb, Pb[:], [(kT[h][:, mc * 128:(mc + 1) * 128], qT[h][:])], [kT[h], qT[h]])
                Eh = E[2 * h + mc]
                k.op("act", lambda e: e.activation(out=Eh[:], in_=Pb[:], func=AF.Exp, scale=scale), reads=[Pb], writes=[Eh])
            Pd = k.P[4 + h % 2]
            k.mmg(Pd, Pd[:], [(C["ones_bf"][:], E[2 * h + mc][:]) for mc in range(2)], [E[2 * h], E[2 * h + 1]])
            rc = rec[h % 2]
            k.op("dve", lambda e: e.reciprocal(out=rc[:], in_=Pd[:]), reads=[Pd], writes=[rc])
            Po = k.P[6]
            k.mmg(Po, Po[:], [(v[mc][:, h * 128:(h + 1) * 128], E[2 * h + mc][:]) for mc in range(2)],
                  [v[0], v[1], E[2 * h], E[2 * h + 1]])
            k.op("dve", lambda e: e.tensor_tensor(out=oT[h][:], in0=Po[:], in1=rc[:], op=ALU.mult),
                 reads=[Po, rc], writes=[oT[h]])
        for t8 in range(8):
            wt = ws.load(wo, 0, 4, t8 * 256, 256)
            for jj in range(2):
                c = t8 * 2 + jj
                Pb = k.P[c % 4]
                k.mmg(Pb, Pb[:], [(wt[:, h, jj * 128:(jj + 1) * 128], oT[h][:]) for h in range(4)], [wt] + oT)
                k.op("act", lambda e: e.activation(out=y[:, c, :], in_=Pb[:], func=AF.Copy), reads=[Pb], writes=[y])
        post_norm_accum(k, C, nb, y, g_post, Pn, x_dst, tg, ws, after=2)
    ws.flush()


def bcast_mid(ap, n):
    pat = [list(p) for p in ap.ap]
    assert len(pat) == 2
    return bass.AP(ap.tensor, ap.offset, [pat[0], [0, n], pat[1]])


def bcast_last(ap, n):
    pat = [list(p) for p in ap.ap]
    assert len(pat) == 2
    return bass.AP(ap.tensor, ap.offset, [pat[0], pat[1], [0, n]])


def emit_outproj(k, C, NT, feat, KC, w_out, g_post_ap, x_src, x_dst, base):
    k.stage(base, "outproj")
    g_post = load_small(k, g_post_ap, [128, DC])
    nb = NormBufs(k)
    y = k.sb([128, DC, TG], F32, "y")
    f = k.sbs(2, [128, KC, TG], BF16, "feat")
    ws = WStream(k, 3, 16, 256)
    Pn = k.P[7]
    nkh = (KC + 15) // 16
    for tg in range(NT):
        ft = f[tg % 2]
        k.dma("sp", ft[:], feat.ap[0:KC * 128, tg * TG:(tg + 1) * TG].rearrange("(c p) t -> p c t", p=128), ft,
              reads=[feat.T], writes=[ft])
        for ct in range(8):
            for kh in range(nkh):
                kc = min(16, KC - kh * 16)
                wt = ws.load(w_out, kh * 2048, kc, ct * 256, 256)
                for jj in range(2):
                    Pb = k.P[(2 * ct + jj) % 4]
                    pairs = [(wt[:, c, jj * 128:(jj + 1) * 128], ft[:, kh * 16 + c, :]) for c in range(kc)]
                    emit_partial_group(k, Pb, pairs, [wt, ft], kh == 0, kh == nkh - 1)
            for jj in range(2):
                c = 2 * ct + jj
                Pb = k.P[c % 4]
                k.op("act", lambda e: e.activation(out=y[:, c, :], in_=Pb[:], func=AF.Copy), reads=[Pb], writes=[y])
        post_norm_accum(k, C, nb, y, g_post, Pn, x_dst, tg, ws)
    ws.flush()


def emit_sgu_front(k, C, NT, x_src, W, li, feat, base):
    k.stage(base, "sgu")
    j = li // 3
    g_pre = load_small(k, W["ln_mix_pre"][li], [128, DC])
    vg = load_small(k, W["sg_v_norm_g"][j], [128, 32])
    vb = load_small(k, W["sg_v_norm_b"][j], [128, 32])
    wsp_f = load_small(k, W["sg_w_spatial"][j], [128, 16, 128])
    bias_bc = k.sb([128, 16, 128], F32, "bias_bc")
    k.dma("sp", bias_bc[:], W["sg_b_spatial"].ap[j].rearrange("g t -> (g t)").partition_broadcast(128)
          .rearrange("p (g t) -> p g t", g=16), bias_bc, writes=[bias_bc])
    k.op("dve", lambda e: e.tensor_tensor(out=wsp_f[:], in0=wsp_f[:], in1=bcast_mid(C["le_f"][:], 16), op=ALU.mult),
         reads=[wsp_f, C["le_f"]], writes=[wsp_f])
    wsp_b = k.sb([128, 16, 128], BF16, "wsp_b")
    k.op("dve", lambda e: e.tensor_copy(out=wsp_b[:], in_=wsp_f[:]), reads=[wsp_f], writes=[wsp_b])
    rs_bc = k.sb([128, 16, 128], F32, "rs_bc")
    for q in range(4):
        Pb = k.P[q]
        k.mmg(Pb, Pb[:], [(C["ones_bf"][:], wsp_b[:, 4 * q:4 * q + 4, :])], [wsp_b])
        k.op("act", lambda e: e.activation(out=rs_bc[:, 4 * q:4 * q + 4, :], in_=Pb[:], func=AF.Copy), reads=[Pb], writes=[rs_bc])
    nb = NormBufs(k)
    xs = k.sb([128, DC, TG], F32, "xs")
    hn = k.sbs(DC, [128, TG], BF16, "hn")
    uT = k.sbs(32, [128, TG], BF16, "uT")
    vraw = k.sbs(4, [128, 4096], BF16, "vraw")
    junk = k.sbs(2, [128, 256], BF16, "junk")
    s1 = k.sbs(4, [128, 16], F32, "s1")
    s2 = k.sbs(4, [128, 16], F32, "s2")
    st = k.sbs(4, [128, 8], F32, "st")
    wsn = k.sbs(4, [128, 16, 128], BF16, "wsn")
    mrep = k.sbs(4, [128, 128], BF16, "mrep")
    stsem = T(None)
    t1 = k.sbs(2, [128, TG], F32, "t1")
    ws = WStream(k, 3, 16, 256)
    w_in = W["sg_w_in"][j]
    Pn = k.P[7]
    for tg in range(NT):
        k.dma("sp", xs[:], x_tile_ap(x_src, tg), xs, reads=[x_src.tt(tg)], writes=[xs])
        pre_norm(k, C, nb, xs, g_pre, hn, Pn)
        for ct in range(16):
            wt = ws.load(w_in, 0, 16, ct * 256, 256)
            for jj in range(2):
                dc = 2 * ct + jj
                Pb = k.P[dc % 4]
                k.mmg(Pb, Pb[:], [(wt[:, c, jj * 128:(jj + 1) * 128], hn[c][:]) for c in range(DC)], [wt] + hn)
                k.op("act", lambda e: e.activation(out=uT[dc][:], in_=Pb[:], func=AF.Gelu_apprx_tanh), reads=[Pb], writes=[uT[dc]])
        for ct in range(16):
            wt = ws.load(w_in, 0, 16, 4096 + ct * 256, 256)
            for tb in range(4):
                Pb = k.P[tb]
                k.mmg(Pb, Pb[:, 0:256], [(hn[c][:, tb * 128:(tb + 1) * 128], wt[:, c, :]) for c in range(DC)], [wt] + hn)
                k.op("act", lambda e: e.activation(out=vraw[tb][:, ct * 256:(ct + 1) * 256], in_=Pb[:, 0:256],
                                                   func=AF.Gelu_apprx_tanh, accum_out=s1[tb][:, ct:ct + 1]),
                     reads=[Pb], writes=[vraw[tb], s1[tb]])
                jk = junk[tb % 2]
                k.op("act", lambda e: e.activation(out=jk[:], in_=vraw[tb][:, ct * 256:(ct + 1) * 256],
                                                   func=AF.Square, accum_out=s2[tb][:, ct:ct + 1]),
                     reads=[vraw[tb]], writes=[jk, s2[tb]])
        for tb in range(4):
            S = st[tb]
            k.op("dve", lambda e: e.reduce_sum(out=S[:, 0:1], in_=s1[tb][:], axis=mybir.AxisListType.X), reads=[s1[tb]], writes=[S])
            k.op("dve", lambda e: e.reduce_sum(out=S[:, 1:2], in_=s2[tb][:], axis=mybir.AxisListType.X), reads=[s2[tb]], writes=[S])
            k.op("dve", lambda e: e.tensor_scalar(out=S[:, 0:2], in0=S[:, 0:2], scalar1=1.0 / 4096, scalar2=None, op0=ALU.mult),
                 reads=[S], writes=[S])
            k.op("dve", lambda e: e.tensor_tensor(out=S[:, 2:3], in0=S[:, 0:1], in1=S[:, 0:1], op=ALU.mult), reads=[S], writes=[S])
            k.op("dve", lambda e: e.tensor_tensor(out=S[:, 2:3], in0=S[:, 1:2], in1=S[:, 2:3], op=ALU.subtract), reads=[S], writes=[S])
            k.op("act", lambda e: e.activation(out=S[:, 3:4], in_=S[:, 2:3], func=AF.Ln, bias=EPS), reads=[S], writes=[S])
            k.op("act", lambda e: e.activation(out=S[:, 3:4], in_=S[:, 3:4], func=AF.Exp, scale=-0.5), reads=[S], writes=[S])
            k.op("dve", lambda e: e.tensor_scalar(out=S[:, 4:5], in0=S[:, 0:1], scalar1=-1.0, scalar2=None, op0=ALU.mult),
                 reads=[S], writes=[S])
            k.op("dve", lambda e: e.tensor_scalar(out=wsn[tb][:], in0=wsp_f[:], scalar1=S[:, 3:4], scalar2=None, op0=ALU.mult),
                 reads=[wsp_f, S], writes=[wsn[tb]])
            k.op("dve", lambda e: e.tensor_scalar(out=mrep[tb][:], in0=C["ones_f"][:], scalar1=S[:, 4:5], scalar2=None, op0=ALU.mult),
                 reads=[C["ones_f"], S], writes=[mrep[tb]])
        for dc in range(32):
            g = dc // 2
            Pb = k.P[dc % 4]
            for tb in range(4):
                k.mmg(Pb, Pb[:, tb * 128:(tb + 1) * 128],
                      [(vraw[tb][:, dc * 128:(dc + 1) * 128], wsn[tb][:, g, :]), (mrep[tb][:], wsn[tb][:, g, :])],
                      [vraw[tb], wsn[tb], mrep[tb]])
            tt = t1[dc % 2]
            ttv = tt[:].rearrange("p (a t) -> p a t", a=4)
            k.op("dve", lambda e: e.scalar_tensor_tensor(out=ttv, in0=Pb[:].rearrange("p (a t) -> p a t", a=4),
                                                         scalar=vg[:, dc:dc + 1], in1=bcast_mid(bias_bc[:, g, :], 4),
                                                         op0=ALU.mult, op1=ALU.add),
                 reads=[Pb, vg, bias_bc], writes=[tt])
            k.op("dve", lambda e: e.scalar_tensor_tensor(out=ttv, in0=bcast_mid(rs_bc[:, g, :], 4),
                                                          scalar=vb[:, dc:dc + 1], in1=ttv, op0=ALU.mult, op1=ALU.add),
                 reads=[rs_bc, vb, tt], writes=[tt])
            k.op("dve", lambda e: e.tensor_tensor(out=uT[dc][:], in0=tt[:], in1=uT[dc][:], op=ALU.mult),
                 reads=[tt, uT[dc]], writes=[uT[dc]])
            k.dma("sp", feat.ap[dc * 128:(dc + 1) * 128, tg * TG:(tg + 1) * TG], uT[dc][:], uT[dc],
                  reads=[uT[dc]], writes=[feat.T])


def emit_sb_front(k, C, NT, x_src, W, li, qk_d, v_d, o_d, base):
    L = NT * TG
    NB = L // 128
    j = li // 3
    w_qkv = W["sb_w_qkv"][j]
    k.stage(base, "sb1")
    g_pre = load_small(k, W["ln_mix_pre"][li], [128, DC])
    nb = NormBufs(k)
    xs = k.sb([128, DC, TG], F32, "xs")
    hn = k.sbs(DC, [128, TG], BF16, "hn")
    stg = k.sbs(4, [128, TG], BF16, "stg")
    vtok = k.sbs(4, [128, 2048], BF16, "vtok")
    stsem = T(None)
    ws = WStream(k, 3, 16, 256)
    Pn = k.P[7]
    scale = 128.0 ** -0.5
    for tg in range(NT):
        k.dma("sp", xs[:], x_tile_ap(x_src, tg), xs, reads=[x_src.tt(tg)], writes=[xs])
        pre_norm(k, C, nb, xs, g_pre, hn, Pn)
        for ct in range(16):
            wt = ws.load(w_qkv, 0, 16, ct * 256, 256)
            for jj in range(2):
                r = 2 * ct + jj
                Pb = k.P[r % 4]
                k.mmg(Pb, Pb[:], [(wt[:, c, jj * 128:(jj + 1) * 128], hn[c][:]) for c in range(DC)], [wt] + hn)
                sg = stg[r % 4]
                k.op("act", lambda e: e.activation(out=sg[:], in_=Pb[:], func=AF.Copy, scale=(scale if ct < 8 else 1.0)),
                     reads=[Pb], writes=[sg])
                k.dma("sp", qk_d.ap[r * 128:(r + 1) * 128, tg * TG:(tg + 1) * TG], sg[:], sg, reads=[sg], writes=[qk_d.T])
        for ct in range(8):
            wt = ws.load(w_qkv, 0, 16, 4096 + ct * 256, 256)
            for tb in range(4):
                Pb = k.P[tb]
                k.mmg(Pb, Pb[:, 0:256], [(hn[c][:, tb * 128:(tb + 1) * 128], wt[:, c, :]) for c in range(DC)], [wt] + hn)
                k.op("act", lambda e: e.activation(out=vtok[tb][:, ct * 256:(ct + 1) * 256], in_=Pb[:, 0:256], func=AF.Copy),
                     reads=[Pb], writes=[vtok[tb]])
        for tb in range(4):
            r0 = tg * TG + tb * 128
            k.dma("sp", v_d.ap[r0:r0 + 128, :], vtok[tb][:], vtok[tb], reads=[vtok[tb]], writes=[v_d.T])
    k.stage(base, "sb2")
    NEGV = -30000.0
    neg = k.sb([128, 4, TG], BF16, "neg")
    negge = k.sb([128, 128], BF16, "negge")
    k.op("dve", lambda e: e.tensor_scalar(out=negge[:], in0=C["gt_f"][:], scalar1=-1.0, scalar2=NEGV, op0=ALU.add, op1=ALU.mult),
         reads=[C["gt_f"]], writes=[negge])
    k.op("dve", lambda e: e.tensor_scalar(out=negge[:], in0=C["lt_f"][:], scalar1=-1.0, scalar2=-NEGV, op0=ALU.add, op1=ALU.mult),
         reads=[C["lt_f"]], writes=[negge])
    k.op("dve", lambda e: e.memset(neg[:], 0.0), writes=[neg])
    for a in range(4):
        k.op("dve", lambda e: e.tensor_copy(out=neg[:, a, a * 128:(a + 1) * 128], in_=negge[:]), reads=[negge], writes=[neg])
        if a > 0:
            k.op("dve", lambda e: e.memset(neg[:, a, 0:a * 128], NEGV), writes=[neg])
    negrow = k.sb([1, 128], BF16, "negrow")
    k.op("dve", lambda e: e.memset(negrow[:], -1.0), writes=[negrow])
    qT = [k.sbs(2, [128, L], BF16, "qT") for _ in range(2)]
    kT = [k.sbs(2, [128, L], BF16, "kT") for _ in range(2)]
    vh = [k.sbs(2, [128, NB, 128], BF16, "vh") for _ in range(2)]
    eb = [k.sbs(2, [128, TG], F32, "eb") for _ in range(2)]
    spb = [k.sbs(2, [128, TG], BF16, "spb") for _ in range(2)]
    Ab = [k.sbs(2, [128, TG], BF16, "Ab") for _ in range(2)]
    Rb = [k.sbs(2, [1, TG], BF16, "Rb") for _ in range(2)]
    Rf = k.sbs(2, [1, TG], F32, "Rf")
    ob = [k.sbs(2, [128, TG], BF16, "ob") for _ in range(2)]
    Pz1 = [k.P[0], k.P[1]]
    Pz2 = [k.P[2], k.P[3]]
    Po = [k.P[4], k.P[5]]
    Pt = [k.P[6], k.P[7]]
    steps = []
    for hp in range(8):
        for tg in range(NT):
            nkb = 4 * (tg + 1)
            for sb in range(nkb - 1, -1, -1):
                steps.append((hp, tg, sb, sb == nkb - 1, sb == 0))
    ns = len(steps)

    def operands(c, i):
        hp, tg, sb, first, last = steps[i]
        q_, k_, v_ = qT[c][hp % 2], kT[c][hp % 2], vh[c][hp % 2]
        a = sb - 4 * tg
        zp = [(k_[:, sb * 128:(sb + 1) * 128], q_[:, tg * TG:(tg + 1) * TG])]
        zr = [k_, q_]
        if a >= 0:
            zp.append((C["ident_bf"][:], neg[:, a, :]))
            zr.append(neg)
        return zp, zr, v_

    for i in range(-1, ns):
        f = i + 1
        for c in range(2):
            if i >= 0:
                zp, zr, v_ = operands(c, i)
                sp_, R_ = spb[c][i % 2], Rb[c][i % 2]
                k.mmg(Pz2[c], Pz2[c][:], zp + [(C["neg_ge_bf"][:], sp_[:]), (negrow[:], R_[:])],
                      zr + [sp_, R_, negrow, C["neg_ge_bf"]])
            if f < ns:
                hp, tg, sb, first, last = steps[f]
                if first and tg == 0:
                    h = 2 * hp + c
                    q_, k_, v_ = qT[c][hp % 2], kT[c][hp % 2], vh[c][hp % 2]
                    k.dma("sp", q_[:], qk_d.ap[h * 128:(h + 1) * 128, :], q_, reads=[qk_d.T], writes=[q_])
                    k.dma("sp", k_[:], qk_d.ap[2048 + h * 128:2048 + (h + 1) * 128, :], k_, reads=[qk_d.T], writes=[k_])
                    for q4 in range(0, NB, 8):
                        k.dma("sp", v_[:, q4:q4 + 8, :],
                              v_d.ap[q4 * 128:(q4 + 8) * 128, h * 128:(h + 1) * 128].rearrange("(b p) d -> p b d", p=128), v_,
                              reads=[v_d.T], writes=[v_])
                zp, zr, v_ = operands(c, f)
                k.mmg(Pz1[c], Pz1[c][:], zp, zr)
        for c in range(2):
            if i >= 0:
                A_ = Ab[c][i % 2]
                k.op("act", lambda e: e.activation(out=A_[:], in_=Pz2[c][:], func=AF.Exp), reads=[Pz2[c]], writes=[A_])
            if f < ns:
                e_ = eb[c][f % 2]
                k.op("act", lambda e: e.activation(out=e_[:], in_=Pz1[c][:], func=AF.Exp), reads=[Pz1[c]], writes=[e_])
        for c in range(2):
            if f < ns:
                e_, sp_ = eb[c][f % 2], spb[c][f % 2]
                k.op("act", lambda e: e.activation(out=sp_[:], in_=e_[:], func=AF.Ln, bias=1.0), reads=[e_], writes=[sp_])
        for c in range(2):
            if i >= 0:
                hp, tg, sb, first, last = steps[i]
                zp, zr, v_ = operands(c, i)
                A_ = Ab[c][i % 2]
                emit_partial_group(k, Po[c], [(v_[:, sb, :], A_[:])], [v_, A_], first, last)
                if last:
                    h = 2 * hp + c
                    o_ = ob[c][tg % 2]
                    k.op("dve", lambda e: e.tensor_copy(out=o_[:], in_=Po[c][:]), reads=[Po[c]], writes=[o_])
                    k.dma("sp", o_d.ap[h * 128:(h + 1) * 128, tg * TG:(tg + 1) * TG], o_[:], o_, reads=[o_], writes=[o_d.T])
            if f < ns:
                hp, tg, sb, first, last = steps[f]
                sp_, R_ = spb[c][f % 2], Rb[c][f % 2]
                if first:
                    k.op("pool", lambda e: e.memset(Rf[c][:], 0.0), writes=[Rf[c]])
                k.op("pool", lambda e: e.tensor_copy(out=R_[:], in_=Rf[c][:]), reads=[Rf[c]], writes=[R_])
                if not last:
                    k.mmg(Pt[c], Pt[c][0:1, :], [(C["ones_bf"][:, 0:1], sp_[:])], [sp_])
                    k.op("dve", lambda e: e.tensor_tensor(out=Rf[c][:], in0=Rf[c][:], in1=Pt[c][0:1, :], op=ALU.add),
                         reads=[Rf[c], Pt[c]], writes=[Rf[c]])


def emit_ssd_front(k, C, NT, x_src, W, li, S, base):
    L = NT * TG
    NCH = L // 128
    j = li // 3
    w_in = W["ssd_w_in"][j]
    zs_d, bc_d, btok_d, xs_d, y_d, dtda_d, feat = S["featA"], S["featC"], S["featV"], S["ssdX"], S["ssdY"], S["dtda"], S["featB"]
    AX = mybir.AxisListType.X
    k.stage(base, "ssd1")
    g_pre = load_small(k, W["ln_mix_pre"][li], [128, DC])
    cw = load_small(k, W["ssd_conv_w"][j], [128, 48, 4])
    cb = load_small(k, W["ssd_conv_b"][j], [128, 48])
    dtb = load_small(k, W["ssd_dt_bias"][j], [64, 1])
    alog = load_small(k, W["ssd_a_log"][j], [64, 1])
    aneg = k.sb([64, 1], F32, "aneg")
    k.op("act", lambda e: e.activation(out=aneg[:], in_=alog[:], func=AF.Exp), reads=[alog], writes=[aneg])
    k.op("dve", lambda e: e.tensor_scalar(out=aneg[:], in0=aneg[:], scalar1=-1.0, scalar2=None, op0=ALU.mult), reads=[aneg], writes=[aneg])
    halo = k.sb([128, 48, 4], F32, "halo")
    k.op("pool", lambda e: e.memset(halo[:], 0.0), writes=[halo])
    nb = NormBufs(k)
    xs = k.sb([128, DC, TG], F32, "xs")
    hn = k.sbs(DC, [128, TG], BF16, "hn")
    xs_tok = k.sb([128, 4, 4096], F32, "xs_tok")
    b_tok = k.sb([128, 4, 1024], BF16, "b_tok")
    xpad = k.sbs(3, [128, TG + 4], F32, "xpad")
    cv = k.sbs(3, [128, TG], F32, "cv")
    stg = k.sbs(4, [128, TG], BF16, "stg")
    e1 = k.sb([64, TG], F32, "e1")
    dtT = k.sb([64, TG], F32, "dtT")
    daT = k.sb([64, TG], F32, "daT")
    dtda_tok = k.sb([128, 4, 128], F32, "dtda_tok")
    ws = WStream(k, 3, 16, 256)
    Pn = k.P[7]
    nstg = 0
    for tg in range(NT):
        tcols = slice(tg * TG, (tg + 1) * TG)
        k.dma("sp", xs[:], x_tile_ap(x_src, tg), xs, reads=[x_src.tt(tg)], writes=[xs])
        pre_norm(k, C, nb, xs, g_pre, hn, Pn)
        for ct in range(16):
            wt = ws.load(w_in, 0, 16, ct * 256, 256)
            for jj in range(2):
                r = 2 * ct + jj
                Pb = k.P[r % 4]
                k.mmg(Pb, Pb[:], [(wt[:, c, jj * 128:(jj + 1) * 128], hn[c][:]) for c in range(DC)], [wt] + hn)
                sg = stg[nstg % 4]
                nstg += 1
                k.op("act", lambda e: e.activation(out=sg[:], in_=Pb[:], func=AF.Silu), reads=[Pb], writes=[sg])
                k.dma("sp", zs_d.ap[r * 128:(r + 1) * 128, tcols], sg[:], sg, reads=[sg], writes=[zs_d.T])
        wcache = {}

        def xchunk(ch):
            nonlocal nstg
            ct, jj = ch // 2, ch % 2
            if jj == 0:
                wcache[ct] = ws.load(w_in, 0, 16, 4096 + ct * 256, 256)
            wt = wcache[ct]
            Pb = k.P[ch % 4]
            xp = xpad[ch % 3]
            o = cv[ch % 3]
            k.mmg(Pb, Pb[:], [(wt[:, c, jj * 128:(jj + 1) * 128], hn[c][:]) for c in range(DC)], [wt] + hn)
            yield
            k.op("dve", lambda e: e.tensor_copy(out=xp[:, 0:4], in_=halo[:, ch, :]), reads=[halo], writes=[xp])
            k.op("act", lambda e: e.activation(out=xp[:, 4:TG + 4], in_=Pb[:], func=AF.Copy), reads=[Pb], writes=[xp])
            k.op("dve", lambda e: e.tensor_copy(out=halo[:, ch, :], in_=xp[:, TG:TG + 4]), reads=[xp], writes=[halo])
            yield
            k.op("dve", lambda e: e.tensor_scalar(out=o[:], in0=xp[:, 4:TG + 4], scalar1=cw[:, ch, 3:4],
                                                  scalar2=cb[:, ch:ch + 1], op0=ALU.mult, op1=ALU.add),
                 reads=[xp, cw, cb], writes=[o])
            yield
            for tap in (2, 1, 0):
                k.op("dve", lambda e: e.scalar_tensor_tensor(out=o[:], in0=xp[:, tap + 1:tap + 1 + TG], scalar=cw[:, ch, tap:tap + 1],
                                                             in1=o[:], op0=ALU.mult, op1=ALU.add),
                     reads=[xp, cw, o], writes=[o])
                yield
            k.op("act", lambda e: e.activation(out=o[:], in_=o[:], func=AF.Silu), reads=[o], writes=[o])
            yield
            if ch >= 32:
                sg = stg[nstg % 4]
                nstg += 1
                k.op("dve", lambda e: e.tensor_copy(out=sg[:], in_=o[:]), reads=[o], writes=[sg])
                r = ch - 32
                k.dma("sp", bc_d.ap[r * 128:(r + 1) * 128, tcols], sg[:], sg, reads=[sg], writes=[bc_d.T])
            if ch < 40:
                Pt = k.P[4 + ch % 3]
                for tb in range(4):
                    k.transpose(Pt, Pt[:, tb * 128:(tb + 1) * 128], o[:, tb * 128:(tb + 1) * 128], C["ident_f"][:], [o, C["ident_f"]])
                yield
                pv = Pt[:].rearrange("p (a t) -> p a t", a=4)
                if ch < 32:
                    k.op("act", lambda e: e.activation(out=xs_tok[:, :, ch * 128:(ch + 1) * 128], in_=pv, func=AF.Copy),
                         reads=[Pt], writes=[xs_tok])
                else:
                    g = ch - 32
                    k.op("act", lambda e: e.activation(out=b_tok[:, :, g * 128:(g + 1) * 128], in_=pv, func=AF.Copy),
                         reads=[Pt], writes=[b_tok])

        run_lockstep((xchunk(ch) for ch in range(48)), 3)
        wt = ws.load(w_in, 0, 16, 10240, 64)
        Pd = k.P[6]
        k.mmg(Pd, Pd[0:64, :], [(wt[:, c, 0:64], hn[c][:]) for c in range(DC)], [wt] + hn)
        k.op("act", lambda e: e.activation(out=e1[:], in_=Pd[0:64, :], func=AF.Exp, bias=dtb[:, 0:1]), reads=[Pd, dtb], writes=[e1])
        k.op("act", lambda e: e.activation(out=dtT[:], in_=e1[:], func=AF.Ln, bias=1.0), reads=[e1], writes=[dtT])
        k.op("dve", lambda e: e.tensor_scalar(out=daT[:], in0=dtT[:], scalar1=aneg[:, 0:1], scalar2=None, op0=ALU.mult),
             reads=[dtT, aneg], writes=[daT])
        Pt = k.P[4]
        for tb in range(4):
            k.transpose(Pt, Pt[:, tb * 128:tb * 128 + 64], dtT[:, tb * 128:(tb + 1) * 128], C["ident_f"][0:64, 0:64], [dtT, C["ident_f"]])
            k.transpose(Pt, Pt[:, tb * 128 + 64:tb * 128 + 128], daT[:, tb * 128:(tb + 1) * 128], C["ident_f"][0:64, 0:64], [daT, C["ident_f"]])
        k.op("act", lambda e: e.activation(out=dtda_tok[:], in_=Pt[:].rearrange("p (a t) -> p a t", a=4), func=AF.Copy),
             reads=[Pt], writes=[dtda_tok])
        rows = slice(tg * TG, (tg + 1) * TG)
        k.dma("sp", dtda_d.ap[rows, :].rearrange("(a p) f -> p a f", p=128), dtda_tok[:], dtda_tok, reads=[dtda_tok], writes=[dtda_d.T])
        k.dma("sp", xs_d.ap[rows, :].rearrange("(a p) f -> p a f", p=128), xs_tok[:], xs_tok, reads=[xs_tok], writes=[xs_d.T])
        k.dma("sp", btok_d.ap[rows, 0:1024].rearrange("(a p) f -> p a f", p=128), b_tok[:], b_tok, reads=[b_tok], writes=[btok_d.T])
    import os
    if os.environ.get("SSD_STOP") == "1":
        return
    k.stage(base, "ssd2")
    dbc = k.sb([128, 64], F32, "dbc")
    k.dma("sp", dbc[:], W["ssd_d"].ap[j].partition_broadcast(128), dbc, writes=[dbc])
    xs_c = k.sbs(2, [128, 4096], F32, "xs_c")
    bt_c = k.sbs(2, [128, 8, 128], BF16, "bt_c")
    ct_c = k.sbs(2, [128, 8, 128], BF16, "ct_c")
    bk_c = k.sbs(2, [128, 1024], BF16, "bk_c")
    dd_c = k.sbs(2, [128, 128], F32, "dd_c")
    xdt_ = k.sbs(2, [128, 4096], BF16, "xdt")
    xdtd_ = k.sbs(2, [128, 4096], BF16, "xdtd")
    prev_f = k.sb([128, 4096], F32, "prev_f")
    prev_b = k.sb([128, 4096], BF16, "prev_b")
    k.op("dve", lambda e: e.memset(prev_f[:], 0.0), writes=[prev_f])
    k.op("dve", lambda e: e.memset(prev_b[:], 0.0), writes=[prev_b])
    y_tok = k.sbs(2, [128, 4096], F32, "y_tok")
    acum_ = k.sbs(2, [128, 64], F32, "acum")
    dah_ = k.sbs(2, [128, 64], BF16, "dah")
    dal_ = k.sbs(2, [128, 64], BF16, "dal")
    ea_ = k.sbs(2, [128, 64], F32, "ea")
    cdb_ = k.sbs(2, [128, 64], F32, "cdb")
    decs_ = k.sbs(2, [128, 64], F32, "decs")
    w2_ = k.sbs(2, [128, 64], F32, "w2")
    rhs4 = [k.sbs(2, [128, 4, 128], F32, "rhs4") for _ in range(2)]
    Eb = [k.sbs(2, [128, TG], F32, "Eb") for _ in range(2)]
    MT = [k.sbs(2, [128, 4, 128], BF16, "MT") for _ in range(2)]
    cbm = k.sbs(2, [128, 128], F32, "cbm")
    tA = k.sbs(2, [128, TG], F32, "tA")
    tB = k.sbs(2, [128, TG], F32, "tB")
    v3 = lambda ap: ap.rearrange("p (h d) -> p h d", d=64)

    def prep(c):
        i = c % 2
        xc, bt, ct, bk, dd = xs_c[i], bt_c[i], ct_c[i], bk_c[i], dd_c[i]
        acum, dah, dal, ea, cdb, decs, w2, xdt, xdtd = acum_[i], dah_[i], dal_[i], ea_[i], cdb_[i], decs_[i], w2_[i], xdt_[i], xdtd_[i]
        rows = slice(c * 128, (c + 1) * 128)
        k.dma("sp", xc[:], xs_d.ap[rows, :], xc, reads=[xs_d.T], writes=[xc])
        k.dma("sp", bt[:], bc_d.ap[0:1024, rows].rearrange("(g n) s -> n g s", n=128), bt, reads=[bc_d.T], writes=[bt])
        k.dma("sp", ct[:], bc_d.ap[1024:2048, rows].rearrange("(g n) s -> n g s", n=128), ct, reads=[bc_d.T], writes=[ct])
        k.dma("sp", bk[:], btok_d.ap[rows, 0:1024], bk, reads=[btok_d.T], writes=[bk])
        k.dma("sp", dd[:], dtda_d.ap[rows, :], dd, reads=[dtda_d.T], writes=[dd])
        dt_ap, da_ap = dd[:, 0:64], dd[:, 64:128]
        Pm = k.P[0]
        k.op("dve", lambda e: e.tensor_copy(out=dah[:], in_=da_ap), reads=[dd], writes=[dah])
        k.op("dve", lambda e: e.tensor_tensor(out=dal[:], in0=da_ap, in1=dah[:], op=ALU.subtract), reads=[dd, dah], writes=[dal])
        k.mmg(Pm, Pm[:, 0:64], [(C["le_bf"][:], dah[:]), (C["le_bf"][:], dal[:])], [C["le_bf"], dah, dal])
        k.mmg(Pm, Pm[:, 64:128], [(C["ones_bf"][:], dah[:]), (C["ones_bf"][:], dal[:])], [C["ones_bf"], dah, dal])
        k.op("act", lambda e: e.activation(out=acum[:], in_=Pm[:, 0:64], func=AF.Copy), reads=[Pm], writes=[acum])
        k.op("act", lambda e: e.activation(out=ea[:], in_=Pm[:, 0:64], func=AF.Exp), reads=[Pm], writes=[ea])
        k.op("act", lambda e: e.activation(out=cdb[:], in_=Pm[:, 64:128], func=AF.Exp), reads=[Pm], writes=[cdb])
        k.op("dve", lambda e: e.tensor_tensor(out=decs[:], in0=Pm[:, 64:128], in1=acum[:], op=ALU.subtract), reads=[Pm, acum], writes=[decs])
        k.op("act", lambda e: e.activation(out=decs[:], in_=decs[:], func=AF.Exp), reads=[decs], writes=[decs])
        k.op("dve", lambda e: e.tensor_tensor(out=w2[:], in0=decs[:], in1=dt_ap, op=ALU.mult), reads=[decs, dd], writes=[w2])
        xv = v3(xc[:])
        k.op("dve", lambda e: e.tensor_tensor(out=v3(xdt[:]), in0=xv, in1=bcast_last(dt_ap, 64), op=ALU.mult), reads=[xc, dd], writes=[xdt])
        k.op("pool", lambda e: e.tensor_tensor(out=v3(xdtd[:]), in0=xv, in1=bcast_last(w2[:], 64), op=ALU.mult), reads=[xc, w2], writes=[xdtd])

    def group(c, g):
        i = c % 2
        sl = g % 2
        xc, bt, ct, bk, dd = xs_c[i], bt_c[i], ct_c[i], bk_c[i], dd_c[i]
        ea, cdb, xdt, xdtd = ea_[i], cdb_[i], xdt_[i], xdtd_[i]
        yt = y_tok[i]
        gc = slice(g * 512, (g + 1) * 512)
        Pcb, Ps, Py, Pq = k.P[1], k.P[2 + sl], k.P[4 + sl], k.P[6 + sl]
        cm, ta, tb_ = cbm[sl], tA[sl], tB[sl]
        k.mmg(Pcb, Pcb[:, sl * 128:(sl + 1) * 128], [(bt[:, g, :], ct[:, g, :])], [bt, ct])
        yield
        k.op("dve", lambda e: e.tensor_tensor(out=cm[:], in0=Pcb[:, sl * 128:(sl + 1) * 128], in1=C["le_f"][:], op=ALU.mult),
             reads=[Pcb, C["le_f"]], writes=[cm])
        for hb in range(2):
            h0 = g * 8 + hb * 4
            r4, E_, M_ = rhs4[sl][hb], Eb[sl][hb], MT[sl][hb]
            k.op("dve", lambda e: e.tensor_tensor(out=r4[:], in0=bcast_mid(C["le_f"][:], 4), in1=bcast_last(dd[:, 64 + h0:64 + h0 + 4], 128),
                                                  op=ALU.mult), reads=[C["le_f"], dd], writes=[r4])
            yield
            k.mmg(Ps, Ps[:], [(C["gt_f"][:], r4[:].rearrange("p a t -> p (a t)"))], [C["gt_f"], r4])
            yield
            k.op("act", lambda e: e.activation(out=E_[:], in_=Ps[:], func=AF.Exp), reads=[Ps], writes=[E_])
            yield
            k.op("dve", lambda e: e.tensor_tensor(out=M_[:], in0=E_[:].rearrange("p (a t) -> p a t", a=4), in1=bcast_mid(cm[:], 4), op=ALU.mult),
                 reads=[E_, cm], writes=[M_])
            yield
            for hh in range(4):
                h = h0 + hh
                col = (hb * 4 + hh) * 64
                k.mmg(Py, Py[:, col:col + 64], [(M_[:, hh, :], xdt[:, h * 64:(h + 1) * 64])], [M_, xdt])
            if hb == 0:
                k.mmg(Pq, Pq[:], [(ct[:, g, :], prev_b[:, gc])], [ct, prev_b])
                k.op("pool", lambda e: e.tensor_tensor(out=v3(tb_[:]), in0=v3(xc[:, gc]), in1=bcast_last(dbc[:, g * 8:(g + 1) * 8], 64), op=ALU.mult),
                     reads=[xc, dbc], writes=[tb_])
                yield
                k.op("dve", lambda e: e.tensor_tensor(out=v3(ta[:]), in0=v3(Pq[:]), in1=bcast_last(ea[:, g * 8:(g + 1) * 8], 64), op=ALU.mult),
                     reads=[Pq, ea], writes=[ta])
                k.op("pool", lambda e: e.tensor_tensor(out=v3(prev_f[:, gc]), in0=v3(prev_f[:, gc]), in1=bcast_last(cdb[:, g * 8:(g + 1) * 8], 64), op=ALU.mult),
                     reads=[prev_f, cdb], writes=[prev_f])
                yield
                k.mmg(Pq, Pq[:], [(bk[:, g * 128:(g + 1) * 128], xdtd[:, gc])], [bk, xdtd])
            yield
        k.op("dve", lambda e: e.tensor_tensor(out=ta[:], in0=ta[:], in1=Py[:], op=ALU.add), reads=[ta, Py], writes=[ta])
        k.op("dve", lambda e: e.tensor_tensor(out=prev_f[:, gc], in0=prev_f[:, gc], in1=Pq[:], op=ALU.add), reads=[prev_f, Pq], writes=[prev_f])
        yield
        k.op("pool", lambda e: e.tensor_tensor(out=yt[:, gc], in0=ta[:], in1=tb_[:], op=ALU.add), reads=[ta, tb_], writes=[yt])
        k.op("act", lambda e: e.activation(out=prev_b[:, gc], in_=prev_f[:, gc], func=AF.Copy), reads=[prev_f], writes=[prev_b])

    prep(0)
    for c in range(NCH):
        if c + 1 < NCH:
            prep(c + 1)
        run_lockstep((group(c, g) for g in range(8)), 2)
        rows = slice(c * 128, (c + 1) * 128)
        k.dma("sp", y_d.ap[rows, :], y_tok[c % 2][:], y_tok[c % 2], reads=[y_tok[c % 2]], writes=[y_d.T])
    if os.environ.get("SSD_STOP") == "2":
        return
    k.stage(base, "ssd3")
    ng = load_small(k, W["ssd_norm"][j], [128, 32])
    nb = NormBufs(k)
    ytk = k.sb([128, 4, 4096], F32, "ytk")
    yg = k.sb([128, 32, TG], F32, "yg")
    zsb = k.sbs(4, [128, TG], BF16, "zsb")
    fo = k.sb([128, 32, TG], BF16, "fo")
    for tg in range(NT):
        rows = slice(tg * TG, (tg + 1) * TG)
        tcols = slice(tg * TG, (tg + 1) * TG)
        k.dma("sp", ytk[:], y_d.ap[rows, :].rearrange("(a p) f -> p a f", p=128), ytk, reads=[y_d.T], writes=[ytk])
        for fc in range(32):
            Pt = k.P[fc % 4]
            for tb in range(4):
                k.transpose(Pt, Pt[:, tb * 128:(tb + 1) * 128], ytk[:, tb, fc * 128:(fc + 1) * 128], C["ident_f"][:], [ytk, C["ident_f"]])
            zt = zsb[fc % 4]
            k.dma("sp", zt[:], zs_d.ap[fc * 128:(fc + 1) * 128, tcols], zt, reads=[zs_d.T], writes=[zt])
            k.op("dve", lambda e: e.tensor_tensor(out=yg[:, fc, :], in0=Pt[:], in1=zt[:], op=ALU.mult),
                 reads=[Pt, zt], writes=[yg])
        rstd = rms_stats(k, C, nb, yg, 32, k.P[7], 4096)
        for fc in range(32):
            k.op("dve", lambda e: e.scalar_tensor_tensor(out=fo[:, fc, :], in0=yg[:, fc, :], scalar=ng[:, fc:fc + 1], in1=rstd[:],
                                                         op0=ALU.mult, op1=ALU.mult), reads=[yg, ng, rstd], writes=[fo])
        k.dma("sp", feat.ap[:, tcols].rearrange("(c p) t -> p c t", p=128), fo[:], fo, reads=[fo], writes=[feat.T])


WEIGHT_SPECS = {
    "ln_mix_pre": [4, 128, DC], "ln_mix_post": [4, 128, DC], "ln_mem": [4, 128, DC],
    "ln_xa_pre": [4, 128, DC], "ln_xa_post": [4, 128, DC], "ln_ffn_pre": [4, 128, DC], "ln_ffn_post": [4, 128, DC],
    "xa_wq": [4, D, 512], "xa_wkv": [4, D, 1024], "xa_wo": [4, 512, D],
    "ffn_w_in": [4, D, 2 * FFN], "ffn_conv_w": [4, 128, 88, 3], "ffn_conv_b": [4, 128, 88], "ffn_w_out": [4, FFN, D],
    "sg_w_in": [1, D, 8192], "sg_v_norm_g": [1, 128, 32], "sg_v_norm_b": [1, 128, 32],
    "sg_w_spatial": [1, 128, 16, 128], "sg_b_spatial": [1, 16, 128], "sg_w_out": [1, 4096, D],
    "sb_w_qkv": [1, D, 3 * D], "sb_w_out": [1, D, D],
    "ssd_w_in": [2, D, 10304], "ssd_conv_w": [2, 128, 48, 4], "ssd_conv_b": [2, 128, 48], "ssd_dt_bias": [2, 64, 1],
    "ssd_a_log": [2, 64, 1], "ssd_d": [2, 64], "ssd_norm": [2, 128, 32], "ssd_w_out": [2, 4096, D],
}


class DT:
    def __init__(self, nc, name, shape, dtype, kind):
        self.h = nc.dram_tensor(name, list(shape), dtype, kind=kind)
        self.ap = self.h.ap()
        self.T = T(self.h)
        self.shape = shape
        self._tt = {}

    def tt(self, tg):
        if tg not in self._tt:
            self._tt[tg] = T(self.h)
        return self._tt[tg]

    def __getitem__(self, key):
        return self.ap[key]


LAST_KB = None

NEED = {
    "ffn": ["ln_ffn_pre", "ln_ffn_post", "ffn_w_in", "ffn_conv_w", "ffn_conv_b", "ffn_w_out"],
    "xa": ["ln_mem", "ln_xa_pre", "ln_xa_post", "xa_wq", "xa_wkv", "xa_wo"],
    "mix0": ["ln_mix_pre", "ln_mix_post", "ssd_w_in", "ssd_conv_w", "ssd_conv_b", "ssd_dt_bias", "ssd_a_log", "ssd_d",
             "ssd_norm", "ssd_w_out"],
    "mix1": ["ln_mix_pre", "ln_mix_post", "sg_w_in", "sg_v_norm_g", "sg_v_norm_b", "sg_w_spatial", "sg_b_spatial", "sg_w_out"],
    "mix2": ["ln_mix_pre", "ln_mix_post", "sb_w_qkv", "sb_w_out"],
}


def needed_weights(plan):
    out = []
    for kind, li in plan:
        key = kind if kind != "mix" else f"mix{li % 3}"
        for n in NEED[key]:
            if n not in out:
                out.append(n)
    return out


def build_program(L, plan, wnames):
    global LAST_KB
    NT = L // TG
    nc = bass.Bass("TRN2", target_bir_lowering=False)
    k = KB(nc)
    LAST_KB = k
    xin = DT(nc, "xT", [D, L], F32, "ExternalInput")
    memT = DT(nc, "memT", [D, 256], F32, "ExternalInput")
    xout = DT(nc, "outT", [D, L], F32, "ExternalOutput")
    xr = DT(nc, "xr", [D, L], F32, "Internal")
    featA = DT(nc, "featA", [4096, L], BF16, "Internal")
    featB = DT(nc, "featB", [4096, L], BF16, "Internal")
    featV = DT(nc, "featV", [L, 2048], BF16, "Internal")
    S = {"featA": featA, "featB": featB, "featV": featV}
    if any(kind == "mix" and li % 3 == 0 for kind, li in plan):
        S["featC"] = DT(nc, "featC", [2048, L], BF16, "Internal")
        S["ssdX"] = DT(nc, "ssdX", [L, 4096], F32, "Internal")
        S["ssdY"] = DT(nc, "ssdY", [L, 4096], F32, "Internal")
        S["dtda"] = DT(nc, "dtda", [L, 128], F32, "Internal")
    W = {"memT": memT}
    for n in wnames:
        W[n] = DT(nc, n, WEIGHT_SPECS[n], F32, "ExternalInput")
    C = make_consts(k)
    base = k.sb_off
    for tg in range(NT):
        k.dma("sp", xout.ap[:, tg * TG:(tg + 1) * TG], xin.ap[:, tg * TG:(tg + 1) * TG], xout.tt(tg),
              reads=[xin.T], writes=[xout.tt(tg)])
    cur = xout
    for i, (kind, li) in enumerate(plan):
        dst = xout
        if kind == "ffn":
            emit_ffn(k, C, NT, cur, dst, W, li, base)
        elif kind == "xa":
            emit_xa(k, C, NT, cur, dst, W, li, base)
        elif kind == "mix" and li % 3 == 0:
            emit_ssd_front(k, C, NT, cur, W, li, S, base)
            emit_outproj(k, C, NT, featB, 32, W["ssd_w_out"][li // 3], W["ln_mix_post"][li], cur, dst, base)
        elif kind == "mix" and li % 3 == 2:
            emit_sb_front(k, C, NT, cur, W, li, featA, featV, featB, base)
            emit_outproj(k, C, NT, featB, 16, W["sb_w_out"][li // 3], W["ln_mix_post"][li], cur, dst, base)
        elif kind == "mix" and li % 3 == 1:
            emit_sgu_front(k, C, NT, cur, W, li, featA, base)
            emit_outproj(k, C, NT, featA, 32, W["sg_w_out"][li // 3], W["ln_mix_post"][li], cur, dst, base)
        else:
            raise ValueError(kind)
        cur = dst
    k.stage(None, None)
    return nc


def host_layout(inputs, wnames):
    out = {}
    for n in wnames:
        a = np.asarray(inputs[n])
        if n.startswith("ln_"):
            a = a.reshape(4, DC, 128).transpose(0, 2, 1)
        elif n == "ffn_conv_w":
            a = a.reshape(4, 3, 88, 128).transpose(0, 3, 2, 1)
        elif n == "ffn_conv_b":
            a = a.reshape(4, 88, 128).transpose(0, 2, 1)
        elif n in ("sg_v_norm_g", "sg_v_norm_b"):
            a = a.reshape(1, 32, 128).transpose(0, 2, 1)
        elif n == "sg_w_spatial":
            a = a.transpose(0, 3, 1, 2)
        elif n == "ssd_conv_w":
            a = a.reshape(2, 4, 48, 128).transpose(0, 3, 2, 1)
        elif n == "ssd_conv_b":
            a = a.reshape(2, 48, 128).transpose(0, 2, 1)
        elif n in ("ssd_dt_bias", "ssd_a_log"):
            a = a.reshape(2, 64, 1)
        elif n == "ssd_norm":
            a = a.reshape(2, 32, 128).transpose(0, 2, 1)
        out[n] = np.ascontiguousarray(a)
    return out


FULL_PLAN = []
for _i in range(4):
    FULL_PLAN += [("mix", _i), ("xa", _i), ("ffn", _i)]


def kernel(**inputs):
    L = 4096
    plan = FULL_PLAN
    wn = needed_weights(plan)
    hl = host_layout(inputs, wn)
    nc = build_program(L, plan, wn)
    x = np.asarray(inputs["x"])
    mem = np.asarray(inputs["mem"])
    in_maps = []
    for b in range(8):
        m = {"xT": np.ascontiguousarray(x[b].T), "memT": np.ascontiguousarray(mem[b].T)}
        m.update(hl)
        in_maps.append(m)
    res = run_bass_kernel_spmd(nc, in_maps, core_ids=list(range(8)))
    out = np.stack([np.ascontiguousarray(res.results[b]["outT"].T) for b in range(8)], axis=0)
    return out.astype(np.float32)
```

```python
import numpy as np
import concourse.bass as bass
import concourse.mybir as mybir
from concourse.bass_utils import run_bass_kernel_spmd

F32 = mybir.dt.float32
BF16 = mybir.dt.bfloat16
AF = mybir.ActivationFunctionType
ALU = mybir.AluOpType

D = 2048
DC = 16
TG = 512
EPS = 1e-6
EPOCH = 30000
SBUF_BASE = 16384
SBUF_CAP = 229000
FFN = 5632
SAME_ENG_SYNC = True


class Dep:
    __slots__ = ("lw", "rd")

    def __init__(self):
        self.lw = None
        self.rd = {}


class T:
    def __init__(self, h, dep=None):
        self.h = h
        self.dep = dep or Dep()
        self.sem = None
        self.semcnt = 0

    def __getitem__(self, k):
        return self.h[k]


class KB:
    def __init__(self, nc):
        self.nc = nc
        self.E = {"pe": nc.tensor, "act": nc.scalar, "dve": nc.vector, "pool": nc.gpsimd, "sp": nc.sync}
        self.cur = {}
        self.seen = {e: {} for e in self.E}
        self.nsem = 0
        self.dmatiles = []
        self.free_sems = {}
        self.sb_off = SBUF_BASE
        self.nalloc = 0
        self.P = [T(nc.alloc_psum_tensor(f"bank{i}", [128, 512], F32)) for i in range(8)]

    def _newsem(self, name):
        self.nsem += 1
        return self.nc.alloc_semaphore(f"{name}_{self.nsem}")

    def _tick(self, eng):
        c = self.cur.get(eng)
        if c is None or c[1] >= EPOCH:
            c = [self._newsem(eng), 0]
            self.cur[eng] = c
        c[1] += 1
        return (c[0], c[1], eng)

    def _wait(self, eng, deps, raw_ts=()):
        for d in deps:
            if d is None:
                continue
            sem, val, src = d
            if src == eng and (eng == "pe" or not SAME_ENG_SYNC):
                continue
            if self.seen[eng].get(sem, 0) >= val:
                continue
            self.E[eng].wait_ge(sem, val)
            self.seen[eng][sem] = val

    def _deps(self, reads, writes):
        deps, raw = [], []
        for t in reads:
            deps.append(t.dep.lw)
            raw.append(t.dep.lw)
        for t in writes:
            deps.append(t.dep.lw)
            deps.extend(t.dep.rd.values())
        return deps, raw

    def _commit(self, d, reads, writes):
        for t in writes:
            t.dep.lw = d
            t.dep.rd = {}
        for t in reads:
            t.dep.rd[d[0]] = d

    def op(self, eng, fn, reads=(), writes=()):
        deps, raw = self._deps(reads, writes)
        self._wait(eng, deps, raw)
        ins = fn(self.E[eng])
        d = self._tick(eng)
        ins.then_inc(d[0], 1)
        self._commit(d, reads, writes)
        return ins

    def mmg(self, P, out_ap, pairs, reads):
        deps, raw = self._deps(reads, [P])
        self._wait("pe", deps, raw)
        n = len(pairs)
        ins = None
        for i, (l, r) in enumerate(pairs):
            ins = self.nc.tensor.matmul(out_ap, l, r, start=(i == 0), stop=(i == n - 1))
        d = self._tick("pe")
        ins.then_inc(d[0], 1)
        self._commit(d, reads, [P])

    def transpose(self, P, out_ap, in_ap, ident_ap, reads):
        deps, raw = self._deps(reads, [P])
        self._wait("pe", deps, raw)
        ins = self.nc.tensor.transpose(out_ap, in_ap, ident_ap)
        d = self._tick("pe")
        ins.then_inc(d[0], 1)
        self._commit(d, reads, [P])

    def dma(self, q, out_ap, in_ap, semT, reads=(), writes=(), accum=False):
        deps, raw = self._deps(reads, writes)
        self._wait(q, deps, raw)
        if semT.sem is None or semT.semcnt >= EPOCH:
            fl = self.free_sems.setdefault(q, [])
            if fl and fl[0][1] < EPOCH - 2000:
                semT.sem, semT.semcnt = fl.pop(0)
            else:
                semT.sem = self._newsem("dma" + q)
                semT.semcnt = 0
            semT.semq = q
            self.dmatiles.append((semT, semT.sem))
        assert semT.semq == q
        semT.semcnt += 16
        if accum:
            self.E[q].dma_start(out=out_ap, in_=in_ap, accum_op=ALU.add).then_inc(semT.sem, 16)
        else:
            self.E[q].dma_start(out=out_ap, in_=in_ap).then_inc(semT.sem, 16)
        d = (semT.sem, semT.semcnt, "dma")
        self._commit(d, reads, writes)

    def barrier(self):
        deps = [(c[0], c[1], e) for e, c in self.cur.items()]
        latest = {}
        for t, sem in self.dmatiles:
            if t.sem is sem:
                latest[id(sem)] = (sem, t.semcnt, "dma")
        deps += list(latest.values())
        for e in self.E:
            self._wait(e, deps)

    def stage(self, base=None, name=None):
        self.barrier()
        if getattr(self, "_scope", None) is not None:
            self.nc.leave_named_scope(self._scope[0], self._scope[1], False)
            self._scope = None
        if name is not None:
            self._nscope = getattr(self, "_nscope", 0) + 1
            nm = f"{self._nscope:02d}_{name}"
            sid, _ = self.nc.enter_named_scope(nm, False)
            self._scope = (nm, sid)
        for t, sem in self.dmatiles:
            if t.sem is sem:
                self.free_sems[t.semq].append((sem, t.semcnt))
                t.sem = None
        self.dmatiles = []
        if base is not None:
            self.sb_off = base

    def sb(self, shape, dtype, name="t"):
        nbytes = int(np.prod(shape[1:])) * (2 if dtype == BF16 else 4)
        nbytes = (nbytes + 31) // 32 * 32
        off = self.sb_off
        self.sb_off += nbytes
        assert self.sb_off <= SBUF_CAP, f"SBUF overflow {self.sb_off}"
        self.nalloc += 1
        return T(self.nc.alloc_sbuf_tensor_at(f"{name}_{self.nalloc}", list(shape), dtype, offset=off))

    def sbs(self, n, shape, dtype, name="t"):
        return [self.sb(shape, dtype, name) for _ in range(n)]


def make_consts(k):
    C = {}
    d = k.sb([128, 128], F32, "iota")
    k.op("pool", lambda e: e.iota(d[:], [[1, 128]], base=0, channel_multiplier=-1,
                                  allow_small_or_imprecise_dtypes=True), writes=[d])

    def cmp(name, op, dtype, val=1.0, thr=0.0):
        t = k.sb([128, 128], dtype, name)
        k.op("dve", lambda e: e.tensor_scalar(out=t[:], in0=d[:], scalar1=thr, scalar2=val, op0=op, op1=ALU.mult),
             reads=[d], writes=[t])
        C[name] = t
    cmp("ident_bf", ALU.is_equal, BF16)
    cmp("ident_f", ALU.is_equal, F32)
    cmp("le_f", ALU.is_ge, F32)
    cmp("le_bf", ALU.is_ge, BF16)
    cmp("lt_f", ALU.is_gt, F32)
    cmp("gt_f", ALU.is_lt, F32)
    cmp("neg_ge_bf", ALU.is_le, BF16, val=-1.0)
    ones = k.sb([128, 128], BF16, "ones")
    k.op("dve", lambda e: e.memset(ones[:], 1.0), writes=[ones])
    C["ones_bf"] = ones
    onesf = k.sb([128, 128], F32, "onesf")
    k.op("dve", lambda e: e.memset(onesf[:], 1.0), writes=[onesf])
    C["ones_f"] = onesf
    return C


def load_small(k, dram_ap, shape, dtype=F32, q="sp"):
    t = k.sb(shape, dtype, "small")
    k.dma(q, t[:], dram_ap, t, writes=[t])
    return t


class NormBufs:
    def __init__(self, k):
        self.sq = k.sbs(2, [128, 4, TG], BF16, "sq")
        self.lnt = k.sb([128, TG], F32, "lnt")
        self.rstd = k.sb([128, TG], F32, "rstd")


def rms_stats(k, C, nb, src, nchunks, Pn, dim, w=TG, eps=EPS):
    allp, rd = [], []
    for q in range(nchunks // 4):
        sq = nb.sq[q % 2]
        k.op("act", lambda e: e.activation(out=sq[:, :, 0:w], in_=src[:, 4 * q:4 * q + 4, 0:w], func=AF.Square),
             reads=[src], writes=[sq])
        allp = [(C["ones_bf"][:], sq[:, j, 0:w]) for j in range(4)]
        emit_partial_group(k, Pn, allp, [sq], q == 0, q == nchunks // 4 - 1, out_ap=Pn[:, 0:w])
    k.op("act", lambda e: e.activation(out=nb.lnt[:, 0:w], in_=Pn[:, 0:w], func=AF.Ln, scale=1.0 / dim, bias=eps),
         reads=[Pn], writes=[nb.lnt])
    k.op("act", lambda e: e.activation(out=nb.rstd[:, 0:w], in_=nb.lnt[:, 0:w], func=AF.Exp, scale=-0.5),
         reads=[nb.lnt], writes=[nb.rstd])
    return nb.rstd


def pre_norm(k, C, nb, xs, gcol, hn, Pn, w=TG):
    rstd = rms_stats(k, C, nb, xs, DC, Pn, D, w)
    for c in range(DC):
        k.op("dve", lambda e: e.scalar_tensor_tensor(out=hn[c][:, 0:w], in0=xs[:, c, 0:w], scalar=gcol[:, c:c + 1],
                                                     in1=rstd[:, 0:w], op0=ALU.mult, op1=ALU.mult),
             reads=[xs, rstd, gcol], writes=[hn[c]])


def post_norm_accum(k, C, nb, y, gcol, Pn, x_dst, tg, ws=None, after=1):
    rstd = rms_stats(k, C, nb, y, DC, Pn, D)
    for c in range(DC):
        k.op("dve", lambda e: e.scalar_tensor_tensor(out=y[:, c, :], in0=y[:, c, :], scalar=gcol[:, c:c + 1],
                                                     in1=rstd[:], op0=ALU.mult, op1=ALU.mult),
             reads=[y, rstd, gcol], writes=[y])

    def store():
        k.dma("pool", x_tile_ap(x_dst, tg), y[:], y, reads=[y], writes=[x_dst.tt(tg)], accum=True)
    if ws is None:
        store()
    else:
        ws.defer(store, after)


class WStream:
    def __init__(self, k, nbuf, kc, ncols):
        self.k = k
        self.bufs = k.sbs(nbuf, [128, kc, ncols], BF16, "w")
        self.i = 0
        self.pending = []

    def defer(self, fn, after=4):
        self.pending.append([after, fn])

    def flush(self):
        for _, fn in self.pending:
            fn()
        self.pending = []

    def load(self, W, k0, kc, n0, ncols):
        t = self.bufs[self.i % len(self.bufs)]
        self.i += 1
        src = W[k0:k0 + kc * 128, n0:n0 + ncols].rearrange("(c p) n -> p c n", p=128)
        self.k.dma("pool", t[:, 0:kc, 0:ncols], src, t, writes=[t])
        for p in list(self.pending):
            p[0] -= 1
            if p[0] <= 0:
                self.pending.remove(p)
                p[1]()
        return t


def run_lockstep(gens, width):
    it = iter(gens)
    active = []
    done = False
    while True:
        while not done and len(active) < width:
            g = next(it, None)
            if g is None:
                done = True
                break
            active.append(g)
        if not active:
            break
        for g in list(active):
            try:
                next(g)
            except StopIteration:
                active.remove(g)


def x_tile_ap(xd, tg):
    return xd[:, tg * TG:(tg + 1) * TG].rearrange("(c p) t -> p c t", p=128)


def emit_ffn(k, C, NT, x_src, x_dst, W, li, base):
    k.stage(base, "ffn")
    g_pre = load_small(k, W["ln_ffn_pre"][li], [128, DC])
    g_post = load_small(k, W["ln_ffn_post"][li], [128, DC])
    cw = load_small(k, W["ffn_conv_w"][li], [128, 88, 3])
    cb = load_small(k, W["ffn_conv_b"][li], [128, 88])
    halo = k.sb([128, 88, 2], F32, "halo")
    k.op("pool", lambda e: e.memset(halo[:], 0.0), writes=[halo])
    nb = NormBufs(k)
    xs = k.sb([128, DC, TG], F32, "xs")
    hn = k.sbs(DC, [128, TG], BF16, "hn")
    act = k.sbs(44, [128, TG], BF16, "act")
    y = k.sb([128, DC, TG], F32, "y")
    xpad = k.sbs(4, [128, TG + 2], F32, "xpad")
    cv = k.sbs(4, [128, TG], F32, "cv")
    ws = WStream(k, 4, 16, 256)
    w_in = W["ffn_w_in"][li]
    w_out = W["ffn_w_out"][li]
    Pn = k.P[7]
    def prefetch(tg):
        k.dma("sp", xs[:], x_tile_ap(x_src, tg), xs, reads=[x_src.tt(tg)], writes=[xs])
        pre_norm(k, C, nb, xs, g_pre, hn, Pn)

    prefetch(0)
    for tg in range(NT):
        wcache = {}

        def pair(j):
            J, jj = j // 2, j % 2
            if jj == 0:
                wcache[J] = (ws.load(w_in, 0, 16, J * 256, 256), ws.load(w_in, 0, 16, FFN + J * 256, 256))
            wts = wcache[J]
            chs = (j, j + 44)
            Pbs = (k.P[(2 * j) % 4], k.P[(2 * j + 1) % 4])
            xps = (xpad[(2 * j) % 4], xpad[(2 * j + 1) % 4])
            os_ = (cv[(2 * j) % 4], cv[(2 * j + 1) % 4])
            for hf in range(2):
                k.mmg(Pbs[hf], Pbs[hf][:], [(wts[hf][:, c, jj * 128:(jj + 1) * 128], hn[c][:]) for c in range(DC)], [wts[hf]] + hn)
            yield
            for hf in range(2):
                xp, ch, Pb = xps[hf], chs[hf], Pbs[hf]
                k.op("act", lambda e: e.activation(out=xp[:, 0:2], in_=halo[:, ch, :], func=AF.Copy), reads=[halo], writes=[xp])
                k.op("act", lambda e: e.activation(out=xp[:, 2:TG + 2], in_=Pb[:], func=AF.Copy), reads=[Pb], writes=[xp])
                k.op("act", lambda e: e.activation(out=halo[:, ch, :], in_=xp[:, TG:TG + 2], func=AF.Copy), reads=[xp], writes=[halo])
            yield
            for hf in range(2):
                xp, ch, o = xps[hf], chs[hf], os_[hf]
                k.op("dve", lambda e: e.tensor_scalar(out=o[:], in0=xp[:, 2:TG + 2], scalar1=cw[:, ch, 2:3],
                                                      scalar2=cb[:, ch:ch + 1], op0=ALU.mult, op1=ALU.add),
                     reads=[xp, cw, cb], writes=[o])
            yield
            for tap in (1, 0):
                for hf in range(2):
                    xp, ch, o = xps[hf], chs[hf], os_[hf]
                    k.op("dve", lambda e: e.scalar_tensor_tensor(out=o[:], in0=xp[:, tap:tap + TG],
                                                                 scalar=cw[:, ch, tap:tap + 1], in1=o[:],
                                                                 op0=ALU.mult, op1=ALU.add),
                         reads=[xp, cw, o], writes=[o])
                yield
            og, ou = os_
            k.op("act", lambda e: e.activation(out=og[:], in_=og[:], func=AF.Gelu_apprx_tanh), reads=[og], writes=[og])
            yield
            k.op("dve", lambda e: e.tensor_tensor(out=act[j][:], in0=og[:], in1=ou[:], op=ALU.mult),
                 reads=[og, ou], writes=[act[j]])

        run_lockstep((pair(j) for j in range(44)), 2)
        if tg + 1 < NT:
            prefetch(tg + 1)
        for nbk in range(4):
            for kp in range(4):
                wts = [ws.load(w_out, kp * 11 * 128, 11, nbk * 512 + h * 256, 256) for h in range(2)]
                for jj in range(4):
                    Pb = k.P[jj]
                    wt = wts[jj // 2]
                    pairs = [(wt[:, c, (jj % 2) * 128:(jj % 2 + 1) * 128], act[kp * 11 + c][:]) for c in range(11)]
                    emit_partial_group(k, Pb, pairs, [wt] + act[kp * 11:kp * 11 + 11], kp == 0, kp == 3)
            for jj in range(4):
                c = nbk * 4 + jj
                k.op("act", lambda e: e.activation(out=y[:, c, :], in_=k.P[jj][:], func=AF.Copy),
                     reads=[k.P[jj]], writes=[y])
        post_norm_accum(k, C, nb, y, g_post, Pn, x_dst, tg, ws, after=4)
    ws.flush()


def emit_partial_group(k, P, pairs, reads, first, last, out_ap=None):
    if out_ap is None:
        out_ap = P[:]
    deps, raw = k._deps(reads, [P] if first else [])
    k._wait("pe", deps, raw)
    n = len(pairs)
    ins = None
    for i, (l, r) in enumerate(pairs):
        ins = k.nc.tensor.matmul(out_ap, l, r, start=(first and i == 0), stop=(last and i == n - 1))
    d = k._tick("pe")
    ins.then_inc(d[0], 1)
    k._commit(d, reads, [P] if last else [])
    if not last:
        pass


def emit_xa(k, C, NT, x_src, x_dst, W, li, base):
    k.stage(base, "xa")
    g_mem = load_small(k, W["ln_mem"][li], [128, DC])
    g_pre = load_small(k, W["ln_xa_pre"][li], [128, DC])
    g_post = load_small(k, W["ln_xa_post"][li], [128, DC])
    nb = NormBufs(k)
    xs = k.sb([128, DC, TG], F32, "xs")
    hn = k.sbs(DC, [128, TG], BF16, "hn")
    y = k.sb([128, DC, TG], F32, "y")
    tmp = k.sbs(2, [128, TG], F32, "tmp")
    ws = WStream(k, 3, 16, 256)
    kT = k.sbs(4, [128, 256], BF16, "kT")
    v = k.sbs(2, [128, 512], BF16, "v")
    qT = k.sbs(4, [128, TG], BF16, "qT")
    E = k.sbs(8, [128, TG], BF16, "E")
    oT = k.sbs(4, [128, TG], BF16, "oT")
    rec = k.sbs(2, [128, TG], F32, "rec")
    Pn = k.P[7]
    wq, wkv, wo = W["xa_wq"][li], W["xa_wkv"][li], W["xa_wo"][li]
    memT = W["memT"]
    k.dma("sp", xs[:, :, 0:256], memT.ap.rearrange("(c p) t -> p c t", p=128), xs, reads=[memT.T], writes=[xs])
    pre_norm(k, C, nb, xs, g_mem, hn, Pn, w=256)
    for t4 in range(4):
        wt = ws.load(wkv, 0, 16, t4 * 256, 256)
        if t4 < 2:
            for jj in range(2):
                h = t4 * 2 + jj
                Pb = k.P[h % 4]
                k.mmg(Pb, Pb[:, 0:256], [(wt[:, c, jj * 128:(jj + 1) * 128], hn[c][:, 0:256]) for c in range(DC)], [wt] + hn)
                k.op("act", lambda e: e.activation(out=kT[h][:], in_=Pb[:, 0:256], func=AF.Copy), reads=[Pb], writes=[kT[h]])
        else:
            half = t4 - 2
            for mc in range(2):
                Pb = k.P[4 + mc]
                k.mmg(Pb, Pb[:, half * 256:(half + 1) * 256],
                      [(hn[c][:, mc * 128:(mc + 1) * 128], wt[:, c, :]) for c in range(DC)], [wt] + hn)
                k.op("act", lambda e: e.activation(out=v[mc][:, half * 256:(half + 1) * 256],
                                                   in_=Pb[:, half * 256:(half + 1) * 256], func=AF.Copy),
                     reads=[Pb], writes=[v[mc]])
    scale = 128.0 ** -0.5
    def prefetch(tg):
        k.dma("sp", xs[:], x_tile_ap(x_src, tg), xs, reads=[x_src.tt(tg)], writes=[xs])
        pre_norm(k, C, nb, xs, g_pre, hn, Pn)

    prefetch(0)
    for tg in range(NT):
        for t2 in range(2):
            wt = ws.load(wq, 0, 16, t2 * 256, 256)
            for jj in range(2):
                h = t2 * 2 + jj
                Pb = k.P[h % 4]
                k.mmg(Pb, Pb[:], [(wt[:, c, jj * 128:(jj + 1) * 128], hn[c][:]) for c in range(DC)], [wt] + hn)
                k.op("act", lambda e: e.activation(out=qT[h][:], in_=Pb[:], func=AF.Copy), reads=[Pb], writes=[qT[h]])
        if tg + 1 < NT:
            prefetch(tg + 1)
        for h in range(4):
            for mc in range(2):
                Pb = k.P[(2 * h + mc) % 4]
                k.mmg(Pb, Pb[:], [(kT[h][:, mc * 128:(mc + 1) * 128], qT[h][:])], [kT[h], qT[h]])
                Eh = E[2 * h + mc]
                k.op("act", lambda e: e.activation(out=Eh[:], in_=Pb[:], func=AF.Exp, scale=scale), reads=[Pb], writes=[Eh])
            Pd = k.P[4 + h % 2]
            k.mmg(Pd, Pd[:], [(C["ones_bf"][:], E[2 * h + mc][:]) for mc in range(2)], [E[2 * h], E[2 * h + 1]])
            rc = rec[h % 2]
            k.op("dve", lambda e: e.reciprocal(out=rc[:], in_=Pd[:]), reads=[Pd], writes=[rc])
            Po = k.P[6]
            k.mmg(Po, Po[:], [(v[mc][:, h * 128:(h + 1) * 128], E[2 * h + mc][:]) for mc in range(2)],
                  [v[0], v[1], E[2 * h], E[2 * h + 1]])
            k.op("dve", lambda e: e.tensor_tensor(out=oT[h][:], in0=Po[:], in1=rc[:], op=ALU.mult),
                 reads=[Po, rc], writes=[oT[h]])
        for t8 in range(8):
            wt = ws.load(wo, 0, 4, t8 * 256, 256)
            for jj in range(2):
                c = t8 * 2 + jj
                Pb = k.P[c % 4]
                k.mmg(Pb, Pb[:], [(wt[:, h, jj * 128:(jj + 1) * 128], oT[h][:]) for h in range(4)], [wt] + oT)
                k.op("act", lambda e: e.activation(out=y[:, c, :], in_=Pb[:], func=AF.Copy), reads=[Pb], writes=[y])
        post_norm_accum(k, C, nb, y, g_post, Pn, x_dst, tg, ws, after=2)
    ws.flush()


def bcast_mid(ap, n):
    pat = [list(p) for p in ap.ap]
    assert len(pat) == 2
    return bass.AP(ap.tensor, ap.offset, [pat[0], [0, n], pat[1]])


def bcast_last(ap, n):
    pat = [list(p) for p in ap.ap]
    assert len(pat) == 2
    return bass.AP(ap.tensor, ap.offset, [pat[0], pat[1], [0, n]])


def emit_outproj(k, C, NT, feat, KC, w_out, g_post_ap, x_src, x_dst, base):
    k.stage(base, "outproj")
    g_post = load_small(k, g_post_ap, [128, DC])
    nb = NormBufs(k)
    y = k.sb([128, DC, TG], F32, "y")
    f = k.sbs(2, [128, KC, TG], BF16, "feat")
    ws = WStream(k, 3, 16, 256)
    Pn = k.P[7]
    nkh = (KC + 15) // 16
    for tg in range(NT):
        ft = f[tg % 2]
        k.dma("sp", ft[:], feat.ap[0:KC * 128, tg * TG:(tg + 1) * TG].rearrange("(c p) t -> p c t", p=128), ft,
              reads=[feat.T], writes=[ft])
        for ct in range(8):
            for kh in range(nkh):
                kc = min(16, KC - kh * 16)
                wt = ws.load(w_out, kh * 2048, kc, ct * 256, 256)
                for jj in range(2):
                    Pb = k.P[(2 * ct + jj) % 4]
                    pairs = [(wt[:, c, jj * 128:(jj + 1) * 128], ft[:, kh * 16 + c, :]) for c in range(kc)]
                    emit_partial_group(k, Pb, pairs, [wt, ft], kh == 0, kh == nkh - 1)
            for jj in range(2):
                c = 2 * ct + jj
                Pb = k.P[c % 4]
                k.op("act", lambda e: e.activation(out=y[:, c, :], in_=Pb[:], func=AF.Copy), reads=[Pb], writes=[y])
        post_norm_accum(k, C, nb, y, g_post, Pn, x_dst, tg, ws)
    ws.flush()


def emit_sgu_front(k, C, NT, x_src, W, li, feat, base):
    k.stage(base, "sgu")
    j = li // 3
    g_pre = load_small(k, W["ln_mix_pre"][li], [128, DC])
    vg = load_small(k, W["sg_v_norm_g"][j], [128, 32])
    vb = load_small(k, W["sg_v_norm_b"][j], [128, 32])
    wsp_f = load_small(k, W["sg_w_spatial"][j], [128, 16, 128])
    bias_bc = k.sb([128, 16, 128], F32, "bias_bc")
    k.dma("sp", bias_bc[:], W["sg_b_spatial"].ap[j].rearrange("g t -> (g t)").partition_broadcast(128)
          .rearrange("p (g t) -> p g t", g=16), bias_bc, writes=[bias_bc])
    k.op("dve", lambda e: e.tensor_tensor(out=wsp_f[:], in0=wsp_f[:], in1=bcast_mid(C["le_f"][:], 16), op=ALU.mult),
         reads=[wsp_f, C["le_f"]], writes=[wsp_f])
    wsp_b = k.sb([128, 16, 128], BF16, "wsp_b")
    k.op("dve", lambda e: e.tensor_copy(out=wsp_b[:], in_=wsp_f[:]), reads=[wsp_f], writes=[wsp_b])
    rs_bc = k.sb([128, 16, 128], F32, "rs_bc")
    for q in range(4):
        Pb = k.P[q]
        k.mmg(Pb, Pb[:], [(C["ones_bf"][:], wsp_b[:, 4 * q:4 * q + 4, :])], [wsp_b])
        k.op("act", lambda e: e.activation(out=rs_bc[:, 4 * q:4 * q + 4, :], in_=Pb[:], func=AF.Copy), reads=[Pb], writes=[rs_bc])
    nb = NormBufs(k)
    xs = k.sb([128, DC, TG], F32, "xs")
    hn = k.sbs(DC, [128, TG], BF16, "hn")
    uT = k.sbs(32, [128, TG], BF16, "uT")
    vraw = k.sbs(4, [128, 4096], BF16, "vraw")
    junk = k.sbs(2, [128, 256], BF16, "junk")
    s1 = k.sbs(4, [128, 16], F32, "s1")
    s2 = k.sbs(4, [128, 16], F32, "s2")
    st = k.sbs(4, [128, 8], F32, "st")
    wsn = k.sbs(4, [128, 16, 128], BF16, "wsn")
    mrep = k.sbs(4, [128, 128], BF16, "mrep")
    stsem = T(None)
    t1 = k.sbs(2, [128, TG], F32, "t1")
    ws = WStream(k, 3, 16, 256)
    w_in = W["sg_w_in"][j]
    Pn = k.P[7]
    for tg in range(NT):
        k.dma("sp", xs[:], x_tile_ap(x_src, tg), xs, reads=[x_src.tt(tg)], writes=[xs])
        pre_norm(k, C, nb, xs, g_pre, hn, Pn)
        for ct in range(16):
            wt = ws.load(w_in, 0, 16, ct * 256, 256)
            for jj in range(2):
                dc = 2 * ct + jj
                Pb = k.P[dc % 4]
                k.mmg(Pb, Pb[:], [(wt[:, c, jj * 128:(jj + 1) * 128], hn[c][:]) for c in range(DC)], [wt] + hn)
                k.op("act", lambda e: e.activation(out=uT[dc][:], in_=Pb[:], func=AF.Gelu_apprx_tanh), reads=[Pb], writes=[uT[dc]])
        for ct in range(16):
            wt = ws.load(w_in, 0, 16, 4096 + ct * 256, 256)
            for tb in range(4):
                Pb = k.P[tb]
                k.mmg(Pb, Pb[:, 0:256], [(hn[c][:, tb * 128:(tb + 1) * 128], wt[:, c, :]) for c in range(DC)], [wt] + hn)
                k.op("act", lambda e: e.activation(out=vraw[tb][:, ct * 256:(ct + 1) * 256], in_=Pb[:, 0:256],
                                                   func=AF.Gelu_apprx_tanh, accum_out=s1[tb][:, ct:ct + 1]),
                     reads=[Pb], writes=[vraw[tb], s1[tb]])
                jk = junk[tb % 2]
                k.op("act", lambda e: e.activation(out=jk[:], in_=vraw[tb][:, ct * 256:(ct + 1) * 256],
                                                   func=AF.Square, accum_out=s2[tb][:, ct:ct + 1]),
                     reads=[vraw[tb]], writes=[jk, s2[tb]])
        for tb in range(4):
            S = st[tb]
            k.op("dve", lambda e: e.reduce_sum(out=S[:, 0:1], in_=s1[tb][:], axis=mybir.AxisListType.X), reads=[s1[tb]], writes=[S])
            k.op("dve", lambda e: e.reduce_sum(out=S[:, 1:2], in_=s2[tb][:], axis=mybir.AxisListType.X), reads=[s2[tb]], writes=[S])
            k.op("dve", lambda e: e.tensor_scalar(out=S[:, 0:2], in0=S[:, 0:2], scalar1=1.0 / 4096, scalar2=None, op0=ALU.mult),
                 reads=[S], writes=[S])
            k.op("dve", lambda e: e.tensor_tensor(out=S[:, 2:3], in0=S[:, 0:1], in1=S[:, 0:1], op=ALU.mult), reads=[S], writes=[S])
            k.op("dve", lambda e: e.tensor_tensor(out=S[:, 2:3], in0=S[:, 1:2], in1=S[:, 2:3], op=ALU.subtract), reads=[S], writes=[S])
            k.op("act", lambda e: e.activation(out=S[:, 3:4], in_=S[:, 2:3], func=AF.Ln, bias=EPS), reads=[S], writes=[S])
            k.op("act", lambda e: e.activation(out=S[:, 3:4], in_=S[:, 3:4], func=AF.Exp, scale=-0.5), reads=[S], writes=[S])
            k.op("dve", lambda e: e.tensor_scalar(out=S[:, 4:5], in0=S[:, 0:1], scalar1=-1.0, scalar2=None, op0=ALU.mult),
                 reads=[S], writes=[S])
            k.op("dve", lambda e: e.tensor_scalar(out=wsn[tb][:], in0=wsp_f[:], scalar1=S[:, 3:4], scalar2=None, op0=ALU.mult),
                 reads=[wsp_f, S], writes=[wsn[tb]])
            k.op("dve", lambda e: e.tensor_scalar(out=mrep[tb][:], in0=C["ones_f"][:], scalar1=S[:, 4:5], scalar2=None, op0=ALU.mult),
                 reads=[C["ones_f"], S], writes=[mrep[tb]])
        for dc in range(32):
            g = dc // 2
            Pb = k.P[dc % 4]
            for tb in range(4):
                k.mmg(Pb, Pb[:, tb * 128:(tb + 1) * 128],
                      [(vraw[tb][:, dc * 128:(dc + 1) * 128], wsn[tb][:, g, :]), (mrep[tb][:], wsn[tb][:, g, :])],
                      [vraw[tb], wsn[tb], mrep[tb]])
            tt = t1[dc % 2]
            ttv = tt[:].rearrange("p (a t) -> p a t", a=4)
            k.op("dve", lambda e: e.scalar_tensor_tensor(out=ttv, in0=Pb[:].rearrange("p (a t) -> p a t", a=4),
                                                         scalar=vg[:, dc:dc + 1], in1=bcast_mid(bias_bc[:, g, :], 4),
                                                         op0=ALU.mult, op1=ALU.add),
                 reads=[Pb, vg, bias_bc], writes=[tt])
            k.op("dve", lambda e: e.scalar_tensor_tensor(out=ttv, in0=bcast_mid(rs_bc[:, g, :], 4),
                                                          scalar=vb[:, dc:dc + 1], in1=ttv, op0=ALU.mult, op1=ALU.add),
                 reads=[rs_bc, vb, tt], writes=[tt])
            k.op("dve", lambda e: e.tensor_tensor(out=uT[dc][:], in0=tt[:], in1=uT[dc][:], op=ALU.mult),
                 reads=[tt, uT[dc]], writes=[uT[dc]])
            k.dma("sp", feat.ap[dc * 128:(dc + 1) * 128, tg * TG:(tg + 1) * TG], uT[dc][:], uT[dc],
                  reads=[uT[dc]], writes=[feat.T])


def emit_sb_front(k, C, NT, x_src, W, li, qk_d, v_d, o_d, base):
    L = NT * TG
    NB = L // 128
    j = li // 3
    w_qkv = W["sb_w_qkv"][j]
    k.stage(base, "sb1")
    g_pre = load_small(k, W["ln_mix_pre"][li], [128, DC])
    nb = NormBufs(k)
    xs = k.sb([128, DC, TG], F32, "xs")
    hn = k.sbs(DC, [128, TG], BF16, "hn")
    stg = k.sbs(4, [128, TG], BF16, "stg")
    vtok = k.sbs(4, [128, 2048], BF16, "vtok")
    stsem = T(None)
    ws = WStream(k, 3, 16, 256)
    Pn = k.P[7]
    scale = 128.0 ** -0.5
    for tg in range(NT):
        k.dma("sp", xs[:], x_tile_ap(x_src, tg), xs, reads=[x_src.tt(tg)], writes=[xs])
        pre_norm(k, C, nb, xs, g_pre, hn, Pn)
        for ct in range(16):
            wt = ws.load(w_qkv, 0, 16, ct * 256, 256)
            for jj in range(2):
                r = 2 * ct + jj
                Pb = k.P[r % 4]
                k.mmg(Pb, Pb[:], [(wt[:, c, jj * 128:(jj + 1) * 128], hn[c][:]) for c in range(DC)], [wt] + hn)
                sg = stg[r % 4]
                k.op("act", lambda e: e.activation(out=sg[:], in_=Pb[:], func=AF.Copy, scale=(scale if ct < 8 else 1.0)),
                     reads=[Pb], writes=[sg])
                k.dma("sp", qk_d.ap[r * 128:(r + 1) * 128, tg * TG:(tg + 1) * TG], sg[:], sg, reads=[sg], writes=[qk_d.T])
        for ct in range(8):
            wt = ws.load(w_qkv, 0, 16, 4096 + ct * 256, 256)
            for tb in range(4):
                Pb = k.P[tb]
                k.mmg(Pb, Pb[:, 0:256], [(hn[c][:, tb * 128:(tb + 1) * 128], wt[:, c, :]) for c in range(DC)], [wt] + hn)
                k.op("act", lambda e: e.activation(out=vtok[tb][:, ct * 256:(ct + 1) * 256], in_=Pb[:, 0:256], func=AF.Copy),
                     reads=[Pb], writes=[vtok[tb]])
        for tb in range(4):
            r0 = tg * TG + tb * 128
            k.dma("sp", v_d.ap[r0:r0 + 128, :], vtok[tb][:], vtok[tb], reads=[vtok[tb]], writes=[v_d.T])
    k.stage(base, "sb2")
    NEGV = -30000.0
    neg = k.sb([128, 4, TG], BF16, "neg")
    negge = k.sb([128, 128], BF16, "negge")
    k.op("dve", lambda e: e.tensor_scalar(out=negge[:], in0=C["gt_f"][:], scalar1=-1.0, scalar2=NEGV, op0=ALU.add, op1=ALU.mult),
         reads=[C["gt_f"]], writes=[negge])
    k.op("dve", lambda e: e.tensor_scalar(out=negge[:], in0=C["lt_f"][:], scalar1=-1.0, scalar2=-NEGV, op0=ALU.add, op1=ALU.mult),
         reads=[C["lt_f"]], writes=[negge])
    k.op("dve", lambda e: e.memset(neg[:], 0.0), writes=[neg])
    for a in range(4):
        k.op("dve", lambda e: e.tensor_copy(out=neg[:, a, a * 128:(a + 1) * 128], in_=negge[:]), reads=[negge], writes=[neg])
        if a > 0:
            k.op("dve", lambda e: e.memset(neg[:, a, 0:a * 128], NEGV), writes=[neg])
    negrow = k.sb([1, 128], BF16, "negrow")
    k.op("dve", lambda e: e.memset(negrow[:], -1.0), writes=[negrow])
    NCHN = 2
    negones = k.sb([128, 128], BF16, "negones")
    k.op("dve", lambda e: e.memset(negones[:], -1.0), writes=[negones])
    qT = k.sbs(NCHN, [128, L], BF16, "qT")
    kT = k.sbs(NCHN, [128, L], BF16, "kT")
    vh = k.sbs(NCHN, [128, NB, 128], BF16, "vh")
    eb = k.sbs(NCHN, [128, TG], F32, "eb")
    spb = k.sbs(NCHN, [128, TG], BF16, "spb")
    Ab = k.sbs(NCHN, [128, TG], BF16, "Ab")
    S32 = k.sbs(NCHN, [128, TG], F32, "S32")
    Sb = k.sbs(NCHN, [128, TG], BF16, "Sb")
    ob = k.sbs(NCHN, [128, TG], BF16, "ob")

    def chain(c):
        Pz, Po = k.P[c], k.P[4 + c]
        q_, k_, v_ = qT[c], kT[c], vh[c]
        e_, sp_, A_, S_, Sb_, o_ = eb[c], spb[c], Ab[c], S32[c], Sb[c], ob[c]
        for quad in range(16 // NCHN):
            h = quad * NCHN + c
            k.dma("sp", q_[:], qk_d.ap[h * 128:(h + 1) * 128, :], q_, reads=[qk_d.T], writes=[q_])
            k.dma("sp", k_[:], qk_d.ap[2048 + h * 128:2048 + (h + 1) * 128, :], k_, reads=[qk_d.T], writes=[k_])
            for q4 in range(0, NB, 8):
                k.dma("sp", v_[:, q4:q4 + 8, :],
                      v_d.ap[q4 * 128:(q4 + 8) * 128, h * 128:(h + 1) * 128].rearrange("(b p) d -> p b d", p=128), v_,
                      reads=[v_d.T], writes=[v_])
            for tg in range(NT):
                nkb = 4 * (tg + 1)
                qs = q_[:, tg * TG:(tg + 1) * TG]
                for sb in range(nkb - 1, -1, -1):
                    first, last = sb == nkb - 1, sb == 0
                    a = sb - 4 * tg
                    zp = [(k_[:, sb * 128:(sb + 1) * 128], qs)]
                    zr = [k_, q_]
                    if a >= 0:
                        zp.append((C["ident_bf"][:], neg[:, a, :]))
                        zr.append(neg)
                    k.mmg(Pz, Pz[:], zp, zr)
                    yield
                    k.op("act", lambda e: e.activation(out=e_[:], in_=Pz[:], func=AF.Exp), reads=[Pz], writes=[e_])
                    yield
                    k.op("act", lambda e: e.activation(out=sp_[:], in_=e_[:], func=AF.Ln, bias=1.0), reads=[e_], writes=[sp_])
                    yield
                    g2 = zp + [(C["neg_ge_bf"][:], sp_[:])]
                    g2r = zr + [sp_, C["neg_ge_bf"]]
                    if not first:
                        g2.append((negones[:], Sb_[:]))
                        g2r += [Sb_, negones]
                    k.mmg(Pz, Pz[:], g2, g2r)
                    yield
                    k.op("act", lambda e: e.activation(out=A_[:], in_=Pz[:], func=AF.Exp), reads=[Pz], writes=[A_])
                    yield
                    emit_partial_group(k, Po, [(v_[:, sb, :], A_[:])], [v_, A_], first, last)
                    if not last:
                        if first:
                            k.op("pool", lambda e: e.tensor_copy(out=S_[:], in_=sp_[:]), reads=[sp_], writes=[S_])
                        else:
                            k.op("pool", lambda e: e.tensor_tensor(out=S_[:], in0=S_[:], in1=sp_[:], op=ALU.add), reads=[S_, sp_], writes=[S_])
                        k.op("dve", lambda e: e.tensor_copy(out=Sb_[:], in_=S_[:]), reads=[S_], writes=[Sb_])
                    else:
                        k.op("dve", lambda e: e.tensor_copy(out=o_[:], in_=Po[:]), reads=[Po], writes=[o_])
                        k.dma("sp", o_d.ap[h * 128:(h + 1) * 128, tg * TG:(tg + 1) * TG], o_[:], o_, reads=[o_], writes=[o_d.T])
                    yield

    run_lockstep((chain(c) for c in range(NCHN)), NCHN)


def emit_ssd_front(k, C, NT, x_src, W, li, S, base):
    L = NT * TG
    NCH = L // 128
    j = li // 3
    w_in = W["ssd_w_in"][j]
    zs_d, bc_d, btok_d, xs_d, y_d, dtda_d, feat = S["featA"], S["featC"], S["featV"], S["ssdX"], S["ssdY"], S["dtda"], S["featB"]
    AX = mybir.AxisListType.X
    k.stage(base, "ssd1")
    g_pre = load_small(k, W["ln_mix_pre"][li], [128, DC])
    cw = load_small(k, W["ssd_conv_w"][j], [128, 48, 4])
    cb = load_small(k, W["ssd_conv_b"][j], [128, 48])
    dtb = load_small(k, W["ssd_dt_bias"][j], [64, 1])
    alog = load_small(k, W["ssd_a_log"][j], [64, 1])
    aneg = k.sb([64, 1], F32, "aneg")
    k.op("act", lambda e: e.activation(out=aneg[:], in_=alog[:], func=AF.Exp), reads=[alog], writes=[aneg])
    k.op("dve", lambda e: e.tensor_scalar(out=aneg[:], in0=aneg[:], scalar1=-1.0, scalar2=None, op0=ALU.mult), reads=[aneg], writes=[aneg])
    halo = k.sb([128, 48, 4], F32, "halo")
    k.op("pool", lambda e: e.memset(halo[:], 0.0), writes=[halo])
    nb = NormBufs(k)
    xs = k.sb([128, DC, TG], F32, "xs")
    hn = k.sbs(DC, [128, TG], BF16, "hn")
    xs_tok = k.sb([128, 4, 4096], F32, "xs_tok")
    b_tok = k.sb([128, 4, 1024], BF16, "b_tok")
    xpad = k.sbs(3, [128, TG + 4], F32, "xpad")
    cv = k.sbs(3, [128, TG], F32, "cv")
    stg = k.sbs(4, [128, TG], BF16, "stg")
    e1 = k.sb([64, TG], F32, "e1")
    dtT = k.sb([64, TG], F32, "dtT")
    daT = k.sb([64, TG], F32, "daT")
    dtda_tok = k.sb([128, 4, 128], F32, "dtda_tok")
    ws = WStream(k, 3, 16, 256)
    Pn = k.P[7]
    nstg = 0
    for tg in range(NT):
        tcols = slice(tg * TG, (tg + 1) * TG)
        k.dma("sp", xs[:], x_tile_ap(x_src, tg), xs, reads=[x_src.tt(tg)], writes=[xs])
        pre_norm(k, C, nb, xs, g_pre, hn, Pn)
        for ct in range(16):
            wt = ws.load(w_in, 0, 16, ct * 256, 256)
            for jj in range(2):
                r = 2 * ct + jj
                Pb = k.P[r % 4]
                k.mmg(Pb, Pb[:], [(wt[:, c, jj * 128:(jj + 1) * 128], hn[c][:]) for c in range(DC)], [wt] + hn)
                sg = stg[nstg % 4]
                nstg += 1
                k.op("act", lambda e: e.activation(out=sg[:], in_=Pb[:], func=AF.Silu), reads=[Pb], writes=[sg])
                k.dma("sp", zs_d.ap[r * 128:(r + 1) * 128, tcols], sg[:], sg, reads=[sg], writes=[zs_d.T])
        wcache = {}

        def xchunk(ch):
            nonlocal nstg
            ct, jj = ch // 2, ch % 2
            if jj == 0:
                wcache[ct] = ws.load(w_in, 0, 16, 4096 + ct * 256, 256)
            wt = wcache[ct]
            Pb = k.P[ch % 4]
            xp = xpad[ch % 3]
            o = cv[ch % 3]
            k.mmg(Pb, Pb[:], [(wt[:, c, jj * 128:(jj + 1) * 128], hn[c][:]) for c in range(DC)], [wt] + hn)
            yield
            k.op("dve", lambda e: e.tensor_copy(out=xp[:, 0:4], in_=halo[:, ch, :]), reads=[halo], writes=[xp])
            k.op("act", lambda e: e.activation(out=xp[:, 4:TG + 4], in_=Pb[:], func=AF.Copy), reads=[Pb], writes=[xp])
            k.op("dve", lambda e: e.tensor_copy(out=halo[:, ch, :], in_=xp[:, TG:TG + 4]), reads=[xp], writes=[halo])
            yield
            k.op("dve", lambda e: e.tensor_scalar(out=o[:], in0=xp[:, 4:TG + 4], scalar1=cw[:, ch, 3:4],
                                                  scalar2=cb[:, ch:ch + 1], op0=ALU.mult, op1=ALU.add),
                 reads=[xp, cw, cb], writes=[o])
            yield
            for tap in (2, 1, 0):
                k.op("dve", lambda e: e.scalar_tensor_tensor(out=o[:], in0=xp[:, tap + 1:tap + 1 + TG], scalar=cw[:, ch, tap:tap + 1],
                                                             in1=o[:], op0=ALU.mult, op1=ALU.add),
                     reads=[xp, cw, o], writes=[o])
                yield
            k.op("act", lambda e: e.activation(out=o[:], in_=o[:], func=AF.Silu), reads=[o], writes=[o])
            yield
            if ch >= 32:
                sg = stg[nstg % 4]
                nstg += 1
                k.op("dve", lambda e: e.tensor_copy(out=sg[:], in_=o[:]), reads=[o], writes=[sg])
                r = ch - 32
                k.dma("sp", bc_d.ap[r * 128:(r + 1) * 128, tcols], sg[:], sg, reads=[sg], writes=[bc_d.T])
            if ch < 40:
                Pt = k.P[4 + ch % 3]
                for tb in range(4):
                    k.transpose(Pt, Pt[:, tb * 128:(tb + 1) * 128], o[:, tb * 128:(tb + 1) * 128], C["ident_f"][:], [o, C["ident_f"]])
                yield
                pv = Pt[:].rearrange("p (a t) -> p a t", a=4)
                if ch < 32:
                    k.op("act", lambda e: e.activation(out=xs_tok[:, :, ch * 128:(ch + 1) * 128], in_=pv, func=AF.Copy),
                         reads=[Pt], writes=[xs_tok])
                else:
                    g = ch - 32
                    k.op("act", lambda e: e.activation(out=b_tok[:, :, g * 128:(g + 1) * 128], in_=pv, func=AF.Copy),
                         reads=[Pt], writes=[b_tok])

        run_lockstep((xchunk(ch) for ch in range(48)), 3)
        wt = ws.load(w_in, 0, 16, 10240, 64)
        Pd = k.P[6]
        k.mmg(Pd, Pd[0:64, :], [(wt[:, c, 0:64], hn[c][:]) for c in range(DC)], [wt] + hn)
        k.op("act", lambda e: e.activation(out=e1[:], in_=Pd[0:64, :], func=AF.Exp, bias=dtb[:, 0:1]), reads=[Pd, dtb], writes=[e1])
        k.op("act", lambda e: e.activation(out=dtT[:], in_=e1[:], func=AF.Ln, bias=1.0), reads=[e1], writes=[dtT])
        k.op("dve", lambda e: e.tensor_scalar(out=daT[:], in0=dtT[:], scalar1=aneg[:, 0:1], scalar2=None, op0=ALU.mult),
             reads=[dtT, aneg], writes=[daT])
        Pt = k.P[4]
        for tb in range(4):
            k.transpose(Pt, Pt[:, tb * 128:tb * 128 + 64], dtT[:, tb * 128:(tb + 1) * 128], C["ident_f"][0:64, 0:64], [dtT, C["ident_f"]])
            k.transpose(Pt, Pt[:, tb * 128 + 64:tb * 128 + 128], daT[:, tb * 128:(tb + 1) * 128], C["ident_f"][0:64, 0:64], [daT, C["ident_f"]])
        k.op("act", lambda e: e.activation(out=dtda_tok[:], in_=Pt[:].rearrange("p (a t) -> p a t", a=4), func=AF.Copy),
             reads=[Pt], writes=[dtda_tok])
        rows = slice(tg * TG, (tg + 1) * TG)
        k.dma("sp", dtda_d.ap[rows, :].rearrange("(a p) f -> p a f", p=128), dtda_tok[:], dtda_tok, reads=[dtda_tok], writes=[dtda_d.T])
        k.dma("sp", xs_d.ap[rows, :].rearrange("(a p) f -> p a f", p=128), xs_tok[:], xs_tok, reads=[xs_tok], writes=[xs_d.T])
        k.dma("sp", btok_d.ap[rows, 0:1024].rearrange("(a p) f -> p a f", p=128), b_tok[:], b_tok, reads=[b_tok], writes=[btok_d.T])
    import os
    if os.environ.get("SSD_STOP") == "1":
        return
    k.stage(base, "ssd2")
    dbc = k.sb([128, 64], F32, "dbc")
    k.dma("sp", dbc[:], W["ssd_d"].ap[j].partition_broadcast(128), dbc, writes=[dbc])
    xs_c = k.sbs(2, [128, 4096], F32, "xs_c")
    bt_c = k.sbs(2, [128, 8, 128], BF16, "bt_c")
    ct_c = k.sbs(2, [128, 8, 128], BF16, "ct_c")
    bk_c = k.sbs(2, [128, 1024], BF16, "bk_c")
    dd_c = k.sbs(2, [128, 128], F32, "dd_c")
    xdt_ = k.sbs(2, [128, 4096], BF16, "xdt")
    xdtd_ = k.sbs(2, [128, 4096], BF16, "xdtd")
    prev_f = k.sb([128, 4096], F32, "prev_f")
    prev_b = k.sb([128, 4096], BF16, "prev_b")
    k.op("dve", lambda e: e.memset(prev_f[:], 0.0), writes=[prev_f])
    k.op("dve", lambda e: e.memset(prev_b[:], 0.0), writes=[prev_b])
    y_tok = k.sbs(2, [128, 4096], F32, "y_tok")
    acum_ = k.sbs(2, [128, 64], F32, "acum")
    dah_ = k.sbs(2, [128, 64], BF16, "dah")
    dal_ = k.sbs(2, [128, 64], BF16, "dal")
    ea_ = k.sbs(2, [128, 64], F32, "ea")
    cdb_ = k.sbs(2, [128, 64], F32, "cdb")
    decs_ = k.sbs(2, [128, 64], F32, "decs")
    w2_ = k.sbs(2, [128, 64], F32, "w2")
    rhs4 = [k.sbs(2, [128, 4, 128], F32, "rhs4") for _ in range(2)]
    Eb = [k.sbs(2, [128, TG], F32, "Eb") for _ in range(2)]
    MT = [k.sbs(2, [128, 4, 128], BF16, "MT") for _ in range(2)]
    cbm = k.sbs(2, [128, 128], F32, "cbm")
    tA = k.sbs(2, [128, TG], F32, "tA")
    tB = k.sbs(2, [128, TG], F32, "tB")
    v3 = lambda ap: ap.rearrange("p (h d) -> p h d", d=64)

    def prep(c):
        i = c % 2
        xc, bt, ct, bk, dd = xs_c[i], bt_c[i], ct_c[i], bk_c[i], dd_c[i]
        acum, dah, dal, ea, cdb, decs, w2, xdt, xdtd = acum_[i], dah_[i], dal_[i], ea_[i], cdb_[i], decs_[i], w2_[i], xdt_[i], xdtd_[i]
        rows = slice(c * 128, (c + 1) * 128)
        k.dma("sp", xc[:], xs_d.ap[rows, :], xc, reads=[xs_d.T], writes=[xc])
        k.dma("sp", bt[:], bc_d.ap[0:1024, rows].rearrange("(g n) s -> n g s", n=128), bt, reads=[bc_d.T], writes=[bt])
        k.dma("sp", ct[:], bc_d.ap[1024:2048, rows].rearrange("(g n) s -> n g s", n=128), ct, reads=[bc_d.T], writes=[ct])
        k.dma("sp", bk[:], btok_d.ap[rows, 0:1024], bk, reads=[btok_d.T], writes=[bk])
        k.dma("sp", dd[:], dtda_d.ap[rows, :], dd, reads=[dtda_d.T], writes=[dd])
        dt_ap, da_ap = dd[:, 0:64], dd[:, 64:128]
        Pm = k.P[0]
        k.op("dve", lambda e: e.tensor_copy(out=dah[:], in_=da_ap), reads=[dd], writes=[dah])
        k.op("dve", lambda e: e.tensor_tensor(out=dal[:], in0=da_ap, in1=dah[:], op=ALU.subtract), reads=[dd, dah], writes=[dal])
        k.mmg(Pm, Pm[:, 0:64], [(C["le_bf"][:], dah[:]), (C["le_bf"][:], dal[:])], [C["le_bf"], dah, dal])
        k.mmg(Pm, Pm[:, 64:128], [(C["ones_bf"][:], dah[:]), (C["ones_bf"][:], dal[:])], [C["ones_bf"], dah, dal])
        k.op("act", lambda e: e.activation(out=acum[:], in_=Pm[:, 0:64], func=AF.Copy), reads=[Pm], writes=[acum])
        k.op("act", lambda e: e.activation(out=ea[:], in_=Pm[:, 0:64], func=AF.Exp), reads=[Pm], writes=[ea])
        k.op("act", lambda e: e.activation(out=cdb[:], in_=Pm[:, 64:128], func=AF.Exp), reads=[Pm], writes=[cdb])
        k.op("dve", lambda e: e.tensor_tensor(out=decs[:], in0=Pm[:, 64:128], in1=acum[:], op=ALU.subtract), reads=[Pm, acum], writes=[decs])
        k.op("act", lambda e: e.activation(out=decs[:], in_=decs[:], func=AF.Exp), reads=[decs], writes=[decs])
        k.op("dve", lambda e: e.tensor_tensor(out=w2[:], in0=decs[:], in1=dt_ap, op=ALU.mult), reads=[decs, dd], writes=[w2])
        xv = v3(xc[:])
        k.op("dve", lambda e: e.tensor_tensor(out=v3(xdt[:]), in0=xv, in1=bcast_last(dt_ap, 64), op=ALU.mult), reads=[xc, dd], writes=[xdt])
        k.op("pool", lambda e: e.tensor_tensor(out=v3(xdtd[:]), in0=xv, in1=bcast_last(w2[:], 64), op=ALU.mult), reads=[xc, w2], writes=[xdtd])

    def group(c, g):
        i = c % 2
        sl = g % 2
        xc, bt, ct, bk, dd = xs_c[i], bt_c[i], ct_c[i], bk_c[i], dd_c[i]
        ea, cdb, xdt, xdtd = ea_[i], cdb_[i], xdt_[i], xdtd_[i]
        yt = y_tok[i]
        gc = slice(g * 512, (g + 1) * 512)
        Pcb, Ps, Py, Pq = k.P[1], k.P[2 + sl], k.P[4 + sl], k.P[6 + sl]
        cm, ta, tb_ = cbm[sl], tA[sl], tB[sl]
        k.mmg(Pcb, Pcb[:, sl * 128:(sl + 1) * 128], [(bt[:, g, :], ct[:, g, :])], [bt, ct])
        yield
        k.op("dve", lambda e: e.tensor_tensor(out=cm[:], in0=Pcb[:, sl * 128:(sl + 1) * 128], in1=C["le_f"][:], op=ALU.mult),
             reads=[Pcb, C["le_f"]], writes=[cm])
        for hb in range(2):
            h0 = g * 8 + hb * 4
            r4, E_, M_ = rhs4[sl][hb], Eb[sl][hb], MT[sl][hb]
            k.op("dve", lambda e: e.tensor_tensor(out=r4[:], in0=bcast_mid(C["le_f"][:], 4), in1=bcast_last(dd[:, 64 + h0:64 + h0 + 4], 128),
                                                  op=ALU.mult), reads=[C["le_f"], dd], writes=[r4])
            yield
            k.mmg(Ps, Ps[:], [(C["gt_f"][:], r4[:].rearrange("p a t -> p (a t)"))], [C["gt_f"], r4])
            yield
            k.op("act", lambda e: e.activation(out=E_[:], in_=Ps[:], func=AF.Exp), reads=[Ps], writes=[E_])
            yield
            k.op("dve", lambda e: e.tensor_tensor(out=M_[:], in0=E_[:].rearrange("p (a t) -> p a t", a=4), in1=bcast_mid(cm[:], 4), op=ALU.mult),
                 reads=[E_, cm], writes=[M_])
            yield
            for hh in range(4):
                h = h0 + hh
                col = (hb * 4 + hh) * 64
                k.mmg(Py, Py[:, col:col + 64], [(M_[:, hh, :], xdt[:, h * 64:(h + 1) * 64])], [M_, xdt])
            if hb == 0:
                k.mmg(Pq, Pq[:], [(ct[:, g, :], prev_b[:, gc])], [ct, prev_b])
                k.op("pool", lambda e: e.tensor_tensor(out=v3(tb_[:]), in0=v3(xc[:, gc]), in1=bcast_last(dbc[:, g * 8:(g + 1) * 8], 64), op=ALU.mult),
                     reads=[xc, dbc], writes=[tb_])
                yield
                k.op("dve", lambda e: e.tensor_tensor(out=v3(ta[:]), in0=v3(Pq[:]), in1=bcast_last(ea[:, g * 8:(g + 1) * 8], 64), op=ALU.mult),
                     reads=[Pq, ea], writes=[ta])
                k.op("pool", lambda e: e.tensor_tensor(out=v3(prev_f[:, gc]), in0=v3(prev_f[:, gc]), in1=bcast_last(cdb[:, g * 8:(g + 1) * 8], 64), op=ALU.mult),
                     reads=[prev_f, cdb], writes=[prev_f])
                yield
                k.mmg(Pq, Pq[:], [(bk[:, g * 128:(g + 1) * 128], xdtd[:, gc])], [bk, xdtd])
            yield
        k.op("dve", lambda e: e.tensor_tensor(out=ta[:], in0=ta[:], in1=Py[:], op=ALU.add), reads=[ta, Py], writes=[ta])
        k.op("dve", lambda e: e.tensor_tensor(out=prev_f[:, gc], in0=prev_f[:, gc], in1=Pq[:], op=ALU.add), reads=[prev_f, Pq], writes=[prev_f])
        yield
        k.op("pool", lambda e: e.tensor_tensor(out=yt[:, gc], in0=ta[:], in1=tb_[:], op=ALU.add), reads=[ta, tb_], writes=[yt])
        k.op("act", lambda e: e.activation(out=prev_b[:, gc], in_=prev_f[:, gc], func=AF.Copy), reads=[prev_f], writes=[prev_b])

    prep(0)
    for c in range(NCH):
        if c + 1 < NCH:
            prep(c + 1)
        run_lockstep((group(c, g) for g in range(8)), 2)
        rows = slice(c * 128, (c + 1) * 128)
        k.dma("sp", y_d.ap[rows, :], y_tok[c % 2][:], y_tok[c % 2], reads=[y_tok[c % 2]], writes=[y_d.T])
    if os.environ.get("SSD_STOP") == "2":
        return
    k.stage(base, "ssd3")
    ng = load_small(k, W["ssd_norm"][j], [128, 32])
    nb = NormBufs(k)
    ytk = k.sb([128, 4, 4096], F32, "ytk")
    yg = k.sb([128, 32, TG], F32, "yg")
    zsb = k.sbs(4, [128, TG], BF16, "zsb")
    fo = k.sb([128, 32, TG], BF16, "fo")
    for tg in range(NT):
        rows = slice(tg * TG, (tg + 1) * TG)
        tcols = slice(tg * TG, (tg + 1) * TG)
        k.dma("sp", ytk[:], y_d.ap[rows, :].rearrange("(a p) f -> p a f", p=128), ytk, reads=[y_d.T], writes=[ytk])
        for fc in range(32):
            Pt = k.P[fc % 4]
            for tb in range(4):
                k.transpose(Pt, Pt[:, tb * 128:(tb + 1) * 128], ytk[:, tb, fc * 128:(fc + 1) * 128], C["ident_f"][:], [ytk, C["ident_f"]])
            zt = zsb[fc % 4]
            k.dma("sp", zt[:], zs_d.ap[fc * 128:(fc + 1) * 128, tcols], zt, reads=[zs_d.T], writes=[zt])
            k.op("dve", lambda e: e.tensor_tensor(out=yg[:, fc, :], in0=Pt[:], in1=zt[:], op=ALU.mult),
                 reads=[Pt, zt], writes=[yg])
        rstd = rms_stats(k, C, nb, yg, 32, k.P[7], 4096)
        for fc in range(32):
            k.op("dve", lambda e: e.scalar_tensor_tensor(out=fo[:, fc, :], in0=yg[:, fc, :], scalar=ng[:, fc:fc + 1], in1=rstd[:],
                                                         op0=ALU.mult, op1=ALU.mult), reads=[yg, ng, rstd], writes=[fo])
        k.dma("sp", feat.ap[:, tcols].rearrange("(c p) t -> p c t", p=128), fo[:], fo, reads=[fo], writes=[feat.T])


WEIGHT_SPECS = {
    "ln_mix_pre": [4, 128, DC], "ln_mix_post": [4, 128, DC], "ln_mem": [4, 128, DC],
    "ln_xa_pre": [4, 128, DC], "ln_xa_post": [4, 128, DC], "ln_ffn_pre": [4, 128, DC], "ln_ffn_post": [4, 128, DC],
    "xa_wq": [4, D, 512], "xa_wkv": [4, D, 1024], "xa_wo": [4, 512, D],
    "ffn_w_in": [4, D, 2 * FFN], "ffn_conv_w": [4, 128, 88, 3], "ffn_conv_b": [4, 128, 88], "ffn_w_out": [4, FFN, D],
    "sg_w_in": [1, D, 8192], "sg_v_norm_g": [1, 128, 32], "sg_v_norm_b": [1, 128, 32],
    "sg_w_spatial": [1, 128, 16, 128], "sg_b_spatial": [1, 16, 128], "sg_w_out": [1, 4096, D],
    "sb_w_qkv": [1, D, 3 * D], "sb_w_out": [1, D, D],
    "ssd_w_in": [2, D, 10304], "ssd_conv_w": [2, 128, 48, 4], "ssd_conv_b": [2, 128, 48], "ssd_dt_bias": [2, 64, 1],
    "ssd_a_log": [2, 64, 1], "ssd_d": [2, 64], "ssd_norm": [2, 128, 32], "ssd_w_out": [2, 4096, D],
}


class DT:
    def __init__(self, nc, name, shape, dtype, kind):
        self.h = nc.dram_tensor(name, list(shape), dtype, kind=kind)
        self.ap = self.h.ap()
        self.T = T(self.h)
        self.shape = shape
        self._tt = {}

    def tt(self, tg):
        if tg not in self._tt:
            self._tt[tg] = T(self.h)
        return self._tt[tg]

    def __getitem__(self, key):
        return self.ap[key]


LAST_KB = None

NEED = {
    "ffn": ["ln_ffn_pre", "ln_ffn_post", "ffn_w_in", "ffn_conv_w", "ffn_conv_b", "ffn_w_out"],
    "xa": ["ln_mem", "ln_xa_pre", "ln_xa_post", "xa_wq", "xa_wkv", "xa_wo"],
    "mix0": ["ln_mix_pre", "ln_mix_post", "ssd_w_in", "ssd_conv_w", "ssd_conv_b", "ssd_dt_bias", "ssd_a_log", "ssd_d",
             "ssd_norm", "ssd_w_out"],
    "mix1": ["ln_mix_pre", "ln_mix_post", "sg_w_in", "sg_v_norm_g", "sg_v_norm_b", "sg_w_spatial", "sg_b_spatial", "sg_w_out"],
    "mix2": ["ln_mix_pre", "ln_mix_post", "sb_w_qkv", "sb_w_out"],
}


def needed_weights(plan):
    out = []
    for kind, li in plan:
        key = kind if kind != "mix" else f"mix{li % 3}"
        for n in NEED[key]:
            if n not in out:
                out.append(n)
    return out


def build_program(L, plan, wnames):
    global LAST_KB
    NT = L // TG
    nc = bass.Bass("TRN2", target_bir_lowering=False)
    k = KB(nc)
    LAST_KB = k
    xin = DT(nc, "xT", [D, L], F32, "ExternalInput")
    memT = DT(nc, "memT", [D, 256], F32, "ExternalInput")
    xout = DT(nc, "outT", [D, L], F32, "ExternalOutput")
    xr = DT(nc, "xr", [D, L], F32, "Internal")
    featA = DT(nc, "featA", [4096, L], BF16, "Internal")
    featB = DT(nc, "featB", [4096, L], BF16, "Internal")
    featV = DT(nc, "featV", [L, 2048], BF16, "Internal")
    S = {"featA": featA, "featB": featB, "featV": featV}
    if any(kind == "mix" and li % 3 == 0 for kind, li in plan):
        S["featC"] = DT(nc, "featC", [2048, L], BF16, "Internal")
        S["ssdX"] = DT(nc, "ssdX", [L, 4096], F32, "Internal")
        S["ssdY"] = DT(nc, "ssdY", [L, 4096], F32, "Internal")
        S["dtda"] = DT(nc, "dtda", [L, 128], F32, "Internal")
    W = {"memT": memT}
    for n in wnames:
        W[n] = DT(nc, n, WEIGHT_SPECS[n], F32, "ExternalInput")
    C = make_consts(k)
    base = k.sb_off
    for tg in range(NT):
        k.dma("sp", xout.ap[:, tg * TG:(tg + 1) * TG], xin.ap[:, tg * TG:(tg + 1) * TG], xout.tt(tg),
              reads=[xin.T], writes=[xout.tt(tg)])
    cur = xout
    for i, (kind, li) in enumerate(plan):
        dst = xout
        if kind == "ffn":
            emit_ffn(k, C, NT, cur, dst, W, li, base)
        elif kind == "xa":
            emit_xa(k, C, NT, cur, dst, W, li, base)
        elif kind == "mix" and li % 3 == 0:
            emit_ssd_front(k, C, NT, cur, W, li, S, base)
            emit_outproj(k, C, NT, featB, 32, W["ssd_w_out"][li // 3], W["ln_mix_post"][li], cur, dst, base)
        elif kind == "mix" and li % 3 == 2:
            emit_sb_front(k, C, NT, cur, W, li, featA, featV, featB, base)
            emit_outproj(k, C, NT, featB, 16, W["sb_w_out"][li // 3], W["ln_mix_post"][li], cur, dst, base)
        elif kind == "mix" and li % 3 == 1:
            emit_sgu_front(k, C, NT, cur, W, li, featA, base)
            emit_outproj(k, C, NT, featA, 32, W["sg_w_out"][li // 3], W["ln_mix_post"][li], cur, dst, base)
        else:
            raise ValueError(kind)
        cur = dst
    k.stage(None, None)
    return nc


def host_layout(inputs, wnames):
    out = {}
    for n in wnames:
        a = np.asarray(inputs[n])
        if n.startswith("ln_"):
            a = a.reshape(4, DC, 128).transpose(0, 2, 1)
        elif n == "ffn_conv_w":
            a = a.reshape(4, 3, 88, 128).transpose(0, 3, 2, 1)
        elif n == "ffn_conv_b":
            a = a.reshape(4, 88, 128).transpose(0, 2, 1)
        elif n in ("sg_v_norm_g", "sg_v_norm_b"):
            a = a.reshape(1, 32, 128).transpose(0, 2, 1)
        elif n == "sg_w_spatial":
            a = a.transpose(0, 3, 1, 2)
        elif n == "ssd_conv_w":
            a = a.reshape(2, 4, 48, 128).transpose(0, 3, 2, 1)
        elif n == "ssd_conv_b":
            a = a.reshape(2, 48, 128).transpose(0, 2, 1)
        elif n in ("ssd_dt_bias", "ssd_a_log"):
            a = a.reshape(2, 64, 1)
        elif n == "ssd_norm":
            a = a.reshape(2, 32, 128).transpose(0, 2, 1)
        out[n] = np.ascontiguousarray(a)
    return out


FULL_PLAN = []
for _i in range(4):
    FULL_PLAN += [("mix", _i), ("xa", _i), ("ffn", _i)]


def kernel(**inputs):
    L = 4096
    plan = FULL_PLAN
    wn = needed_weights(plan)
    hl = host_layout(inputs, wn)
    nc = build_program(L, plan, wn)
    x = np.asarray(inputs["x"])
    mem = np.asarray(inputs["mem"])
    in_maps = []
    for b in range(8):
        m = {"xT": np.ascontiguousarray(x[b].T), "memT": np.ascontiguousarray(mem[b].T)}
        m.update(hl)
        in_maps.append(m)
    res = run_bass_kernel_spmd(nc, in_maps, core_ids=list(range(8)))
    out = np.stack([np.ascontiguousarray(res.results[b]["outT"].T) for b in range(8)], axis=0)
    return out.astype(np.float32)
```

```python
import numpy as np
import concourse.bass as bass
import concourse.mybir as mybir
from concourse.bass_utils import run_bass_kernel_spmd

F32 = mybir.dt.float32
BF16 = mybir.dt.bfloat16
AF = mybir.ActivationFunctionType
ALU = mybir.AluOpType

D = 2048
DC = 16
TG = 512
EPS = 1e-6
EPOCH = 30000
SBUF_BASE = 16384
SBUF_CAP = 229000
FFN = 5632
SAME_ENG_SYNC = True


class Dep:
    __slots__ = ("lw", "rd")

    def __init__(self):
        self.lw = None
        self.rd = {}


class T:
    def __init__(self, h, dep=None):
        self.h = h
        self.dep = dep or Dep()
        self.sem = None
        self.semcnt = 0

    def __getitem__(self, k):
        return self.h[k]


class KB:
    def __init__(self, nc):
        self.nc = nc
        self.E = {"pe": nc.tensor, "act": nc.scalar, "dve": nc.vector, "pool": nc.gpsimd, "sp": nc.sync}
        self.cur = {}
        self.seen = {e: {} for e in self.E}
        self.nsem = 0
        self.dmatiles = []
        self.free_sems = {}
        self.sb_off = SBUF_BASE
        self.nalloc = 0
        self.P = [T(nc.alloc_psum_tensor(f"bank{i}", [128, 512], F32)) for i in range(8)]

    def _newsem(self, name):
        self.nsem += 1
        return self.nc.alloc_semaphore(f"{name}_{self.nsem}")

    def _tick(self, eng):
        c = self.cur.get(eng)
        if c is None or c[1] >= EPOCH:
            c = [self._newsem(eng), 0]
            self.cur[eng] = c
        c[1] += 1
        return (c[0], c[1], eng)

    def _wait(self, eng, deps, raw_ts=()):
        for d in deps:
            if d is None:
                continue
            sem, val, src = d
            if src == eng and (eng == "pe" or not SAME_ENG_SYNC):
                continue
            if self.seen[eng].get(sem, 0) >= val:
                continue
            self.E[eng].wait_ge(sem, val)
            self.seen[eng][sem] = val

    def _deps(self, reads, writes):
        deps, raw = [], []
        for t in reads:
            deps.append(t.dep.lw)
            raw.append(t.dep.lw)
        for t in writes:
            deps.append(t.dep.lw)
            deps.extend(t.dep.rd.values())
        return deps, raw

    def _commit(self, d, reads, writes):
        for t in writes:
            t.dep.lw = d
            t.dep.rd = {}
        for t in reads:
            t.dep.rd[d[0]] = d

    def op(self, eng, fn, reads=(), writes=()):
        deps, raw = self._deps(reads, writes)
        self._wait(eng, deps, raw)
        ins = fn(self.E[eng])
        d = self._tick(eng)
        ins.then_inc(d[0], 1)
        self._commit(d, reads, writes)
        return ins

    def mmg(self, P, out_ap, pairs, reads):
        deps, raw = self._deps(reads, [P])
        self._wait("pe", deps, raw)
        n = len(pairs)
        ins = None
        for i, (l, r) in enumerate(pairs):
            ins = self.nc.tensor.matmul(out_ap, l, r, start=(i == 0), stop=(i == n - 1))
        d = self._tick("pe")
        ins.then_inc(d[0], 1)
        self._commit(d, reads, [P])

    def transpose(self, P, out_ap, in_ap, ident_ap, reads):
        deps, raw = self._deps(reads, [P])
        self._wait("pe", deps, raw)
        ins = self.nc.tensor.transpose(out_ap, in_ap, ident_ap)
        d = self._tick("pe")
        ins.then_inc(d[0], 1)
        self._commit(d, reads, [P])

    def dma(self, q, out_ap, in_ap, semT, reads=(), writes=(), accum=False):
        deps, raw = self._deps(reads, writes)
        self._wait(q, deps, raw)
        if semT.sem is None or semT.semcnt >= EPOCH:
            fl = self.free_sems.setdefault(q, [])
            if fl and fl[0][1] < EPOCH - 2000:
                semT.sem, semT.semcnt = fl.pop(0)
            else:
                semT.sem = self._newsem("dma" + q)
                semT.semcnt = 0
            semT.semq = q
            self.dmatiles.append((semT, semT.sem))
        assert semT.semq == q
        semT.semcnt += 16
        if accum:
            self.E[q].dma_start(out=out_ap, in_=in_ap, accum_op=ALU.add).then_inc(semT.sem, 16)
        else:
            self.E[q].dma_start(out=out_ap, in_=in_ap).then_inc(semT.sem, 16)
        d = (semT.sem, semT.semcnt, "dma")
        self._commit(d, reads, writes)

    def barrier(self):
        deps = [(c[0], c[1], e) for e, c in self.cur.items()]
        latest = {}
        for t, sem in self.dmatiles:
            if t.sem is sem:
                latest[id(sem)] = (sem, t.semcnt, "dma")
        deps += list(latest.values())
        for e in self.E:
            self._wait(e, deps)

    def stage(self, base=None, name=None):
        self.barrier()
        if getattr(self, "_scope", None) is not None:
            self.nc.leave_named_scope(self._scope[0], self._scope[1], False)
            self._scope = None
        if name is not None:
            self._nscope = getattr(self, "_nscope", 0) + 1
            nm = f"{self._nscope:02d}_{name}"
            sid, _ = self.nc.enter_named_scope(nm, False)
            self._scope = (nm, sid)
        for t, sem in self.dmatiles:
            if t.sem is sem:
                self.free_sems[t.semq].append((sem, t.semcnt))
                t.sem = None
        self.dmatiles = []
        if base is not None:
            self.sb_off = base

    def sb(self, shape, dtype, name="t"):
        nbytes = int(np.prod(shape[1:])) * (2 if dtype == BF16 else 4)
        nbytes = (nbytes + 31) // 32 * 32
        off = self.sb_off
        self.sb_off += nbytes
        assert self.sb_off <= SBUF_CAP, f"SBUF overflow {self.sb_off}"
        self.nalloc += 1
        return T(self.nc.alloc_sbuf_tensor_at(f"{name}_{self.nalloc}", list(shape), dtype, offset=off))

    def sbs(self, n, shape, dtype, name="t"):
        return [self.sb(shape, dtype, name) for _ in range(n)]


def make_consts(k):
    C = {}
    d = k.sb([128, 128], F32, "iota")
    k.op("pool", lambda e: e.iota(d[:], [[1, 128]], base=0, channel_multiplier=-1,
                                  allow_small_or_imprecise_dtypes=True), writes=[d])

    def cmp(name, op, dtype, val=1.0, thr=0.0):
        t = k.sb([128, 128], dtype, name)
        k.op("dve", lambda e: e.tensor_scalar(out=t[:], in0=d[:], scalar1=thr, scalar2=val, op0=op, op1=ALU.mult),
             reads=[d], writes=[t])
        C[name] = t
    cmp("ident_bf", ALU.is_equal, BF16)
    cmp("ident_f", ALU.is_equal, F32)
    cmp("le_f", ALU.is_ge, F32)
    cmp("le_bf", ALU.is_ge, BF16)
    cmp("lt_f", ALU.is_gt, F32)
    cmp("gt_f", ALU.is_lt, F32)
    cmp("neg_ge_bf", ALU.is_le, BF16, val=-1.0)
    ones = k.sb([128, 128], BF16, "ones")
    k.op("dve", lambda e: e.memset(ones[:], 1.0), writes=[ones])
    C["ones_bf"] = ones
    onesf = k.sb([128, 128], F32, "onesf")
    k.op("dve", lambda e: e.memset(onesf[:], 1.0), writes=[onesf])
    C["ones_f"] = onesf
    return C


def load_small(k, dram_ap, shape, dtype=F32, q="sp"):
    t = k.sb(shape, dtype, "small")
    k.dma(q, t[:], dram_ap, t, writes=[t])
    return t


class NormBufs:
    def __init__(self, k):
        self.sq = k.sbs(2, [128, 4, TG], BF16, "sq")
        self.lnt = k.sb([128, TG], F32, "lnt")
        self.rstd = k.sb([128, TG], F32, "rstd")


def rms_stats(k, C, nb, src, nchunks, Pn, dim, w=TG, eps=EPS):
    allp, rd = [], []
    for q in range(nchunks // 4):
        sq = nb.sq[q % 2]
        k.op("act", lambda e: e.activation(out=sq[:, :, 0:w], in_=src[:, 4 * q:4 * q + 4, 0:w], func=AF.Square),
             reads=[src], writes=[sq])
        allp = [(C["ones_bf"][:], sq[:, j, 0:w]) for j in range(4)]
        emit_partial_group(k, Pn, allp, [sq], q == 0, q == nchunks // 4 - 1, out_ap=Pn[:, 0:w])
    k.op("act", lambda e: e.activation(out=nb.lnt[:, 0:w], in_=Pn[:, 0:w], func=AF.Ln, scale=1.0 / dim, bias=eps),
         reads=[Pn], writes=[nb.lnt])
    k.op("act", lambda e: e.activation(out=nb.rstd[:, 0:w], in_=nb.lnt[:, 0:w], func=AF.Exp, scale=-0.5),
         reads=[nb.lnt], writes=[nb.rstd])
    return nb.rstd


def pre_norm(k, C, nb, xs, gcol, hn, Pn, w=TG):
    rstd = rms_stats(k, C, nb, xs, DC, Pn, D, w)
    for c in range(DC):
        k.op("dve", lambda e: e.scalar_tensor_tensor(out=hn[c][:, 0:w], in0=xs[:, c, 0:w], scalar=gcol[:, c:c + 1],
                                                     in1=rstd[:, 0:w], op0=ALU.mult, op1=ALU.mult),
             reads=[xs, rstd, gcol], writes=[hn[c]])


def post_norm_accum(k, C, nb, y, gcol, Pn, x_dst, tg, ws=None, after=1):
    rstd = rms_stats(k, C, nb, y, DC, Pn, D)
    for c in range(DC):
        k.op("dve", lambda e: e.scalar_tensor_tensor(out=y[:, c, :], in0=y[:, c, :], scalar=gcol[:, c:c + 1],
                                                     in1=rstd[:], op0=ALU.mult, op1=ALU.mult),
             reads=[y, rstd, gcol], writes=[y])

    def store():
        k.dma("pool", x_tile_ap(x_dst, tg), y[:], y, reads=[y], writes=[x_dst.tt(tg)], accum=True)
    if ws is None:
        store()
    else:
        ws.defer(store, after)


class WStream:
    def __init__(self, k, nbuf, kc, ncols):
        self.k = k
        self.bufs = k.sbs(nbuf, [128, kc, ncols], BF16, "w")
        self.i = 0
        self.pending = []

    def defer(self, fn, after=4):
        self.pending.append([after, fn])

    def flush(self):
        for _, fn in self.pending:
            fn()
        self.pending = []

    def load(self, W, k0, kc, n0, ncols):
        t = self.bufs[self.i % len(self.bufs)]
        self.i += 1
        src = W[k0:k0 + kc * 128, n0:n0 + ncols].rearrange("(c p) n -> p c n", p=128)
        self.k.dma("pool", t[:, 0:kc, 0:ncols], src, t, writes=[t])
        for p in list(self.pending):
            p[0] -= 1
            if p[0] <= 0:
                self.pending.remove(p)
                p[1]()
        return t


def run_lockstep(gens, width):
    it = iter(gens)
    active = []
    done = False
    while True:
        while not done and len(active) < width:
            g = next(it, None)
            if g is None:
                done = True
                break
            active.append(g)
        if not active:
            break
        for g in list(active):
            try:
                next(g)
            except StopIteration:
                active.remove(g)


def x_tile_ap(xd, tg):
    return xd[:, tg * TG:(tg + 1) * TG].rearrange("(c p) t -> p c t", p=128)


def emit_ffn(k, C, NT, x_src, x_dst, W, li, base):
    k.stage(base, "ffn")
    g_pre = load_small(k, W["ln_ffn_pre"][li], [128, DC])
    g_post = load_small(k, W["ln_ffn_post"][li], [128, DC])
    cw = load_small(k, W["ffn_conv_w"][li], [128, 88, 3])
    cb = load_small(k, W["ffn_conv_b"][li], [128, 88])
    halo = k.sb([128, 88, 2], F32, "halo")
    k.op("pool", lambda e: e.memset(halo[:], 0.0), writes=[halo])
    nb = NormBufs(k)
    xs = k.sb([128, DC, TG], F32, "xs")
    hn = k.sbs(DC, [128, TG], BF16, "hn")
    act = k.sbs(44, [128, TG], BF16, "act")
    y = k.sb([128, DC, TG], F32, "y")
    xpad = k.sbs(4, [128, TG + 2], F32, "xpad")
    cv = k.sbs(4, [128, TG], F32, "cv")
    ws = WStream(k, 4, 16, 256)
    w_in = W["ffn_w_in"][li]
    w_out = W["ffn_w_out"][li]
    Pn = k.P[7]
    def prefetch(tg):
        k.dma("sp", xs[:], x_tile_ap(x_src, tg), xs, reads=[x_src.tt(tg)], writes=[xs])
        pre_norm(k, C, nb, xs, g_pre, hn, Pn)

    prefetch(0)
    for tg in range(NT):
        wcache = {}

        def pair(j):
            J, jj = j // 2, j % 2
            if jj == 0:
                wcache[J] = (ws.load(w_in, 0, 16, J * 256, 256), ws.load(w_in, 0, 16, FFN + J * 256, 256))
            wts = wcache[J]
            chs = (j, j + 44)
            Pbs = (k.P[(2 * j) % 4], k.P[(2 * j + 1) % 4])
            xps = (xpad[(2 * j) % 4], xpad[(2 * j + 1) % 4])
            os_ = (cv[(2 * j) % 4], cv[(2 * j + 1) % 4])
            for hf in range(2):
                k.mmg(Pbs[hf], Pbs[hf][:], [(wts[hf][:, c, jj * 128:(jj + 1) * 128], hn[c][:]) for c in range(DC)], [wts[hf]] + hn)
            yield
            for hf in range(2):
                xp, ch, Pb = xps[hf], chs[hf], Pbs[hf]
                k.op("act", lambda e: e.activation(out=xp[:, 0:2], in_=halo[:, ch, :], func=AF.Copy), reads=[halo], writes=[xp])
                k.op("act", lambda e: e.activation(out=xp[:, 2:TG + 2], in_=Pb[:], func=AF.Copy), reads=[Pb], writes=[xp])
                k.op("act", lambda e: e.activation(out=halo[:, ch, :], in_=xp[:, TG:TG + 2], func=AF.Copy), reads=[xp], writes=[halo])
            yield
            for hf in range(2):
                xp, ch, o = xps[hf], chs[hf], os_[hf]
                k.op("dve", lambda e: e.tensor_scalar(out=o[:], in0=xp[:, 2:TG + 2], scalar1=cw[:, ch, 2:3],
                                                      scalar2=cb[:, ch:ch + 1], op0=ALU.mult, op1=ALU.add),
                     reads=[xp, cw, cb], writes=[o])
            yield
            for tap in (1, 0):
                for hf in range(2):
                    xp, ch, o = xps[hf], chs[hf], os_[hf]
                    k.op("dve", lambda e: e.scalar_tensor_tensor(out=o[:], in0=xp[:, tap:tap + TG],
                                                                 scalar=cw[:, ch, tap:tap + 1], in1=o[:],
                                                                 op0=ALU.mult, op1=ALU.add),
                         reads=[xp, cw, o], writes=[o])
                yield
            og, ou = os_
            k.op("act", lambda e: e.activation(out=og[:], in_=og[:], func=AF.Gelu_apprx_tanh), reads=[og], writes=[og])
            yield
            k.op("dve", lambda e: e.tensor_tensor(out=act[j][:], in0=og[:], in1=ou[:], op=ALU.mult),
                 reads=[og, ou], writes=[act[j]])

        run_lockstep((pair(j) for j in range(44)), 2)
        if tg + 1 < NT:
            prefetch(tg + 1)
        for nbk in range(4):
            for kp in range(4):
                wts = [ws.load(w_out, kp * 11 * 128, 11, nbk * 512 + h * 256, 256) for h in range(2)]
                for jj in range(4):
                    Pb = k.P[jj]
                    wt = wts[jj // 2]
                    pairs = [(wt[:, c, (jj % 2) * 128:(jj % 2 + 1) * 128], act[kp * 11 + c][:]) for c in range(11)]
                    emit_partial_group(k, Pb, pairs, [wt] + act[kp * 11:kp * 11 + 11], kp == 0, kp == 3)
            for jj in range(4):
                c = nbk * 4 + jj
                k.op("act", lambda e: e.activation(out=y[:, c, :], in_=k.P[jj][:], func=AF.Copy),
                     reads=[k.P[jj]], writes=[y])
        post_norm_accum(k, C, nb, y, g_post, Pn, x_dst, tg, ws, after=4)
    ws.flush()


def emit_partial_group(k, P, pairs, reads, first, last, out_ap=None):
    if out_ap is None:
        out_ap = P[:]
    deps, raw = k._deps(reads, [P] if first else [])
    k._wait("pe", deps, raw)
    n = len(pairs)
    ins = None
    for i, (l, r) in enumerate(pairs):
        ins = k.nc.tensor.matmul(out_ap, l, r, start=(first and i == 0), stop=(last and i == n - 1))
    d = k._tick("pe")
    ins.then_inc(d[0], 1)
    k._commit(d, reads, [P] if last else [])
    if not last:
        pass


def emit_xa(k, C, NT, x_src, x_dst, W, li, base):
    k.stage(base, "xa")
    g_mem = load_small(k, W["ln_mem"][li], [128, DC])
    g_pre = load_small(k, W["ln_xa_pre"][li], [128, DC])
    g_post = load_small(k, W["ln_xa_post"][li], [128, DC])
    nb = NormBufs(k)
    xs = k.sb([128, DC, TG], F32, "xs")
    hn = k.sbs(DC, [128, TG], BF16, "hn")
    y = k.sb([128, DC, TG], F32, "y")
    tmp = k.sbs(2, [128, TG], F32, "tmp")
    ws = WStream(k, 3, 16, 256)
    kT = k.sbs(4, [128, 256], BF16, "kT")
    v = k.sbs(2, [128, 512], BF16, "v")
    qT = k.sbs(4, [128, TG], BF16, "qT")
    E = k.sbs(8, [128, TG], BF16, "E")
    oT = k.sbs(4, [128, TG], BF16, "oT")
    rec = k.sbs(2, [128, TG], F32, "rec")
    Pn = k.P[7]
    wq, wkv, wo = W["xa_wq"][li], W["xa_wkv"][li], W["xa_wo"][li]
    memT = W["memT"]
    k.dma("sp", xs[:, :, 0:256], memT.ap.rearrange("(c p) t -> p c t", p=128), xs, reads=[memT.T], writes=[xs])
    pre_norm(k, C, nb, xs, g_mem, hn, Pn, w=256)
    for t4 in range(4):
        wt = ws.load(wkv, 0, 16, t4 * 256, 256)
        if t4 < 2:
            for jj in range(2):
                h = t4 * 2 + jj
                Pb = k.P[h % 4]
                k.mmg(Pb, Pb[:, 0:256], [(wt[:, c, jj * 128:(jj + 1) * 128], hn[c][:, 0:256]) for c in range(DC)], [wt] + hn)
                k.op("act", lambda e: e.activation(out=kT[h][:], in_=Pb[:, 0:256], func=AF.Copy), reads=[Pb], writes=[kT[h]])
        else:
            half = t4 - 2
            for mc in range(2):
                Pb = k.P[4 + mc]
                k.mmg(Pb, Pb[:, half * 256:(half + 1) * 256],
                      [(hn[c][:, mc * 128:(mc + 1) * 128], wt[:, c, :]) for c in range(DC)], [wt] + hn)
                k.op("act", lambda e: e.activation(out=v[mc][:, half * 256:(half + 1) * 256],
                                                   in_=Pb[:, half * 256:(half + 1) * 256], func=AF.Copy),
                     reads=[Pb], writes=[v[mc]])
    scale = 128.0 ** -0.5
    def prefetch(tg):
        k.dma("sp", xs[:], x_tile_ap(x_src, tg), xs, reads=[x_src.tt(tg)], writes=[xs])
        pre_norm(k, C, nb, xs, g_pre, hn, Pn)

    prefetch(0)
    for tg in range(NT):
        for t2 in range(2):
            wt = ws.load(wq, 0, 16, t2 * 256, 256)
            for jj in range(2):
                h = t2 * 2 + jj
                Pb = k.P[h % 4]
                k.mmg(Pb, Pb[:], [(wt[:, c, jj * 128:(jj + 1) * 128], hn[c][:]) for c in range(DC)], [wt] + hn)
                k.op("act", lambda e: e.activation(out=qT[h][:], in_=Pb[:], func=AF.Copy), reads=[Pb], writes=[qT[h]])
        if tg + 1 < NT:
            prefetch(tg + 1)
        for h in range(4):
            for mc in range(2):
                Pb = k.P[(2 * h + mc) % 4]
                k.mmg(Pb, Pb[:], [(kT[h][:, mc * 128:(mc + 1) * 128], qT[h][:])], [kT[h], qT[h]])
                Eh = E[2 * h + mc]
                k.op("act", lambda e: e.activation(out=Eh[:], in_=Pb[:], func=AF.Exp, scale=scale), reads=[Pb], writes=[Eh])
            Pd = k.P[4 + h % 2]
            k.mmg(Pd, Pd[:], [(C["ones_bf"][:], E[2 * h + mc][:]) for mc in range(2)], [E[2 * h], E[2 * h + 1]])
            rc = rec[h % 2]
            k.op("dve", lambda e: e.reciprocal(out=rc[:], in_=Pd[:]), reads=[Pd], writes=[rc])
            Po = k.P[6]
            k.mmg(Po, Po[:], [(v[mc][:, h * 128:(h + 1) * 128], E[2 * h + mc][:]) for mc in range(2)],
                  [v[0], v[1], E[2 * h], E[2 * h + 1]])
            k.op("dve", lambda e: e.tensor_tensor(out=oT[h][:], in0=Po[:], in1=rc[:], op=ALU.mult),
                 reads=[Po, rc], writes=[oT[h]])
        for t8 in range(8):
            wt = ws.load(wo, 0, 4, t8 * 256, 256)
            for jj in range(2):
                c = t8 * 2 + jj
                Pb = k.P[c % 4]
                k.mmg(Pb, Pb[:], [(wt[:, h, jj * 128:(jj + 1) * 128], oT[h][:]) for h in range(4)], [wt] + oT)
                k.op("act", lambda e: e.activation(out=y[:, c, :], in_=Pb[:], func=AF.Copy), reads=[Pb], writes=[y])
        post_norm_accum(k, C, nb, y, g_post, Pn, x_dst, tg, ws, after=2)
    ws.flush()


def bcast_mid(ap, n):
    pat = [list(p) for p in ap.ap]
    assert len(pat) == 2
    return bass.AP(ap.tensor, ap.offset, [pat[0], [0, n], pat[1]])


def bcast_last(ap, n):
    pat = [list(p) for p in ap.ap]
    assert len(pat) == 2
    return bass.AP(ap.tensor, ap.offset, [pat[0], pat[1], [0, n]])


def emit_outproj(k, C, NT, feat, KC, w_out, g_post_ap, x_src, x_dst, base):
    k.stage(base, "outproj")
    g_post = load_small(k, g_post_ap, [128, DC])
    nb = NormBufs(k)
    y = k.sb([128, DC, TG], F32, "y")
    f = k.sbs(2, [128, KC, TG], BF16, "feat")
    ws = WStream(k, 3, 16, 256)
    Pn = k.P[7]
    nkh = (KC + 15) // 16
    for tg in range(NT):
        ft = f[tg % 2]
        k.dma("sp", ft[:], feat.ap[0:KC * 128, tg * TG:(tg + 1) * TG].rearrange("(c p) t -> p c t", p=128), ft,
              reads=[feat.T], writes=[ft])
        for ct in range(8):
            for kh in range(nkh):
                kc = min(16, KC - kh * 16)
                wt = ws.load(w_out, kh * 2048, kc, ct * 256, 256)
                for jj in range(2):
                    Pb = k.P[(2 * ct + jj) % 4]
                    pairs = [(wt[:, c, jj * 128:(jj + 1) * 128], ft[:, kh * 16 + c, :]) for c in range(kc)]
                    emit_partial_group(k, Pb, pairs, [wt, ft], kh == 0, kh == nkh - 1)
            for jj in range(2):
                c = 2 * ct + jj
                Pb = k.P[c % 4]
                k.op("act", lambda e: e.activation(out=y[:, c, :], in_=Pb[:], func=AF.Copy), reads=[Pb], writes=[y])
        post_norm_accum(k, C, nb, y, g_post, Pn, x_dst, tg, ws)
    ws.flush()


def emit_sgu_front(k, C, NT, x_src, W, li, feat, base):
    k.stage(base, "sgu")
    j = li // 3
    g_pre = load_small(k, W["ln_mix_pre"][li], [128, DC])
    vg = load_small(k, W["sg_v_norm_g"][j], [128, 32])
    vb = load_small(k, W["sg_v_norm_b"][j], [128, 32])
    wsp_f = load_small(k, W["sg_w_spatial"][j], [128, 16, 128])
    bias_bc = k.sb([128, 16, 128], F32, "bias_bc")
    k.dma("sp", bias_bc[:], W["sg_b_spatial"].ap[j].rearrange("g t -> (g t)").partition_broadcast(128)
          .rearrange("p (g t) -> p g t", g=16), bias_bc, writes=[bias_bc])
    k.op("dve", lambda e: e.tensor_tensor(out=wsp_f[:], in0=wsp_f[:], in1=bcast_mid(C["le_f"][:], 16), op=ALU.mult),
         reads=[wsp_f, C["le_f"]], writes=[wsp_f])
    wsp_b = k.sb([128, 16, 128], BF16, "wsp_b")
    k.op("dve", lambda e: e.tensor_copy(out=wsp_b[:], in_=wsp_f[:]), reads=[wsp_f], writes=[wsp_b])
    rs_bc = k.sb([128, 16, 128], F32, "rs_bc")
    for q in range(4):
        Pb = k.P[q]
        k.mmg(Pb, Pb[:], [(C["ones_bf"][:], wsp_b[:, 4 * q:4 * q + 4, :])], [wsp_b])
        k.op("act", lambda e: e.activation(out=rs_bc[:, 4 * q:4 * q + 4, :], in_=Pb[:], func=AF.Copy), reads=[Pb], writes=[rs_bc])
    nb = NormBufs(k)
    xs = k.sb([128, DC, TG], F32, "xs")
    hn = k.sbs(DC, [128, TG], BF16, "hn")
    uT = k.sbs(32, [128, TG], BF16, "uT")
    vraw = k.sbs(4, [128, 4096], BF16, "vraw")
    junk = k.sbs(2, [128, 256], BF16, "junk")
    s1 = k.sbs(4, [128, 16], F32, "s1")
    s2 = k.sbs(4, [128, 16], F32, "s2")
    st = k.sbs(4, [128, 8], F32, "st")
    wsn = k.sbs(4, [128, 16, 128], BF16, "wsn")
    mrep = k.sbs(4, [128, 128], BF16, "mrep")
    stsem = T(None)
    t1 = k.sbs(2, [128, TG], F32, "t1")
    ws = WStream(k, 3, 16, 256)
    w_in = W["sg_w_in"][j]
    Pn = k.P[7]
    for tg in range(NT):
        k.dma("sp", xs[:], x_tile_ap(x_src, tg), xs, reads=[x_src.tt(tg)], writes=[xs])
        pre_norm(k, C, nb, xs, g_pre, hn, Pn)
        for ct in range(16):
            wt = ws.load(w_in, 0, 16, ct * 256, 256)
            for jj in range(2):
                dc = 2 * ct + jj
                Pb = k.P[dc % 4]
                k.mmg(Pb, Pb[:], [(wt[:, c, jj * 128:(jj + 1) * 128], hn[c][:]) for c in range(DC)], [wt] + hn)
                k.op("act", lambda e: e.activation(out=uT[dc][:], in_=Pb[:], func=AF.Gelu_apprx_tanh), reads=[Pb], writes=[uT[dc]])
        for ct in range(16):
            wt = ws.load(w_in, 0, 16, 4096 + ct * 256, 256)
            for tb in range(4):
                Pb = k.P[tb]
                k.mmg(Pb, Pb[:, 0:256], [(hn[c][:, tb * 128:(tb + 1) * 128], wt[:, c, :]) for c in range(DC)], [wt] + hn)
                k.op("act", lambda e: e.activation(out=vraw[tb][:, ct * 256:(ct + 1) * 256], in_=Pb[:, 0:256],
                                                   func=AF.Gelu_apprx_tanh, accum_out=s1[tb][:, ct:ct + 1]),
                     reads=[Pb], writes=[vraw[tb], s1[tb]])
                jk = junk[tb % 2]
                k.op("act", lambda e: e.activation(out=jk[:], in_=vraw[tb][:, ct * 256:(ct + 1) * 256],
                                                   func=AF.Square, accum_out=s2[tb][:, ct:ct + 1]),
                     reads=[vraw[tb]], writes=[jk, s2[tb]])
        for tb in range(4):
            S = st[tb]
            k.op("dve", lambda e: e.reduce_sum(out=S[:, 0:1], in_=s1[tb][:], axis=mybir.AxisListType.X), reads=[s1[tb]], writes=[S])
            k.op("dve", lambda e: e.reduce_sum(out=S[:, 1:2], in_=s2[tb][:], axis=mybir.AxisListType.X), reads=[s2[tb]], writes=[S])
            k.op("dve", lambda e: e.tensor_scalar(out=S[:, 0:2], in0=S[:, 0:2], scalar1=1.0 / 4096, scalar2=None, op0=ALU.mult),
                 reads=[S], writes=[S])
            k.op("dve", lambda e: e.tensor_tensor(out=S[:, 2:3], in0=S[:, 0:1], in1=S[:, 0:1], op=ALU.mult), reads=[S], writes=[S])
            k.op("dve", lambda e: e.tensor_tensor(out=S[:, 2:3], in0=S[:, 1:2], in1=S[:, 2:3], op=ALU.subtract), reads=[S], writes=[S])
            k.op("act", lambda e: e.activation(out=S[:, 3:4], in_=S[:, 2:3], func=AF.Ln, bias=EPS), reads=[S], writes=[S])
            k.op("act", lambda e: e.activation(out=S[:, 3:4], in_=S[:, 3:4], func=AF.Exp, scale=-0.5), reads=[S], writes=[S])
            k.op("dve", lambda e: e.tensor_scalar(out=S[:, 4:5], in0=S[:, 0:1], scalar1=-1.0, scalar2=None, op0=ALU.mult),
                 reads=[S], writes=[S])
            k.op("dve", lambda e: e.tensor_scalar(out=wsn[tb][:], in0=wsp_f[:], scalar1=S[:, 3:4], scalar2=None, op0=ALU.mult),
                 reads=[wsp_f, S], writes=[wsn[tb]])
            k.op("dve", lambda e: e.tensor_scalar(out=mrep[tb][:], in0=C["ones_f"][:], scalar1=S[:, 4:5], scalar2=None, op0=ALU.mult),
                 reads=[C["ones_f"], S], writes=[mrep[tb]])
        for dc in range(32):
            g = dc // 2
            Pb = k.P[dc % 4]
            for tb in range(4):
                k.mmg(Pb, Pb[:, tb * 128:(tb + 1) * 128],
                      [(vraw[tb][:, dc * 128:(dc + 1) * 128], wsn[tb][:, g, :]), (mrep[tb][:], wsn[tb][:, g, :])],
                      [vraw[tb], wsn[tb], mrep[tb]])
            tt = t1[dc % 2]
            ttv = tt[:].rearrange("p (a t) -> p a t", a=4)
            k.op("dve", lambda e: e.scalar_tensor_tensor(out=ttv, in0=Pb[:].rearrange("p (a t) -> p a t", a=4),
                                                         scalar=vg[:, dc:dc + 1], in1=bcast_mid(bias_bc[:, g, :], 4),
                                                         op0=ALU.mult, op1=ALU.add),
                 reads=[Pb, vg, bias_bc], writes=[tt])
            k.op("dve", lambda e: e.scalar_tensor_tensor(out=ttv, in0=bcast_mid(rs_bc[:, g, :], 4),
                                                          scalar=vb[:, dc:dc + 1], in1=ttv, op0=ALU.mult, op1=ALU.add),
                 reads=[rs_bc, vb, tt], writes=[tt])
            k.op("dve", lambda e: e.tensor_tensor(out=uT[dc][:], in0=tt[:], in1=uT[dc][:], op=ALU.mult),
                 reads=[tt, uT[dc]], writes=[uT[dc]])
            k.dma("sp", feat.ap[dc * 128:(dc + 1) * 128, tg * TG:(tg + 1) * TG], uT[dc][:], uT[dc],
                  reads=[uT[dc]], writes=[feat.T])


def emit_sb_front(k, C, NT, x_src, W, li, qk_d, v_d, o_d, base):
    L = NT * TG
    NB = L // 128
    j = li // 3
    w_qkv = W["sb_w_qkv"][j]
    k.stage(base, "sb1")
    g_pre = load_small(k, W["ln_mix_pre"][li], [128, DC])
    nb = NormBufs(k)
    xs = k.sb([128, DC, TG], F32, "xs")
    hn = k.sbs(DC, [128, TG], BF16, "hn")
    stg = k.sbs(4, [128, TG], BF16, "stg")
    vtok = k.sbs(4, [128, 2048], BF16, "vtok")
    stsem = T(None)
    ws = WStream(k, 3, 16, 256)
    Pn = k.P[7]
    scale = 128.0 ** -0.5
    for tg in range(NT):
        k.dma("sp", xs[:], x_tile_ap(x_src, tg), xs, reads=[x_src.tt(tg)], writes=[xs])
        pre_norm(k, C, nb, xs, g_pre, hn, Pn)
        for ct in range(16):
            wt = ws.load(w_qkv, 0, 16, ct * 256, 256)
            for jj in range(2):
                r = 2 * ct + jj
                Pb = k.P[r % 4]
                k.mmg(Pb, Pb[:], [(wt[:, c, jj * 128:(jj + 1) * 128], hn[c][:]) for c in range(DC)], [wt] + hn)
                sg = stg[r % 4]
                k.op("act", lambda e: e.activation(out=sg[:], in_=Pb[:], func=AF.Copy, scale=(scale if ct < 8 else 1.0)),
                     reads=[Pb], writes=[sg])
                k.dma("sp", qk_d.ap[r * 128:(r + 1) * 128, tg * TG:(tg + 1) * TG], sg[:], sg, reads=[sg], writes=[qk_d.T])
        for ct in range(8):
            wt = ws.load(w_qkv, 0, 16, 4096 + ct * 256, 256)
            for tb in range(4):
                Pb = k.P[tb]
                k.mmg(Pb, Pb[:, 0:256], [(hn[c][:, tb * 128:(tb + 1) * 128], wt[:, c, :]) for c in range(DC)], [wt] + hn)
                k.op("act", lambda e: e.activation(out=vtok[tb][:, ct * 256:(ct + 1) * 256], in_=Pb[:, 0:256], func=AF.Copy),
                     reads=[Pb], writes=[vtok[tb]])
        for tb in range(4):
            r0 = tg * TG + tb * 128
            k.dma("sp", v_d.ap[r0:r0 + 128, :], vtok[tb][:], vtok[tb], reads=[vtok[tb]], writes=[v_d.T])
    k.stage(base, "sb2")
    NEGV = -30000.0
    neg = k.sb([128, 4, TG], BF16, "neg")
    negge = k.sb([128, 128], BF16, "negge")
    k.op("dve", lambda e: e.tensor_scalar(out=negge[:], in0=C["gt_f"][:], scalar1=-1.0, scalar2=NEGV, op0=ALU.add, op1=ALU.mult),
         reads=[C["gt_f"]], writes=[negge])
    k.op("dve", lambda e: e.tensor_scalar(out=negge[:], in0=C["lt_f"][:], scalar1=-1.0, scalar2=-NEGV, op0=ALU.add, op1=ALU.mult),
         reads=[C["lt_f"]], writes=[negge])
    k.op("dve", lambda e: e.memset(neg[:], 0.0), writes=[neg])
    for a in range(4):
        k.op("dve", lambda e: e.tensor_copy(out=neg[:, a, a * 128:(a + 1) * 128], in_=negge[:]), reads=[negge], writes=[neg])
        if a > 0:
            k.op("dve", lambda e: e.memset(neg[:, a, 0:a * 128], NEGV), writes=[neg])
    negrow = k.sb([1, 128], BF16, "negrow")
    k.op("dve", lambda e: e.memset(negrow[:], -1.0), writes=[negrow])
    NCHN = 3
    negones = k.sb([128, 128], BF16, "negones")
    k.op("dve", lambda e: e.memset(negones[:], -1.0), writes=[negones])
    qT = k.sbs(NCHN, [128, L], BF16, "qT")
    kT = k.sbs(NCHN, [128, L], BF16, "kT")
    vh = k.sbs(NCHN, [128, NB, 128], BF16, "vh")
    eb = k.sbs(NCHN, [128, TG], F32, "eb")
    spb = k.sbs(NCHN, [128, TG], BF16, "spb")
    Ab = k.sbs(NCHN, [128, TG], BF16, "Ab")
    S32 = k.sbs(NCHN, [128, TG], F32, "S32")
    Sb = k.sbs(NCHN, [128, TG], BF16, "Sb")
    ob = k.sbs(NCHN, [128, TG], BF16, "ob")

    def chain(c):
        Pz, Po = k.P[c], k.P[4 + c]
        q_, k_, v_ = qT[c], kT[c], vh[c]
        e_, sp_, A_, S_, Sb_, o_ = eb[c], spb[c], Ab[c], S32[c], Sb[c], ob[c]
        for quad in range((16 + NCHN - 1) // NCHN):
            h = quad * NCHN + c
            if h >= 16:
                return
            k.dma("sp", q_[:], qk_d.ap[h * 128:(h + 1) * 128, :], q_, reads=[qk_d.T], writes=[q_])
            k.dma("sp", k_[:], qk_d.ap[2048 + h * 128:2048 + (h + 1) * 128, :], k_, reads=[qk_d.T], writes=[k_])
            for q4 in range(0, NB, 8):
                k.dma("sp", v_[:, q4:q4 + 8, :],
                      v_d.ap[q4 * 128:(q4 + 8) * 128, h * 128:(h + 1) * 128].rearrange("(b p) d -> p b d", p=128), v_,
                      reads=[v_d.T], writes=[v_])
            for tg in range(NT):
                nkb = 4 * (tg + 1)
                qs = q_[:, tg * TG:(tg + 1) * TG]
                for sb in range(nkb - 1, -1, -1):
                    first, last = sb == nkb - 1, sb == 0
                    a = sb - 4 * tg
                    zp = [(k_[:, sb * 128:(sb + 1) * 128], qs)]
                    zr = [k_, q_]
                    if a >= 0:
                        zp.append((C["ident_bf"][:], neg[:, a, :]))
                        zr.append(neg)
                    k.mmg(Pz, Pz[:], zp, zr)
                    yield
                    k.op("act", lambda e: e.activation(out=e_[:], in_=Pz[:], func=AF.Exp), reads=[Pz], writes=[e_])
                    yield
                    k.op("act", lambda e: e.activation(out=sp_[:], in_=e_[:], func=AF.Ln, bias=1.0), reads=[e_], writes=[sp_])
                    yield
                    g2 = zp + [(C["neg_ge_bf"][:], sp_[:])]
                    g2r = zr + [sp_, C["neg_ge_bf"]]
                    if not first:
                        g2.append((negones[:], Sb_[:]))
                        g2r += [Sb_, negones]
                    k.mmg(Pz, Pz[:], g2, g2r)
                    yield
                    k.op("act", lambda e: e.activation(out=A_[:], in_=Pz[:], func=AF.Exp), reads=[Pz], writes=[A_])
                    yield
                    emit_partial_group(k, Po, [(v_[:, sb, :], A_[:])], [v_, A_], first, last)
                    if not last:
                        if first:
                            k.op("pool", lambda e: e.tensor_copy(out=S_[:], in_=sp_[:]), reads=[sp_], writes=[S_])
                        else:
                            k.op("pool", lambda e: e.tensor_tensor(out=S_[:], in0=S_[:], in1=sp_[:], op=ALU.add), reads=[S_, sp_], writes=[S_])
                        k.op("dve", lambda e: e.tensor_copy(out=Sb_[:], in_=S_[:]), reads=[S_], writes=[Sb_])
                    else:
                        k.op("dve", lambda e: e.tensor_copy(out=o_[:], in_=Po[:]), reads=[Po], writes=[o_])
                        k.dma("sp", o_d.ap[h * 128:(h + 1) * 128, tg * TG:(tg + 1) * TG], o_[:], o_, reads=[o_], writes=[o_d.T])
                    yield

    run_lockstep((chain(c) for c in range(NCHN)), NCHN)


def emit_ssd_front(k, C, NT, x_src, W, li, S, base):
    L = NT * TG
    NCH = L // 128
    j = li // 3
    w_in = W["ssd_w_in"][j]
    zs_d, bc_d, btok_d, xs_d, y_d, dtda_d, feat = S["featA"], S["featC"], S["featV"], S["ssdX"], S["ssdY"], S["dtda"], S["featB"]
    AX = mybir.AxisListType.X
    k.stage(base, "ssd1")
    g_pre = load_small(k, W["ln_mix_pre"][li], [128, DC])
    cw = load_small(k, W["ssd_conv_w"][j], [128, 48, 4])
    cb = load_small(k, W["ssd_conv_b"][j], [128, 48])
    dtb = load_small(k, W["ssd_dt_bias"][j], [64, 1])
    alog = load_small(k, W["ssd_a_log"][j], [64, 1])
    aneg = k.sb([64, 1], F32, "aneg")
    k.op("act", lambda e: e.activation(out=aneg[:], in_=alog[:], func=AF.Exp), reads=[alog], writes=[aneg])
    k.op("dve", lambda e: e.tensor_scalar(out=aneg[:], in0=aneg[:], scalar1=-1.0, scalar2=None, op0=ALU.mult), reads=[aneg], writes=[aneg])
    halo = k.sb([128, 48, 4], F32, "halo")
    k.op("pool", lambda e: e.memset(halo[:], 0.0), writes=[halo])
    nb = NormBufs(k)
    xs = k.sb([128, DC, TG], F32, "xs")
    hn = k.sbs(DC, [128, TG], BF16, "hn")
    xs_tok = k.sb([128, 4, 4096], F32, "xs_tok")
    b_tok = k.sb([128, 4, 1024], BF16, "b_tok")
    xpad = k.sbs(3, [128, TG + 4], F32, "xpad")
    cv = k.sbs(3, [128, TG], F32, "cv")
    stg = k.sbs(4, [128, TG], BF16, "stg")
    e1 = k.sb([64, TG], F32, "e1")
    dtT = k.sb([64, TG], F32, "dtT")
    daT = k.sb([64, TG], F32, "daT")
    dtda_tok = k.sb([128, 4, 128], F32, "dtda_tok")
    ws = WStream(k, 3, 16, 256)
    Pn = k.P[7]
    nstg = 0
    for tg in range(NT):
        tcols = slice(tg * TG, (tg + 1) * TG)
        k.dma("sp", xs[:], x_tile_ap(x_src, tg), xs, reads=[x_src.tt(tg)], writes=[xs])
        pre_norm(k, C, nb, xs, g_pre, hn, Pn)
        for ct in range(16):
            wt = ws.load(w_in, 0, 16, ct * 256, 256)
            for jj in range(2):
                r = 2 * ct + jj
                Pb = k.P[r % 4]
                k.mmg(Pb, Pb[:], [(wt[:, c, jj * 128:(jj + 1) * 128], hn[c][:]) for c in range(DC)], [wt] + hn)
                sg = stg[nstg % 4]
                nstg += 1
                k.op("act", lambda e: e.activation(out=sg[:], in_=Pb[:], func=AF.Silu), reads=[Pb], writes=[sg])
                k.dma("sp", zs_d.ap[r * 128:(r + 1) * 128, tcols], sg[:], sg, reads=[sg], writes=[zs_d.T])
        wcache = {}

        def xchunk(ch):
            nonlocal nstg
            ct, jj = ch // 2, ch % 2
            if jj == 0:
                wcache[ct] = ws.load(w_in, 0, 16, 4096 + ct * 256, 256)
            wt = wcache[ct]
            Pb = k.P[ch % 4]
            xp = xpad[ch % 3]
            o = cv[ch % 3]
            k.mmg(Pb, Pb[:], [(wt[:, c, jj * 128:(jj + 1) * 128], hn[c][:]) for c in range(DC)], [wt] + hn)
            yield
            k.op("dve", lambda e: e.tensor_copy(out=xp[:, 0:4], in_=halo[:, ch, :]), reads=[halo], writes=[xp])
            k.op("act", lambda e: e.activation(out=xp[:, 4:TG + 4], in_=Pb[:], func=AF.Copy), reads=[Pb], writes=[xp])
            k.op("dve", lambda e: e.tensor_copy(out=halo[:, ch, :], in_=xp[:, TG:TG + 4]), reads=[xp], writes=[halo])
            yield
            k.op("dve", lambda e: e.tensor_scalar(out=o[:], in0=xp[:, 4:TG + 4], scalar1=cw[:, ch, 3:4],
                                                  scalar2=cb[:, ch:ch + 1], op0=ALU.mult, op1=ALU.add),
                 reads=[xp, cw, cb], writes=[o])
            yield
            for tap in (2, 1, 0):
                k.op("dve", lambda e: e.scalar_tensor_tensor(out=o[:], in0=xp[:, tap + 1:tap + 1 + TG], scalar=cw[:, ch, tap:tap + 1],
                                                             in1=o[:], op0=ALU.mult, op1=ALU.add),
                     reads=[xp, cw, o], writes=[o])
                yield
            k.op("act", lambda e: e.activation(out=o[:], in_=o[:], func=AF.Silu), reads=[o], writes=[o])
            yield
            if ch >= 32:
                sg = stg[nstg % 4]
                nstg += 1
                k.op("dve", lambda e: e.tensor_copy(out=sg[:], in_=o[:]), reads=[o], writes=[sg])
                r = ch - 32
                k.dma("sp", bc_d.ap[r * 128:(r + 1) * 128, tcols], sg[:], sg, reads=[sg], writes=[bc_d.T])
            if ch < 40:
                Pt = k.P[4 + ch % 3]
                for tb in range(4):
                    k.transpose(Pt, Pt[:, tb * 128:(tb + 1) * 128], o[:, tb * 128:(tb + 1) * 128], C["ident_f"][:], [o, C["ident_f"]])
                yield
                pv = Pt[:].rearrange("p (a t) -> p a t", a=4)
                if ch < 32:
                    k.op("act", lambda e: e.activation(out=xs_tok[:, :, ch * 128:(ch + 1) * 128], in_=pv, func=AF.Copy),
                         reads=[Pt], writes=[xs_tok])
                else:
                    g = ch - 32
                    k.op("act", lambda e: e.activation(out=b_tok[:, :, g * 128:(g + 1) * 128], in_=pv, func=AF.Copy),
                         reads=[Pt], writes=[b_tok])

        run_lockstep((xchunk(ch) for ch in range(48)), 3)
        wt = ws.load(w_in, 0, 16, 10240, 64)
        Pd = k.P[6]
        k.mmg(Pd, Pd[0:64, :], [(wt[:, c, 0:64], hn[c][:]) for c in range(DC)], [wt] + hn)
        k.op("act", lambda e: e.activation(out=e1[:], in_=Pd[0:64, :], func=AF.Exp, bias=dtb[:, 0:1]), reads=[Pd, dtb], writes=[e1])
        k.op("act", lambda e: e.activation(out=dtT[:], in_=e1[:], func=AF.Ln, bias=1.0), reads=[e1], writes=[dtT])
        k.op("dve", lambda e: e.tensor_scalar(out=daT[:], in0=dtT[:], scalar1=aneg[:, 0:1], scalar2=None, op0=ALU.mult),
             reads=[dtT, aneg], writes=[daT])
        Pt = k.P[4]
        for tb in range(4):
            k.transpose(Pt, Pt[:, tb * 128:tb * 128 + 64], dtT[:, tb * 128:(tb + 1) * 128], C["ident_f"][0:64, 0:64], [dtT, C["ident_f"]])
            k.transpose(Pt, Pt[:, tb * 128 + 64:tb * 128 + 128], daT[:, tb * 128:(tb + 1) * 128], C["ident_f"][0:64, 0:64], [daT, C["ident_f"]])
        k.op("act", lambda e: e.activation(out=dtda_tok[:], in_=Pt[:].rearrange("p (a t) -> p a t", a=4), func=AF.Copy),
             reads=[Pt], writes=[dtda_tok])
        rows = slice(tg * TG, (tg + 1) * TG)
        k.dma("sp", dtda_d.ap[rows, :].rearrange("(a p) f -> p a f", p=128), dtda_tok[:], dtda_tok, reads=[dtda_tok], writes=[dtda_d.T])
        k.dma("sp", xs_d.ap[rows, :].rearrange("(a p) f -> p a f", p=128), xs_tok[:], xs_tok, reads=[xs_tok], writes=[xs_d.T])
        k.dma("sp", btok_d.ap[rows, 0:1024].rearrange("(a p) f -> p a f", p=128), b_tok[:], b_tok, reads=[b_tok], writes=[btok_d.T])
    import os
    if os.environ.get("SSD_STOP") == "1":
        return
    k.stage(base, "ssd2")
    dbc = k.sb([128, 64], F32, "dbc")
    k.dma("sp", dbc[:], W["ssd_d"].ap[j].partition_broadcast(128), dbc, writes=[dbc])
    xs_c = k.sbs(2, [128, 4096], F32, "xs_c")
    bt_c = k.sbs(2, [128, 8, 128], BF16, "bt_c")
    ct_c = k.sbs(2, [128, 8, 128], BF16, "ct_c")
    bk_c = k.sbs(2, [128, 1024], BF16, "bk_c")
    dd_c = k.sbs(2, [128, 128], F32, "dd_c")
    xdt_ = k.sbs(2, [128, 4096], BF16, "xdt")
    xdtd_ = k.sbs(2, [128, 4096], BF16, "xdtd")
    prev_f = k.sb([128, 4096], F32, "prev_f")
    prev_b = k.sb([128, 4096], BF16, "prev_b")
    k.op("dve", lambda e: e.memset(prev_f[:], 0.0), writes=[prev_f])
    k.op("dve", lambda e: e.memset(prev_b[:], 0.0), writes=[prev_b])
    y_tok = k.sbs(2, [128, 4096], F32, "y_tok")
    acum_ = k.sbs(2, [128, 64], F32, "acum")
    dah_ = k.sbs(2, [128, 64], BF16, "dah")
    dal_ = k.sbs(2, [128, 64], BF16, "dal")
    ea_ = k.sbs(2, [128, 64], F32, "ea")
    cdb_ = k.sbs(2, [128, 64], F32, "cdb")
    decs_ = k.sbs(2, [128, 64], F32, "decs")
    w2_ = k.sbs(2, [128, 64], F32, "w2")
    rhs4 = [k.sbs(2, [128, 4, 128], F32, "rhs4") for _ in range(2)]
    Eb = [k.sbs(2, [128, TG], F32, "Eb") for _ in range(2)]
    MT = [k.sbs(2, [128, 4, 128], BF16, "MT") for _ in range(2)]
    cbm = k.sbs(2, [128, 128], F32, "cbm")
    tA = k.sbs(2, [128, TG], F32, "tA")
    tB = k.sbs(2, [128, TG], F32, "tB")
    v3 = lambda ap: ap.rearrange("p (h d) -> p h d", d=64)

    def prep(c):
        i = c % 2
        xc, bt, ct, bk, dd = xs_c[i], bt_c[i], ct_c[i], bk_c[i], dd_c[i]
        acum, dah, dal, ea, cdb, decs, w2, xdt, xdtd = acum_[i], dah_[i], dal_[i], ea_[i], cdb_[i], decs_[i], w2_[i], xdt_[i], xdtd_[i]
        rows = slice(c * 128, (c + 1) * 128)
        k.dma("sp", xc[:], xs_d.ap[rows, :], xc, reads=[xs_d.T], writes=[xc])
        k.dma("sp", bt[:], bc_d.ap[0:1024, rows].rearrange("(g n) s -> n g s", n=128), bt, reads=[bc_d.T], writes=[bt])
        k.dma("sp", ct[:], bc_d.ap[1024:2048, rows].rearrange("(g n) s -> n g s", n=128), ct, reads=[bc_d.T], writes=[ct])
        k.dma("sp", bk[:], btok_d.ap[rows, 0:1024], bk, reads=[btok_d.T], writes=[bk])
        k.dma("sp", dd[:], dtda_d.ap[rows, :], dd, reads=[dtda_d.T], writes=[dd])
        dt_ap, da_ap = dd[:, 0:64], dd[:, 64:128]
        Pm = k.P[0]
        k.op("dve", lambda e: e.tensor_copy(out=dah[:], in_=da_ap), reads=[dd], writes=[dah])
        k.op("dve", lambda e: e.tensor_tensor(out=dal[:], in0=da_ap, in1=dah[:], op=ALU.subtract), reads=[dd, dah], writes=[dal])
        k.mmg(Pm, Pm[:, 0:64], [(C["le_bf"][:], dah[:]), (C["le_bf"][:], dal[:])], [C["le_bf"], dah, dal])
        k.mmg(Pm, Pm[:, 64:128], [(C["ones_bf"][:], dah[:]), (C["ones_bf"][:], dal[:])], [C["ones_bf"], dah, dal])
        k.op("act", lambda e: e.activation(out=acum[:], in_=Pm[:, 0:64], func=AF.Copy), reads=[Pm], writes=[acum])
        k.op("act", lambda e: e.activation(out=ea[:], in_=Pm[:, 0:64], func=AF.Exp), reads=[Pm], writes=[ea])
        k.op("act", lambda e: e.activation(out=cdb[:], in_=Pm[:, 64:128], func=AF.Exp), reads=[Pm], writes=[cdb])
        k.op("dve", lambda e: e.tensor_tensor(out=decs[:], in0=Pm[:, 64:128], in1=acum[:], op=ALU.subtract), reads=[Pm, acum], writes=[decs])
        k.op("act", lambda e: e.activation(out=decs[:], in_=decs[:], func=AF.Exp), reads=[decs], writes=[decs])
        k.op("dve", lambda e: e.tensor_tensor(out=w2[:], in0=decs[:], in1=dt_ap, op=ALU.mult), reads=[decs, dd], writes=[w2])
        xv = v3(xc[:])
        k.op("dve", lambda e: e.tensor_tensor(out=v3(xdt[:]), in0=xv, in1=bcast_last(dt_ap, 64), op=ALU.mult), reads=[xc, dd], writes=[xdt])
        k.op("pool", lambda e: e.tensor_tensor(out=v3(xdtd[:]), in0=xv, in1=bcast_last(w2[:], 64), op=ALU.mult), reads=[xc, w2], writes=[xdtd])

    def group(c, g):
        i = c % 2
        sl = g % 2
        xc, bt, ct, bk, dd = xs_c[i], bt_c[i], ct_c[i], bk_c[i], dd_c[i]
        ea, cdb, xdt, xdtd = ea_[i], cdb_[i], xdt_[i], xdtd_[i]
        yt = y_tok[i]
        gc = slice(g * 512, (g + 1) * 512)
        Pcb, Ps, Py, Pq = k.P[1], k.P[2 + sl], k.P[4 + sl], k.P[6 + sl]
        cm, ta, tb_ = cbm[sl], tA[sl], tB[sl]
        k.mmg(Pcb, Pcb[:, sl * 128:(sl + 1) * 128], [(bt[:, g, :], ct[:, g, :])], [bt, ct])
        yield
        k.op("dve", lambda e: e.tensor_tensor(out=cm[:], in0=Pcb[:, sl * 128:(sl + 1) * 128], in1=C["le_f"][:], op=ALU.mult),
             reads=[Pcb, C["le_f"]], writes=[cm])
        for hb in range(2):
            h0 = g * 8 + hb * 4
            r4, E_, M_ = rhs4[sl][hb], Eb[sl][hb], MT[sl][hb]
            k.op("dve", lambda e: e.tensor_tensor(out=r4[:], in0=bcast_mid(C["le_f"][:], 4), in1=bcast_last(dd[:, 64 + h0:64 + h0 + 4], 128),
                                                  op=ALU.mult), reads=[C["le_f"], dd], writes=[r4])
            yield
            k.mmg(Ps, Ps[:], [(C["gt_f"][:], r4[:].rearrange("p a t -> p (a t)"))], [C["gt_f"], r4])
            yield
            k.op("act", lambda e: e.activation(out=E_[:], in_=Ps[:], func=AF.Exp), reads=[Ps], writes=[E_])
            yield
            k.op("dve", lambda e: e.tensor_tensor(out=M_[:], in0=E_[:].rearrange("p (a t) -> p a t", a=4), in1=bcast_mid(cm[:], 4), op=ALU.mult),
                 reads=[E_, cm], writes=[M_])
            yield
            for hh in range(4):
                h = h0 + hh
                col = (hb * 4 + hh) * 64
                k.mmg(Py, Py[:, col:col + 64], [(M_[:, hh, :], xdt[:, h * 64:(h + 1) * 64])], [M_, xdt])
            if hb == 0:
                k.mmg(Pq, Pq[:], [(ct[:, g, :], prev_b[:, gc])], [ct, prev_b])
                k.op("pool", lambda e: e.tensor_tensor(out=v3(tb_[:]), in0=v3(xc[:, gc]), in1=bcast_last(dbc[:, g * 8:(g + 1) * 8], 64), op=ALU.mult),
                     reads=[xc, dbc], writes=[tb_])
                yield
                k.op("dve", lambda e: e.tensor_tensor(out=v3(ta[:]), in0=v3(Pq[:]), in1=bcast_last(ea[:, g * 8:(g + 1) * 8], 64), op=ALU.mult),
                     reads=[Pq, ea], writes=[ta])
                k.op("pool", lambda e: e.tensor_tensor(out=v3(prev_f[:, gc]), in0=v3(prev_f[:, gc]), in1=bcast_last(cdb[:, g * 8:(g + 1) * 8], 64), op=ALU.mult),
                     reads=[prev_f, cdb], writes=[prev_f])
                yield
                k.mmg(Pq, Pq[:], [(bk[:, g * 128:(g + 1) * 128], xdtd[:, gc])], [bk, xdtd])
            yield
        k.op("dve", lambda e: e.tensor_tensor(out=ta[:], in0=ta[:], in1=Py[:], op=ALU.add), reads=[ta, Py], writes=[ta])
        k.op("dve", lambda e: e.tensor_tensor(out=prev_f[:, gc], in0=prev_f[:, gc], in1=Pq[:], op=ALU.add), reads=[prev_f, Pq], writes=[prev_f])
        yield
        k.op("pool", lambda e: e.tensor_tensor(out=yt[:, gc], in0=ta[:], in1=tb_[:], op=ALU.add), reads=[ta, tb_], writes=[yt])
        k.op("act", lambda e: e.activation(out=prev_b[:, gc], in_=prev_f[:, gc], func=AF.Copy), reads=[prev_f], writes=[prev_b])

    prep(0)
    for c in range(NCH):
        if c + 1 < NCH:
            prep(c + 1)
        run_lockstep((group(c, g) for g in range(8)), 2)
        rows = slice(c * 128, (c + 1) * 128)
        k.dma("sp", y_d.ap[rows, :], y_tok[c % 2][:], y_tok[c % 2], reads=[y_tok[c % 2]], writes=[y_d.T])
    if os.environ.get("SSD_STOP") == "2":
        return
    k.stage(base, "ssd3")
    ng = load_small(k, W["ssd_norm"][j], [128, 32])
    nb = NormBufs(k)
    ytk = k.sb([128, 4, 4096], F32, "ytk")
    yg = k.sb([128, 32, TG], F32, "yg")
    zsb = k.sbs(4, [128, TG], BF16, "zsb")
    fo = k.sb([128, 32, TG], BF16, "fo")
    for tg in range(NT):
        rows = slice(tg * TG, (tg + 1) * TG)
        tcols = slice(tg * TG, (tg + 1) * TG)
        k.dma("sp", ytk[:], y_d.ap[rows, :].rearrange("(a p) f -> p a f", p=128), ytk, reads=[y_d.T], writes=[ytk])
        for fc in range(32):
            Pt = k.P[fc % 4]
            for tb in range(4):
                k.transpose(Pt, Pt[:, tb * 128:(tb + 1) * 128], ytk[:, tb, fc * 128:(fc + 1) * 128], C["ident_f"][:], [ytk, C["ident_f"]])
            zt = zsb[fc % 4]
            k.dma("sp", zt[:], zs_d.ap[fc * 128:(fc + 1) * 128, tcols], zt, reads=[zs_d.T], writes=[zt])
            k.op("dve", lambda e: e.tensor_tensor(out=yg[:, fc, :], in0=Pt[:], in1=zt[:], op=ALU.mult),
                 reads=[Pt, zt], writes=[yg])
        rstd = rms_stats(k, C, nb, yg, 32, k.P[7], 4096)
        for fc in range(32):
            k.op("dve", lambda e: e.scalar_tensor_tensor(out=fo[:, fc, :], in0=yg[:, fc, :], scalar=ng[:, fc:fc + 1], in1=rstd[:],
                                                         op0=ALU.mult, op1=ALU.mult), reads=[yg, ng, rstd], writes=[fo])
        k.dma("sp", feat.ap[:, tcols].rearrange("(c p) t -> p c t", p=128), fo[:], fo, reads=[fo], writes=[feat.T])


WEIGHT_SPECS = {
    "ln_mix_pre": [4, 128, DC], "ln_mix_post": [4, 128, DC], "ln_mem": [4, 128, DC],
    "ln_xa_pre": [4, 128, DC], "ln_xa_post": [4, 128, DC], "ln_ffn_pre": [4, 128, DC], "ln_ffn_post": [4, 128, DC],
    "xa_wq": [4, D, 512], "xa_wkv": [4, D, 1024], "xa_wo": [4, 512, D],
    "ffn_w_in": [4, D, 2 * FFN], "ffn_conv_w": [4, 128, 88, 3], "ffn_conv_b": [4, 128, 88], "ffn_w_out": [4, FFN, D],
    "sg_w_in": [1, D, 8192], "sg_v_norm_g": [1, 128, 32], "sg_v_norm_b": [1, 128, 32],
    "sg_w_spatial": [1, 128, 16, 128], "sg_b_spatial": [1, 16, 128], "sg_w_out": [1, 4096, D],
    "sb_w_qkv": [1, D, 3 * D], "sb_w_out": [1, D, D],
    "ssd_w_in": [2, D, 10304], "ssd_conv_w": [2, 128, 48, 4], "ssd_conv_b": [2, 128, 48], "ssd_dt_bias": [2, 64, 1],
    "ssd_a_log": [2, 64, 1], "ssd_d": [2, 64], "ssd_norm": [2, 128, 32], "ssd_w_out": [2, 4096, D],
}


class DT:
    def __init__(self, nc, name, shape, dtype, kind):
        self.h = nc.dram_tensor(name, list(shape), dtype, kind=kind)
        self.ap = self.h.ap()
        self.T = T(self.h)
        self.shape = shape
        self._tt = {}

    def tt(self, tg):
        if tg not in self._tt:
            self._tt[tg] = T(self.h)
        return self._tt[tg]

    def __getitem__(self, key):
        return self.ap[key]


LAST_KB = None

NEED = {
    "ffn": ["ln_ffn_pre", "ln_ffn_post", "ffn_w_in", "ffn_conv_w", "ffn_conv_b", "ffn_w_out"],
    "xa": ["ln_mem", "ln_xa_pre", "ln_xa_post", "xa_wq", "xa_wkv", "xa_wo"],
    "mix0": ["ln_mix_pre", "ln_mix_post", "ssd_w_in", "ssd_conv_w", "ssd_conv_b", "ssd_dt_bias", "ssd_a_log", "ssd_d",
             "ssd_norm", "ssd_w_out"],
    "mix1": ["ln_mix_pre", "ln_mix_post", "sg_w_in", "sg_v_norm_g", "sg_v_norm_b", "sg_w_spatial", "sg_b_spatial", "sg_w_out"],
    "mix2": ["ln_mix_pre", "ln_mix_post", "sb_w_qkv", "sb_w_out"],
}


def needed_weights(plan):
    out = []
    for kind, li in plan:
        key = kind if kind != "mix" else f"mix{li % 3}"
        for n in NEED[key]:
            if n not in out:
                out.append(n)
    return out


def build_program(L, plan, wnames):
    global LAST_KB
    NT = L // TG
    nc = bass.Bass("TRN2", target_bir_lowering=False)
    k = KB(nc)
    LAST_KB = k
    xin = DT(nc, "xT", [D, L], F32, "ExternalInput")
    memT = DT(nc, "memT", [D, 256], F32, "ExternalInput")
    xout = DT(nc, "outT", [D, L], F32, "ExternalOutput")
    xr = DT(nc, "xr", [D, L], F32, "Internal")
    featA = DT(nc, "featA", [4096, L], BF16, "Internal")
    featB = DT(nc, "featB", [4096, L], BF16, "Internal")
    featV = DT(nc, "featV", [L, 2048], BF16, "Internal")
    S = {"featA": featA, "featB": featB, "featV": featV}
    if any(kind == "mix" and li % 3 == 0 for kind, li in plan):
        S["featC"] = DT(nc, "featC", [2048, L], BF16, "Internal")
        S["ssdX"] = DT(nc, "ssdX", [L, 4096], F32, "Internal")
        S["ssdY"] = DT(nc, "ssdY", [L, 4096], F32, "Internal")
        S["dtda"] = DT(nc, "dtda", [L, 128], F32, "Internal")
    W = {"memT": memT}
    for n in wnames:
        W[n] = DT(nc, n, WEIGHT_SPECS[n], F32, "ExternalInput")
    C = make_consts(k)
    base = k.sb_off
    for tg in range(NT):
        k.dma("sp", xout.ap[:, tg * TG:(tg + 1) * TG], xin.ap[:, tg * TG:(tg + 1) * TG], xout.tt(tg),
              reads=[xin.T], writes=[xout.tt(tg)])
    cur = xout
    for i, (kind, li) in enumerate(plan):
        dst = xout
        if kind == "ffn":
            emit_ffn(k, C, NT, cur, dst, W, li, base)
        elif kind == "xa":
            emit_xa(k, C, NT, cur, dst, W, li, base)
        elif kind == "mix" and li % 3 == 0:
            emit_ssd_front(k, C, NT, cur, W, li, S, base)
            emit_outproj(k, C, NT, featB, 32, W["ssd_w_out"][li // 3], W["ln_mix_post"][li], cur, dst, base)
        elif kind == "mix" and li % 3 == 2:
            emit_sb_front(k, C, NT, cur, W, li, featA, featV, featB, base)
            emit_outproj(k, C, NT, featB, 16, W["sb_w_out"][li // 3], W["ln_mix_post"][li], cur, dst, base)
        elif kind == "mix" and li % 3 == 1:
            emit_sgu_front(k, C, NT, cur, W, li, featA, base)
            emit_outproj(k, C, NT, featA, 32, W["sg_w_out"][li // 3], W["ln_mix_post"][li], cur, dst, base)
        else:
            raise ValueError(kind)
        cur = dst
    k.stage(None, None)
    return nc


def host_layout(inputs, wnames):
    out = {}
    for n in wnames:
        a = np.asarray(inputs[n])
        if n.startswith("ln_"):
            a = a.reshape(4, DC, 128).transpose(0, 2, 1)
        elif n == "ffn_conv_w":
            a = a.reshape(4, 3, 88, 128).transpose(0, 3, 2, 1)
        elif n == "ffn_conv_b":
            a = a.reshape(4, 88, 128).transpose(0, 2, 1)
        elif n in ("sg_v_norm_g", "sg_v_norm_b"):
            a = a.reshape(1, 32, 128).transpose(0, 2, 1)
        elif n == "sg_w_spatial":
            a = a.transpose(0, 3, 1, 2)
        elif n == "ssd_conv_w":
            a = a.reshape(2, 4, 48, 128).transpose(0, 3, 2, 1)
        elif n == "ssd_conv_b":
            a = a.reshape(2, 48, 128).transpose(0, 2, 1)
        elif n in ("ssd_dt_bias", "ssd_a_log"):
            a = a.reshape(2, 64, 1)
        elif n == "ssd_norm":
            a = a.reshape(2, 32, 128).transpose(0, 2, 1)
        out[n] = np.ascontiguousarray(a)
    return out


FULL_PLAN = []
for _i in range(4):
    FULL_PLAN += [("mix", _i), ("xa", _i), ("ffn", _i)]


def kernel(**inputs):
    L = 4096
    plan = FULL_PLAN
    wn = needed_weights(plan)
    hl = host_layout(inputs, wn)
    nc = build_program(L, plan, wn)
    x = np.asarray(inputs["x"])
    mem = np.asarray(inputs["mem"])
    in_maps = []
    for b in range(8):
        m = {"xT": np.ascontiguousarray(x[b].T), "memT": np.ascontiguousarray(mem[b].T)}
        m.update(hl)
        in_maps.append(m)
    res = run_bass_kernel_spmd(nc, in_maps, core_ids=list(range(8)))
    out = np.stack([np.ascontiguousarray(res.results[b]["outT"].T) for b in range(8)], axis=0)
    return out.astype(np.float32)
```
